# Optimizing a Trainium2 kernel written in Bass

```python
import math
import jax
import jax.numpy as jnp
from jax import lax
import numpy as np

D_MODEL = 1024
BATCH = 8
SEQ = 2048
DEPTH = 2
DEC_BATCH = 4
DEC_SEQ = 8192
PAST_LEN = 128

GRID_W = 64
EPS = 1e-6
Q_BLOCK = 128
ROPE_THETA = 500000.0
AXIAL_THETA = 10000.0
NEG = -1e30

A_GROUPS = ((128, 1), (512, 4), (2048, 16))
A_HEADS = 4
A_HEAD_DIM = 96
A_ROT = A_HEAD_DIM // 4
A_QKV = len(A_GROUPS) * A_HEADS * A_HEAD_DIM
A_WIDTH = A_HEADS * A_HEAD_DIM

B_HEADS = 6
B_KV_HEADS = 2
B_HEAD_DIM = 64
B_WIDTH = B_HEADS * B_HEAD_DIM
B_KV = B_KV_HEADS * B_HEAD_DIM

C_HEADS = 4
C_HEAD_DIM = 64
C_WIDTH = C_HEADS * 2 * C_HEAD_DIM
C_ROT = C_HEAD_DIM // 4

IN_SIZES = (A_QKV, A_QKV, A_QKV, A_WIDTH,
            B_WIDTH, B_KV, B_KV, B_WIDTH,
            C_WIDTH, C_WIDTH, C_WIDTH, C_WIDTH)
D_IN = sum(IN_SIZES)
N_BRANCH = 3

kernel_name = 'hybrid_gated_dilated_gqa_diff_encoder'


def rms_norm(x, g):
    xf = x.astype(jnp.float32)
    y = xf * lax.rsqrt(jnp.mean(xf * xf, axis=-1, keepdims=True) + EPS)
    return (y * g.astype(jnp.float32)).astype(x.dtype)


def rope_angles(pos, dim, theta):
    inv = theta ** (-jnp.arange(0, dim, 2, dtype=jnp.float32) / dim)
    return pos.astype(jnp.float32)[:, None] * inv[None, :]


def rotate(x, ang):
    half = x.shape[-1] // 2
    xf = x.astype(jnp.float32)
    x1, x2 = xf[..., :half], xf[..., half:]
    cos = jnp.cos(ang)[None, :, None, :]
    sin = jnp.sin(ang)[None, :, None, :]
    return jnp.concatenate([x1 * cos - x2 * sin, x2 * cos + x1 * sin], axis=-1).astype(x.dtype)


def partial_rope(x, ang):
    n = 2 * ang.shape[-1]
    return jnp.concatenate([rotate(x[..., :n], ang), x[..., n:]], axis=-1)


def axial_rope(x, ang_row, ang_col):
    n = 2 * ang_row.shape[-1]
    return jnp.concatenate([rotate(x[..., :n], ang_row), rotate(x[..., n:2 * n], ang_col)], axis=-1)


def dilated_window_attention(q, k, v, window, dilation):
    Bn, S, H, Dh = q.shape
    R = window // (2 * dilation)
    L = S // dilation
    N = Bn * dilation

    def split(t):
        return t.reshape(Bn, L, dilation, H, Dh).transpose(0, 2, 1, 3, 4).reshape(N, L, H, Dh)

    qs, ks, vs = split(q), split(k), split(v)
    nb = -(-L // R)
    Lp = nb * R
    qs = jnp.pad(qs, ((0, 0), (0, Lp - L), (0, 0), (0, 0)))
    kp = jnp.pad(ks, ((0, 0), (R, Lp - L + R), (0, 0), (0, 0)))
    vp = jnp.pad(vs, ((0, 0), (R, Lp - L + R), (0, 0), (0, 0)))
    qb = qs.reshape(N, nb, R, H, Dh)

    def band(t):
        tb = t.reshape(N, nb + 2, R, H, Dh)
        return jnp.concatenate([tb[:, :-2], tb[:, 1:-1], tb[:, 2:]], axis=2)

    kb, vb = band(kp), band(vp)
    s = jnp.einsum('nbqhd,nbkhd->nbhqk', qb, kb, preferred_element_type=jnp.float32) * (Dh ** -0.5)
    qi = jnp.arange(nb)[:, None] * R + jnp.arange(R)[None, :]
    kj = jnp.arange(nb)[:, None] * R - R + jnp.arange(3 * R)[None, :]
    dist = kj[:, None, :] - qi[:, :, None]
    mask = (jnp.abs(dist) <= R) & (kj[:, None, :] >= 0) & (kj[:, None, :] < L)
    s = jnp.where(mask[None, :, None, :, :], s, NEG)
    m = jnp.max(s, axis=-1, keepdims=True)
    p = jnp.exp(s - m)
    l = jnp.sum(p, axis=-1)
    o = jnp.einsum('nbhqk,nbkhd->nbqhd', p.astype(v.dtype), vb, preferred_element_type=jnp.float32)
    o = o / l.transpose(0, 1, 3, 2)[..., None]
    lse = (m[..., 0] + jnp.log(l)).transpose(0, 1, 3, 2)
    o = o.reshape(N, Lp, H, Dh)[:, :L].reshape(Bn, dilation, L, H, Dh)
    o = o.transpose(0, 2, 1, 3, 4).reshape(Bn, S, H, Dh)
    lse = lse.reshape(N, Lp, H)[:, :L].reshape(Bn, dilation, L, H).transpose(0, 2, 1, 3).reshape(Bn, S, H)
    return o, lse


def dilated_mixture(q, k, v):
    outs, lses = [], []
    for g, (window, dil) in enumerate(A_GROUPS):
        sl = slice(g * A_HEADS, (g + 1) * A_HEADS)
        o, lse = dilated_window_attention(q[:, :, sl], k[:, :, sl], v[:, :, sl], window, dil)
        outs.append(o.astype(jnp.float32))
        lses.append(lse)
    w = jax.nn.softmax(jnp.stack(lses, axis=0), axis=0)
    o = jnp.sum(w[..., None] * jnp.stack(outs, axis=0), axis=0)
    return o.astype(q.dtype)


def gqa_attention(q, k, v):
    Bn, S, Hq, Dh = q.shape
    Hkv = k.shape[2]
    G = Hq // Hkv
    nq = S // Q_BLOCK
    qb = q.reshape(Bn, nq, Q_BLOCK, Hkv, G, Dh).transpose(1, 0, 2, 3, 4, 5)
    scale = Dh ** -0.5

    def block(qi):
        s = jnp.einsum('bqhgd,bkhd->bhgqk', qi, k, preferred_element_type=jnp.float32) * scale
        p = jax.nn.softmax(s, axis=-1).astype(v.dtype)
        return jnp.einsum('bhgqk,bkhd->bqhgd', p, v, preferred_element_type=jnp.float32).astype(v.dtype)

    o = lax.map(block, qb)
    return o.transpose(1, 0, 2, 3, 4, 5).reshape(Bn, S, Hq, Dh)


def diff_attention(q1, q2, k1, k2, v, lam):
    Bn, S, H, Dh = q1.shape
    nq = S // Q_BLOCK
    scale = Dh ** -0.5

    def blk(t):
        return t.reshape(Bn, nq, Q_BLOCK, H, Dh).transpose(1, 0, 2, 3, 4)

    def block(args):
        a, b = args
        s1 = jnp.einsum('bqhd,bkhd->bhqk', a, k1, preferred_element_type=jnp.float32) * scale
        s2 = jnp.einsum('bqhd,bkhd->bhqk', b, k2, preferred_element_type=jnp.float32) * scale
        w = jax.nn.softmax(s1, axis=-1) - lam * jax.nn.softmax(s2, axis=-1)
        return jnp.einsum('bhqk,bkhe->bqhe', w.astype(v.dtype), v, preferred_element_type=jnp.float32).astype(v.dtype)

    o = lax.map(block, (blk(q1), blk(q2)))
    return o.transpose(1, 0, 2, 3, 4).reshape(Bn, S, H, v.shape[-1])


def encoder_layer(x, c, ang_a, ang_row, ang_col, ang_c, lam_init,
                  norm_g, w_ada, b_ada, w_in, qn_a, kn_a, qn_b, kn_b, qn_c, kn_c,
                  lam_q1, lam_k1, lam_q2, lam_k2, subln_c, w_oa, w_ob, w_oc, w_bg, b_bg, w_out):
    Bn, S, _ = x.shape
    mod = jnp.einsum('bd,de->be', jax.nn.silu(c), w_ada) + b_ada
    shift, scale, gate = jnp.split(mod, 3, axis=-1)
    h = rms_norm(x, norm_g) * (1 + scale[:, None, :]) + shift[:, None, :]

    u = jnp.einsum('bsd,de->bse', h, w_in)
    offs = []
    acc = 0
    for n in IN_SIZES[:-1]:
        acc += n
        offs.append(acc)
    qa, ka, va, za, qb, kb, vb, zb, qc, kc, vc, zc = jnp.split(u, offs, axis=-1)

    n_a = len(A_GROUPS) * A_HEADS
    qa = partial_rope(rms_norm(qa.reshape(Bn, S, n_a, A_HEAD_DIM), qn_a), ang_a)
    ka = partial_rope(rms_norm(ka.reshape(Bn, S, n_a, A_HEAD_DIM), kn_a), ang_a)
    va = va.reshape(Bn, S, n_a, A_HEAD_DIM)
    ya = dilated_mixture(qa, ka, va).reshape(Bn, S, A_WIDTH)

    qb = axial_rope(rms_norm(qb.reshape(Bn, S, B_HEADS, B_HEAD_DIM), qn_b), ang_row, ang_col)
    kb = axial_rope(rms_norm(kb.reshape(Bn, S, B_KV_HEADS, B_HEAD_DIM), kn_b), ang_row, ang_col)
    yb = gqa_attention(qb, kb, vb.reshape(Bn, S, B_KV_HEADS, B_HEAD_DIM)).reshape(Bn, S, B_WIDTH)

    qc = partial_rope(rms_norm(qc.reshape(Bn, S, 2 * C_HEADS, C_HEAD_DIM), qn_c), ang_c)
    kc = partial_rope(rms_norm(kc.reshape(Bn, S, 2 * C_HEADS, C_HEAD_DIM), kn_c), ang_c)
    f32 = jnp.float32
    lam = (jnp.exp(jnp.sum(lam_q1.astype(f32) * lam_k1.astype(f32)))
           - jnp.exp(jnp.sum(lam_q2.astype(f32) * lam_k2.astype(f32))) + lam_init)
    oc = diff_attention(qc[:, :, 0::2], qc[:, :, 1::2], kc[:, :, 0::2], kc[:, :, 1::2],
                        vc.reshape(Bn, S, C_HEADS, 2 * C_HEAD_DIM), lam)
    yc = (rms_norm(oc, subln_c) * (1.0 - lam_init)).reshape(Bn, S, C_WIDTH)

    pa = jnp.einsum('bse,ed->bsd', ya * jax.nn.silu(za), w_oa)
    pb = jnp.einsum('bse,ed->bsd', yb * jax.nn.silu(zb), w_ob)
    pc = jnp.einsum('bse,ed->bsd', yc * jax.nn.silu(zc), w_oc)

    g = jax.nn.sigmoid(jnp.einsum('bsd,de->bse', h, w_bg) + b_bg)
    ga, gb, gc = jnp.split(g, N_BRANCH, axis=-1)
    merged = ga * pa + gb * pb + gc * pc
    out = jnp.einsum('bsd,de->bse', merged, w_out)
    return x + gate[:, None, :] * out


def trunk(x, c, layer_params):
    S = x.shape[1]
    rows = S // GRID_W
    pos = jnp.arange(S)
    row = jnp.repeat(jnp.arange(rows), GRID_W)
    col = jnp.tile(jnp.arange(GRID_W), rows)
    ang_a = rope_angles(pos, A_ROT, ROPE_THETA)
    ang_c = rope_angles(pos, C_ROT, ROPE_THETA)
    ang_row = rope_angles(row, B_HEAD_DIM // 2, AXIAL_THETA)
    ang_col = rope_angles(col, B_HEAD_DIM // 2, AXIAL_THETA)
    for l in range(DEPTH):
        lam_init = 0.8 - 0.6 * math.exp(-0.3 * l)
        x = encoder_layer(x, c, ang_a, ang_row, ang_col, ang_c, lam_init,
                          *[p[l] for p in layer_params])
    return x


def setup_inputs(seed: int = 0) -> dict:
    key = jax.random.key(seed)
    ks = jax.random.split(key, 26)
    f32 = jnp.float32

    def nrm(k, shape, s):
        return jax.random.normal(k, shape, f32) * s

    D = D_MODEL
    return {
        'x_prompt': nrm(ks[0], (BATCH, SEQ, D), 1.0),
        'x_sample': nrm(ks[1], (DEC_BATCH, DEC_SEQ, D), 1.0),
        'c_prompt': nrm(ks[2], (BATCH, D), 1.0),
        'c_sample': nrm(ks[3], (DEC_BATCH, D), 1.0),
        'norm_g': 1.0 + nrm(ks[4], (DEPTH, D), 0.05),
        'w_ada': nrm(ks[5], (DEPTH, D, 3 * D), D ** -0.5),
        'b_ada': nrm(ks[6], (DEPTH, 3 * D), 0.02),
        'w_in': nrm(ks[7], (DEPTH, D, D_IN), D ** -0.5),
        'qn_a': 1.0 + nrm(ks[8], (DEPTH, A_HEAD_DIM), 0.05),
        'kn_a': 1.0 + nrm(ks[9], (DEPTH, A_HEAD_DIM), 0.05),
        'qn_b': 1.0 + nrm(ks[10], (DEPTH, B_HEAD_DIM), 0.05),
        'kn_b': 1.0 + nrm(ks[11], (DEPTH, B_HEAD_DIM), 0.05),
        'qn_c': 1.0 + nrm(ks[12], (DEPTH, C_HEAD_DIM), 0.05),
        'kn_c': 1.0 + nrm(ks[13], (DEPTH, C_HEAD_DIM), 0.05),
        'lam_q1': nrm(ks[14], (DEPTH, C_HEAD_DIM), 0.1),
        'lam_k1': nrm(ks[15], (DEPTH, C_HEAD_DIM), 0.1),
        'lam_q2': nrm(ks[16], (DEPTH, C_HEAD_DIM), 0.1),
        'lam_k2': nrm(ks[17], (DEPTH, C_HEAD_DIM), 0.1),
        'subln_c': 1.0 + nrm(ks[18], (DEPTH, 2 * C_HEAD_DIM), 0.05),
        'w_oa': nrm(ks[19], (DEPTH, A_WIDTH, D), A_WIDTH ** -0.5),
        'w_ob': nrm(ks[20], (DEPTH, B_WIDTH, D), B_WIDTH ** -0.5),
        'w_oc': nrm(ks[21], (DEPTH, C_WIDTH, D), C_WIDTH ** -0.5),
        'w_bg': nrm(ks[22], (DEPTH, D, N_BRANCH * D), D ** -0.5),
        'b_bg': nrm(ks[23], (DEPTH, N_BRANCH * D), 0.02),
        'w_out': nrm(ks[24], (DEPTH, D, D), D ** -0.5),
    }


def reference(x_prompt, x_sample, c_prompt, c_sample, norm_g, w_ada, b_ada, w_in,
              qn_a, kn_a, qn_b, kn_b, qn_c, kn_c, lam_q1, lam_k1, lam_q2, lam_k2, subln_c,
              w_oa, w_ob, w_oc, w_bg, b_bg, w_out):
    layer_params = (norm_g, w_ada, b_ada, w_in, qn_a, kn_a, qn_b, kn_b, qn_c, kn_c,
                    lam_q1, lam_k1, lam_q2, lam_k2, subln_c, w_oa, w_ob, w_oc, w_bg, b_bg, w_out)
    y_prompt = trunk(x_prompt, c_prompt, layer_params)
    y_sample = trunk(x_sample, c_sample, layer_params)
    return (y_prompt, y_sample)
```

```python
import math
import types
import numpy as np
import concourse.bass as bass
import concourse.mybir as mybir
from concourse.bass_utils import run_bass_kernel_spmd

F32 = mybir.dt.float32
BF16 = mybir.dt.bfloat16
AF = mybir.ActivationFunctionType
ALU = mybir.AluOpType

D = 1024
DIN = 6912
EPS = 1e-6
A_GROUPS = ((128, 1), (512, 4), (2048, 16))
OFF = dict(qa=0, ka=1152, va=2304, za=3456, qb=3840, kb=4224, vb=4352, zb=4480, qc=4864, kc=5376, vc=5888, zc=6400)
ENGS = ("pe", "act", "dve", "pool", "sp")


def _freeze(fn):
    if fn.__closure__ is None:
        return fn
    cells = []
    for c in fn.__closure__:
        try:
            cells.append(types.CellType(c.cell_contents))
        except ValueError:
            cells.append(c)
    return types.FunctionType(fn.__code__, fn.__globals__, fn.__name__, fn.__defaults__, tuple(cells))


class Buf:
    __slots__ = ("name", "w", "r")

    def __init__(self, name):
        self.name = name
        self.w = None
        self.r = []


class Sched:
    def __init__(self, nc):
        self.nc = nc
        self.q = {e: [] for e in ENGS}
        self.sems = {}
        self.cnt = {}
        self.seen = {e: {} for e in ENGS}
        for e in ENGS:
            self._sem("E_" + e)
        self.n_ops = 0

    def _sem(self, key):
        if key not in self.sems:
            self.sems[key] = self.nc.alloc_semaphore(key)
            self.cnt[key] = 0
        return self.sems[key]

    def _need(self, eng, ev, force=False):
        if ev is None:
            return
        key, val = ev
        if val <= 0:
            return
        if eng == "pe" and key == "E_pe" and not force:
            return
        if self.seen[eng].get(key, 0) >= val:
            return
        self.seen[eng][key] = val
        self.q[eng].append(("wait", key, val))

    def _deps(self, eng, reads, writes):
        for b in reads:
            self._need(eng, b.w)
        for b in writes:
            self._need(eng, b.w)
            for ev in b.r:
                self._need(eng, ev)

    def _commit(self, ev, reads, writes):
        for b in reads:
            b.r.append(ev)
            if len(b.r) > 48:
                best = {}
                for k, v in b.r:
                    if best.get(k, 0) < v:
                        best[k] = v
                b.r = list(best.items())
        for b in writes:
            b.w = ev
            b.r = []

    def op(self, eng, fn, reads=(), writes=(), sig=True):
        self._deps(eng, reads, writes)
        key = "E_" + eng
        if sig:
            self.cnt[key] += 1
            ev = (key, self.cnt[key])
        else:
            ev = (key, self.cnt[key] + 1)
        self.q[eng].append(("op", _freeze(fn), key if sig else None))
        self._commit(ev, reads, writes)
        self.n_ops += 1

    def dma(self, eng, fn, sem_key, reads=(), writes=()):
        self._deps(eng, reads, writes)
        self._sem(sem_key)
        self.cnt[sem_key] += 16
        ev = (sem_key, self.cnt[sem_key])
        self.q[eng].append(("dma", _freeze(fn), sem_key))
        self._commit(ev, reads, writes)
        self.n_ops += 1

    def cc(self, fn, sem_key, reads=(), writes=()):
        eng = "pool"
        self._deps(eng, reads, writes)
        self._sem(sem_key)
        self.cnt[sem_key] += 1
        ev = (sem_key, self.cnt[sem_key])
        self.q[eng].append(("cc", _freeze(fn), sem_key))
        self._commit(ev, reads, writes)
        self.n_ops += 1

    def barrier(self):
        evs = [(k, v) for k, v in self.cnt.items() if v > 0]
        for e in ENGS:
            for ev in evs:
                if ev[0] != "E_" + e:
                    self._need(e, ev, force=True)

    def final_wait(self, eng="sp"):
        for k, v in self.cnt.items():
            if v > 0 and k != "E_" + eng:
                self._need(eng, (k, v), force=True)

    def emit(self):
        nc = self.nc
        engmap = {"pe": "tensor", "act": "scalar", "dve": "vector", "pool": "gpsimd", "sp": "sync"}
        sems = self.sems
        with nc.Block() as block:
            for e in ENGS:
                items = self.q[e]
                if not items:
                    continue

                def body(eng, items=items):
                    for it in items:
                        if it[0] == "wait":
                            eng.wait_ge(sems[it[1]], it[2])
                        elif it[0] == "op":
                            ins = it[1](eng)
                            if it[2] is not None:
                                ins.then_inc(sems[it[2]], 1)
                        elif it[0] == "cc":
                            it[1](eng).then_inc(sems[it[2]])
                        else:
                            it[1](eng).then_inc(sems[it[2]], 16)

                getattr(block, engmap[e])(body)


def _rope_tables(smax):
    pos = np.arange(smax)
    f32 = np.float32

    def ang(p, dim, theta):
        inv = (f32(theta) ** (-np.arange(0, dim, 2, dtype=f32) / f32(dim))).astype(f32)
        return (p.astype(f32)[:, None] * inv[None, :]).astype(f32)

    tab = np.zeros((3, 2, 128, smax), f32)
    tab[:, 0] = 1.0
    aa = ang(pos, 24, 500000.0)
    tab[0, 0, 0:12] = np.cos(aa).T
    tab[0, 0, 12:24] = np.cos(aa).T
    tab[0, 1, 0:12] = np.sin(aa).T
    tab[0, 1, 12:24] = np.sin(aa).T
    tab[0, :, 96:] = 0.0
    ar = ang(pos // 64, 32, 10000.0)
    ac = ang(pos % 64, 32, 10000.0)
    tab[1, 0, 0:16] = np.cos(ar).T
    tab[1, 0, 16:32] = np.cos(ar).T
    tab[1, 1, 0:16] = np.sin(ar).T
    tab[1, 1, 16:32] = np.sin(ar).T
    tab[1, 0, 32:48] = np.cos(ac).T
    tab[1, 0, 48:64] = np.cos(ac).T
    tab[1, 1, 32:48] = np.sin(ac).T
    tab[1, 1, 48:64] = np.sin(ac).T
    a_c = ang(pos, 16, 500000.0)
    tab[2, 0, 0:8] = np.cos(a_c).T
    tab[2, 0, 8:16] = np.cos(a_c).T
    tab[2, 1, 0:8] = np.sin(a_c).T
    tab[2, 1, 8:16] = np.sin(a_c).T
    return tab


def _rot_mats():
    R = np.zeros((3, 128, 128), np.float32)

    def fill(t, base, n):
        h = n // 2
        for i in range(h):
            R[t, base + i + h, base + i] = -1.0
            R[t, base + i, base + i + h] = 1.0

    fill(0, 0, 24)
    fill(1, 0, 32)
    fill(1, 32, 32)
    fill(2, 0, 16)
    return R


def _band_mask():
    p = np.arange(128)[:, None]
    j = np.arange(256)[None, :]
    return ((j >= p) & (j <= p + 128)).astype(np.float32)


def build(SP, H, depth, lam_inits):
    NT = SP + 2 * H
    NF = SP + H
    nseq = 2
    SMAX = max(SP, 2 * H)
    segs = [(0, SP, 0, True), (SP, H, 1, True), (SP + H, H, 1, False)]
    attn = [(0, SP, 0, SP, False), (SP, H, SP, 2 * H, True)]
    nc = bass.Bass("TRN2", target_bir_lowering=False)

    def din(name, shape, dt=F32):
        return nc.dram_tensor(name, list(shape), dt, kind="ExternalInput").ap()

    x_in = din("x", [NT, D])
    cT_in = din("cT", [128, 8, nseq])
    w_ada = din("w_ada", [depth, 8, 128, 3 * D])
    b_adaT = din("b_adaT", [depth, 128, 24])
    norm_gT = din("norm_gT", [depth, 128, 8])
    w_in = din("w_in", [depth, 8, 128, DIN])
    gcols_in = din("gcols", [depth, 128, 6])
    lam_in = din("lam", [depth, 1, 256])
    subln_in = din("subln", [depth, 128, 1])
    w_oa = din("w_oa", [depth, 4, 96, D])
    w_ob = din("w_ob", [depth, 6, 64, D])
    w_oc = din("w_oc", [depth, 4, 128, D])
    w_bg = din("w_bg", [depth, 8, 128, 3 * D])
    b_bgT = din("b_bgT", [depth, 128, 24])
    w_out = din("w_out", [depth, 8, 128, D])
    rope_in = din("rope", [3, 2, 128, NT])
    flags_in = din("flags", [128, 2])
    rot_in = din("rotm", [3, 128, 128])
    mask_in = din("bandmask", [128, 256])
    ident_in = din("ident", [128, 128])
    y_out = nc.dram_tensor("y", [NF, D], F32, kind="ExternalOutput").ap()

    def dscr(name, shape, dt):
        return nc.dram_tensor(name, list(shape), dt).ap()

    XT = [dscr(f"XT{l}", [8, 128, NT], F32) for l in range(depth)]
    HT = dscr("HT", [8, 128, NT], BF16)
    QK = dscr("QK", [48, 128, NT], BF16)
    ZT = dscr("ZT", [14, 128, NT], BF16)
    VS = dscr("VS", [NT, 1792], BF16)
    YZ = dscr("YZ", [14, 128, NT], BF16)
    AGsrc = dscr("AGsrc", [8, 128, H], F32)
    AGdst = dscr("AGdst", [8, 2, 128, H], F32)
    B_AGsrc, B_AGdst = Buf("AGsrc"), Buf("AGdst")

    S = Sched(nc)
    _bn = [0]

    def sb(shape, dt, name=None):
        _bn[0] += 1
        name = "s_" + (name or f"t{_bn[0]}")
        return nc.alloc_sbuf_tensor(name, list(shape), dt), Buf(name)

    psum = [nc.alloc_psum_tensor(f"ps{i}", [128, 512], F32) for i in range(8)]
    PB = [Buf(f"ps{i}") for i in range(8)]

    ident, B_ident = sb([128, 128], F32, "ident")
    ones_bf, B_ones = sb([128, 128], BF16, "ones")
    zeros_bf, B_zeros = sb([128, 128], BF16, "zeros")
    zeros_w, _ = sb([128, 512], BF16, "zerosw")
    ones_lo, _ = sb([128, 96], BF16, "oneslo")
    ones_hi, _ = sb([128, 96], BF16, "oneshi")
    eps_t, B_eps = sb([128, 1], F32, "eps")
    mask_f, B_maskf = sb([128, 256], F32, "maskf")
    mask_bf, B_mask = sb([128, 256], BF16, "maskbf")
    rot_f, B_rotf = sb([128, 3, 128], F32, "rotf")
    cT, B_cT = sb([128, 8, nseq], F32, "cT")
    scT, B_scT = sb([128, 8, nseq], F32, "scT")

    S.dma("sp", lambda e: e.dma_start(out=ident[:], in_=ident_in), "ld_ident", writes=[B_ident])
    S.dma("sp", lambda e: e.dma_start(out=mask_f[:], in_=mask_in), "ld_mask", writes=[B_maskf])
    S.dma("sp", lambda e: e.dma_start(out=rot_f[:], in_=rot_in.rearrange("t p m -> p t m")), "ld_rot", writes=[B_rotf])
    S.dma("sp", lambda e: e.dma_start(out=cT[:], in_=cT_in), "ld_cT", writes=[B_cT])
    S.op("pool", lambda e: e.memset(ones_bf[:], 1.0), writes=[B_ones])
    S.op("pool", lambda e: e.memset(zeros_bf[:], 0.0), writes=[B_zeros])
    S.op("pool", lambda e: e.memset(zeros_w[:], 0.0), writes=[B_zeros])
    S.op("pool", lambda e: e.memset(ones_lo[:], 1.0), writes=[B_ones])
    S.op("pool", lambda e: e.memset(ones_lo[0:64, :], 0.0), writes=[B_ones])
    S.op("pool", lambda e: e.memset(ones_hi[:], 0.0), writes=[B_ones])
    S.op("pool", lambda e: e.memset(ones_hi[0:64, :], 1.0), writes=[B_ones])
    S.op("pool", lambda e: e.memset(eps_t[:], EPS), writes=[B_eps])
    S.op("dve", lambda e: e.tensor_copy(out=mask_bf[:], in_=mask_f[:]), reads=[B_maskf], writes=[B_mask])
    S.op("act", lambda e: e.activation(out=scT[:], in_=cT[:], func=AF.Silu), reads=[B_cT], writes=[B_scT])
    flags, B_flags = sb([128, 2], F32, "flags")
    onesL, B_onesL = sb([128, 96], BF16, "onesL")
    onesR, _ = sb([128, 96], BF16, "onesR")
    S.dma("sp", lambda e: e.dma_start(out=flags[:], in_=flags_in), "ld_flags", writes=[B_flags])
    S.op("pool", lambda e: e.memset(onesL[:], 1.0), writes=[B_onesL])
    S.op("pool", lambda e: e.memset(onesR[:], 1.0), writes=[B_onesL])
    S.op("dve", lambda e: e.tensor_scalar(out=onesL[0:64, :], in0=onesL[0:64, :], scalar1=flags[0:64, 0:1], scalar2=None, op0=ALU.mult),
         reads=[B_flags, B_onesL], writes=[B_onesL])
    S.op("dve", lambda e: e.tensor_scalar(out=onesR[64:128, :], in0=onesR[64:128, :], scalar1=flags[64:128, 1:2], scalar2=None,
                                          op0=ALU.mult), reads=[B_flags, B_onesL], writes=[B_onesL])

    modT, B_mod = sb([128, 24, nseq], F32, "modT")
    gsT, B_gs = sb([128, 8, nseq], F32, "gsT")
    b_ada_t, B_bada = sb([128, 24], F32, "bada")
    ng_t, B_ng = sb([128, 8], F32, "ng")
    gcols, B_gcols = sb([128, 6], F32, "gcols")
    bbg_t, B_bbg = sb([128, 24], F32, "bbg")
    subln_t, B_subln = sb([128, 1], F32, "subln")
    sg_t, B_sg = sb([128, 1], F32, "sg")
    lam_t, B_lam = sb([1, 256], F32, "lam")
    lam_w, B_lamw = sb([1, 8], F32, "lamw")
    lam_bf, B_lambf = sb([1, 128], F32, "lambf")
    nlam_t, B_nlam = sb([128, 1], F32, "nlam")
    rotg, B_rotg = sb([128, 6, 128], BF16, "rotg")

    SB_TOP_CONST = nc.sbuf_base

    def phase_reset():
        S.barrier()
        nc.sbuf_base = SB_TOP_CONST

    for l in range(depth):
        lam_init = lam_inits[l]
        phase_reset()
        S.dma("sp", lambda e, l=l: e.dma_start(out=b_ada_t[:], in_=b_adaT[l]), "ld_bada", writes=[B_bada])
        S.dma("sp", lambda e, l=l: e.dma_start(out=ng_t[:], in_=norm_gT[l]), "ld_ng", writes=[B_ng])
        S.dma("sp", lambda e, l=l: e.dma_start(out=gcols[:], in_=gcols_in[l]), "ld_gcols", writes=[B_gcols])
        S.dma("sp", lambda e, l=l: e.dma_start(out=bbg_t[:], in_=b_bgT[l]), "ld_bbg", writes=[B_bbg])
        S.dma("sp", lambda e, l=l: e.dma_start(out=subln_t[:], in_=subln_in[l]), "ld_subln", writes=[B_subln])
        S.dma("sp", lambda e, l=l: e.dma_start(out=lam_t[:], in_=lam_in[l]), "ld_lam", writes=[B_lam])
        S.op("dve", lambda e: e.tensor_scalar(out=sg_t[:], in0=subln_t[:], scalar1=float(1.0 - lam_init), scalar2=None,
                                              op0=ALU.mult), reads=[B_subln], writes=[B_sg])
        S.op("dve", lambda e: e.tensor_tensor(out=lam_t[:, 0:64], in0=lam_t[:, 0:64], in1=lam_t[:, 64:128], op=ALU.mult),
             reads=[B_lam], writes=[B_lam])
        S.op("dve", lambda e: e.tensor_tensor(out=lam_t[:, 128:192], in0=lam_t[:, 128:192], in1=lam_t[:, 192:256], op=ALU.mult),
             reads=[B_lam], writes=[B_lam])
        S.op("dve", lambda e: e.reduce_sum(out=lam_w[:, 0:1], in_=lam_t[:, 0:64], axis=mybir.AxisListType.X),
             reads=[B_lam], writes=[B_lamw])
        S.op("dve", lambda e: e.reduce_sum(out=lam_w[:, 1:2], in_=lam_t[:, 128:192], axis=mybir.AxisListType.X),
             reads=[B_lam], writes=[B_lamw])
        S.op("act", lambda e: e.activation(out=lam_w[:, 2:4], in_=lam_w[:, 0:2], func=AF.Exp), reads=[B_lamw], writes=[B_lamw])
        S.op("dve", lambda e: e.tensor_tensor(out=lam_w[:, 4:5], in0=lam_w[:, 3:4], in1=lam_w[:, 2:3], op=ALU.subtract),
             reads=[B_lamw], writes=[B_lamw])
        S.op("dve", lambda e: e.tensor_scalar(out=lam_w[:, 5:6], in0=lam_w[:, 4:5], scalar1=float(-lam_init), scalar2=None,
                                              op0=ALU.add), reads=[B_lamw], writes=[B_lamw])
        S.op("pool", lambda e: e.memset(lam_bf[:], 1.0), writes=[B_lambf])
        S.op("pe", lambda e: e.matmul(psum[0][:, 0:1], lhsT=lam_bf[0:1, :], rhs=lam_w[0:1, 5:6], start=True, stop=True),
             reads=[B_lambf, B_lamw], writes=[PB[0]])
        S.op("dve", lambda e: e.tensor_copy(out=nlam_t[:], in_=psum[0][:, 0:1]), reads=[PB[0]], writes=[B_nlam])
        for j in range(6):
            S.op("dve", lambda e, j=j: e.tensor_scalar(out=rotg[:, j, :], in0=rot_f[:, j // 2, :], scalar1=gcols[:, j:j + 1],
                                                       scalar2=None, op0=ALU.mult),
                 reads=[B_rotf, B_gcols], writes=[B_rotg])
        wst, B_wst = sb([128, 8, 1536], F32)
        for half in range(2):
            for kc in range(8):
                S.dma("sp", lambda e, l=l, kc=kc, half=half: e.dma_start(
                    out=wst[:, kc, :], in_=w_ada[l, kc, :, half * 1536:(half + 1) * 1536]), "ld_wst", writes=[B_wst])
            for cc in range(12):
                ch = half * 12 + cc
                for kc in range(8):
                    S.op("pe", lambda e, kc=kc, cc=cc, ch=ch: e.matmul(
                        psum[1][:, ch * nseq:(ch + 1) * nseq], lhsT=wst[:, kc, cc * 128:(cc + 1) * 128], rhs=scT[:, kc, :],
                        start=(kc == 0), stop=(kc == 7)), reads=[B_wst, B_scT], writes=[PB[1]], sig=(kc == 7))
        for s in range(nseq):
            S.op("dve", lambda e, s=s: e.tensor_tensor(
                out=modT[:, :, s], in0=psum[1][:, 0:24 * nseq].rearrange("p (c s) -> p c s", s=nseq)[:, :, s], in1=b_ada_t[:],
                op=ALU.add), reads=[PB[1], B_bada], writes=[B_mod])
            S.op("dve", lambda e, s=s: e.scalar_tensor_tensor(
                out=gsT[:, :, s], in0=modT[:, 8:16, s], scalar=1.0, in1=ng_t[:], op0=ALU.add, op1=ALU.mult),
                reads=[B_mod, B_ng], writes=[B_gs])

        phase_reset()
        win_bf, B_win = sb([128, 8, DIN], BF16)
        wstg, B_wstg = sb([128, 1152], F32)
        for kc in range(8):
            for cg in range(6):
                S.dma("sp", lambda e, l=l, kc=kc, cg=cg: e.dma_start(
                    out=wstg[:], in_=w_in[l, kc, :, cg * 1152:(cg + 1) * 1152]), "ld_wstg", writes=[B_wstg])
                eng = ("act", "dve", "pool")[(kc * 6 + cg) % 3]
                if eng == "act":
                    S.op("act", lambda e, kc=kc, cg=cg: e.activation(out=win_bf[:, kc, cg * 1152:(cg + 1) * 1152], in_=wstg[:],
                                                                     func=AF.Copy), reads=[B_wstg], writes=[B_win])
                else:
                    S.op(eng, lambda e, kc=kc, cg=cg: e.tensor_copy(out=win_bf[:, kc, cg * 1152:(cg + 1) * 1152], in_=wstg[:]),
                         reads=[B_wstg], writes=[B_win])
        xT, B_xT = sb([128, 8, 512], F32)
        hT, B_hT = sb([128, 8, 512], BF16)
        if l == 0:
            xtok, B_xtok = sb([128, D], F32)
        tabs, B_tabs = sb([128, 3, 2, 512], F32)
        sq = [sb([128, 512], BF16) for _ in range(2)]
        ubf = [sb([128, 512], BF16) for _ in range(2)]
        rs = [sb([128, 512], F32) for _ in range(2)]
        t1 = [sb([128, 512], F32) for _ in range(3)]
        t2 = [sb([128, 512], F32) for _ in range(2)]
        qo = [sb([128, 512], BF16) for _ in range(2)]
        zo = [sb([128, 512], BF16) for _ in range(2)]
        vo = [sb([128, 1792], BF16) for _ in range(2)]
        tmpn, B_tmpn = sb([128, 512], F32)

        qk_chunks = []
        for h in range(12):
            qk_chunks.append((OFF["qa"] + 96 * h, 96, 0, 0, h))
        for h in range(12):
            qk_chunks.append((OFF["ka"] + 96 * h, 96, 0, 1, 12 + h))
        for h in range(6):
            qk_chunks.append((OFF["qb"] + 64 * h, 64, 1, 2, 24 + h))
        for h in range(2):
            qk_chunks.append((OFF["kb"] + 64 * h, 64, 1, 3, 30 + h))
        for h in range(8):
            qk_chunks.append((OFF["qc"] + 64 * h, 64, 2, 4, 32 + h))
        for h in range(8):
            qk_chunks.append((OFF["kc"] + 64 * h, 64, 2, 5, 40 + h))
        z_chunks = []
        for h in range(4):
            z_chunks.append((OFF["za"] + 96 * h, 96, h))
        for h in range(6):
            z_chunks.append((OFF["zb"] + 64 * h, 64, 4 + h))
        for h in range(4):
            z_chunks.append((OFF["zc"] + 128 * h, 128, 10 + h))
        v_groups = [(OFF["va"], 512, 0), (OFF["va"] + 512, 512, 512), (OFF["va"] + 1024, 128, 1024),
                    (OFF["vb"], 128, 1152), (OFF["vc"], 512, 1280)]

        it = 0
        if l > 0:
            xa, B_xa = sb([128, 8, 512], F32)
        for (sg0, sgn, si, full) in segs:
            for tt in range(sgn // 512):
                t0 = sg0 + tt * 512
                if l > 0 and not full:
                    o0 = t0 - (SP + H)
                    S.dma("sp", lambda e, o0=o0: e.dma_start(out=xa[:], in_=AGdst[:, 0, :, o0:o0 + 512].rearrange("k p t -> p k t")),
                          "ld_xa", reads=[B_AGdst], writes=[B_xa])
                    S.dma("sp", lambda e, o0=o0: e.dma_start(out=xT[:], in_=AGdst[:, 1, :, o0:o0 + 512].rearrange("k p t -> p k t")),
                          "ld_xT", reads=[B_AGdst], writes=[B_xT])
                    S.op("pool", lambda e: e.tensor_scalar(out=xa[:], in0=xa[:], scalar1=flags[:, 0:1], scalar2=None, op0=ALU.mult),
                         reads=[B_xa, B_flags], writes=[B_xa])
                    S.op("dve", lambda e: e.scalar_tensor_tensor(out=xT[:], in0=xT[:], scalar=flags[:, 1:2], in1=xa[:], op0=ALU.mult,
                                                                 op1=ALU.add), reads=[B_xT, B_xa, B_flags], writes=[B_xT])
                elif l == 0:
                    for sub in range(4):
                        S.dma("sp", lambda e, t0=t0, sub=sub: e.dma_start(out=xtok[:], in_=x_in[t0 + sub * 128:t0 + (sub + 1) * 128, :]),
                              "ld_xtok", writes=[B_xtok])
                        for hf in range(2):
                            pb = 6 + hf
                            for k4 in range(4):
                                kc = hf * 4 + k4
                                S.op("pe", lambda e, pb=pb, k4=k4, kc=kc: e.transpose(
                                    psum[pb][:, k4 * 128:(k4 + 1) * 128], xtok[:, kc * 128:(kc + 1) * 128], ident[:]),
                                    reads=[B_xtok, B_ident], writes=[PB[pb]], sig=(k4 == 3))
                            S.op("dve" if hf == 0 else "act",
                                 (lambda e, pb=pb, hf=hf, sub=sub: e.tensor_copy(
                                     out=xT[:, hf * 4:(hf + 1) * 4, sub * 128:(sub + 1) * 128],
                                     in_=psum[pb][:].rearrange("p (k t) -> p k t", k=4))) if hf == 0 else
                                 (lambda e, pb=pb, hf=hf, sub=sub: e.activation(
                                     out=xT[:, hf * 4:(hf + 1) * 4, sub * 128:(sub + 1) * 128],
                                     in_=psum[pb][:].rearrange("p (k t) -> p k t", k=4), func=AF.Copy)),
                                 reads=[PB[pb]], writes=[B_xT])
                    if full:
                        S.dma("pool", lambda e, t0=t0, l=l: e.dma_start(
                            out=XT[l][:, :, t0:t0 + 512].rearrange("k p t -> p k t"), in_=xT[:]), "st_xT", reads=[B_xT])
                else:
                    S.dma("sp", lambda e, t0=t0, l=l: e.dma_start(
                        out=xT[:], in_=XT[l][:, :, t0:t0 + 512].rearrange("k p t -> p k t")), "ld_xT", writes=[B_xT])
                for kc in range(8):
                    sqt, B_sq = sq[kc % 2]
                    S.op("act", lambda e, kc=kc, sqt=sqt: e.activation(out=sqt[:], in_=xT[:, kc, :], func=AF.Square),
                         reads=[B_xT], writes=[B_sq])
                    S.op("pe", lambda e, kc=kc, sqt=sqt: e.matmul(psum[7][:], lhsT=ones_bf[:], rhs=sqt[:], start=(kc == 0),
                                                                  stop=(kc == 7)), reads=[B_ones, B_sq], writes=[PB[7]])
                rst, B_rs = rs[0]
                S.op("act", lambda e, rst=rst: e.activation(out=rst[:], in_=psum[7][:], func=AF.Sqrt, bias=eps_t[:], scale=1.0 / D),
                     reads=[PB[7], B_eps], writes=[B_rs])
                S.op("dve", lambda e, rst=rst: e.reciprocal(out=rst[:], in_=rst[:]), reads=[B_rs], writes=[B_rs])
                for kc in range(8):
                    S.op("dve", lambda e, kc=kc, rst=rst: e.tensor_tensor(out=tmpn[:], in0=xT[:, kc, :], in1=rst[:], op=ALU.mult),
                         reads=[B_xT, B_rs], writes=[B_tmpn])
                    S.op("act", lambda e, kc=kc, si=si: e.activation(out=hT[:, kc, :], in_=tmpn[:], func=AF.Identity,
                                                                     bias=modT[:, kc, si:si + 1], scale=gsT[:, kc, si:si + 1]),
                         reads=[B_tmpn, B_mod, B_gs], writes=[B_hT])
                if full:
                    S.dma("pool", lambda e, t0=t0: e.dma_start(out=HT[:, :, t0:t0 + 512].rearrange("k p t -> p k t"), in_=hT[:]),
                          "st_hT", reads=[B_hT])
                S.dma("sp", lambda e, t0=t0: e.dma_start(
                    out=tabs[:], in_=rope_in[:, :, :, t0:t0 + 512].rearrange("t c p s -> p t c s")), "ld_tabs", writes=[B_tabs])
                def stA(c0, dh, ty, gj, cid, u):
                    pu = u % 3
                    sqt, B_sq = sq[u % len(sq)]
                    ubt, B_ub = ubf[u % len(ubf)]
                    for kc in range(8):
                        S.op("pe", lambda e, kc=kc, c0=c0, dh=dh, pu=pu: e.matmul(
                            psum[pu][0:dh, :], lhsT=win_bf[:, kc, c0:c0 + dh], rhs=hT[:, kc, :], start=(kc == 0), stop=(kc == 7)),
                            reads=[B_win, B_hT], writes=[PB[pu]], sig=(kc == 7))
                    S.op("act", lambda e, dh=dh, pu=pu, sqt=sqt: e.activation(out=sqt[0:dh, :], in_=psum[pu][0:dh, :], func=AF.Square),
                         reads=[PB[pu]], writes=[B_sq])
                    S.op("act", lambda e, dh=dh, pu=pu, ubt=ubt: e.activation(out=ubt[0:dh, :], in_=psum[pu][0:dh, :], func=AF.Copy),
                         reads=[PB[pu]], writes=[B_ub])

                def stB(c0, dh, ty, gj, cid, u, t0=t0):
                    pu, pss, prt = u % 3, 3 + u % 2, 5 + u % 2
                    sqt, B_sq = sq[u % len(sq)]
                    ubt, B_ub = ubf[u % len(ubf)]
                    rst, B_rs = rs[u % len(rs)]
                    t1t, B_t1 = t1[u % len(t1)]
                    t2t, B_t2 = t2[u % len(t2)]
                    qot, B_qo = qo[u % len(qo)]
                    S.op("pe", lambda e, dh=dh, pss=pss, sqt=sqt: e.matmul(psum[pss][0:dh, :], lhsT=ones_bf[0:dh, 0:dh], rhs=sqt[0:dh, :],
                                                                          start=True, stop=True),
                         reads=[B_ones, B_sq], writes=[PB[pss]])
                    S.op("pe", lambda e, dh=dh, prt=prt, ubt=ubt, gj=gj: e.matmul(psum[prt][0:dh, :], lhsT=rotg[0:dh, gj, 0:dh],
                                                                                 rhs=ubt[0:dh, :], start=True, stop=True),
                         reads=[B_rotg, B_ub], writes=[PB[prt]])
                    S.op("act", lambda e, dh=dh, pss=pss, rst=rst: e.activation(out=rst[0:dh, :], in_=psum[pss][0:dh, :], func=AF.Sqrt,
                                                                               bias=eps_t[0:dh, :], scale=1.0 / dh),
                         reads=[PB[pss], B_eps], writes=[B_rs])
                    S.op("dve", lambda e, dh=dh, rst=rst: e.reciprocal(out=rst[0:dh, :], in_=rst[0:dh, :]), reads=[B_rs], writes=[B_rs])
                    S.op("dve", lambda e, dh=dh, pu=pu, t1t=t1t, gj=gj, ty=ty: e.scalar_tensor_tensor(
                        out=t1t[0:dh, :], in0=psum[pu][0:dh, :], scalar=gcols[0:dh, gj:gj + 1], in1=tabs[0:dh, ty, 0, :],
                        op0=ALU.mult, op1=ALU.mult), reads=[PB[pu], B_gcols, B_tabs], writes=[B_t1])
                    S.op("dve", lambda e, dh=dh, prt=prt, t2t=t2t, ty=ty: e.tensor_tensor(
                        out=t2t[0:dh, :], in0=psum[prt][0:dh, :], in1=tabs[0:dh, ty, 1, :], op=ALU.mult),
                        reads=[PB[prt], B_tabs], writes=[B_t2])
                    S.op("pool", lambda e, dh=dh, t1t=t1t, t2t=t2t: e.tensor_tensor(out=t1t[0:dh, :], in0=t1t[0:dh, :], in1=t2t[0:dh, :],
                                                                                   op=ALU.add), reads=[B_t1, B_t2], writes=[B_t1])
                    S.op("dve", lambda e, dh=dh, t1t=t1t, rst=rst, qot=qot: e.tensor_tensor(out=qot[0:dh, :], in0=t1t[0:dh, :],
                                                                                           in1=rst[0:dh, :], op=ALU.mult),
                         reads=[B_t1, B_rs], writes=[B_qo])
                    S.dma("pool", lambda e, dh=dh, cid=cid, t0=t0, qot=qot: e.dma_start(out=QK[cid, 0:dh, t0:t0 + 512], in_=qot[0:dh, :]),
                          f"st_qo{u % len(qo)}", reads=[B_qo])

                chs = qk_chunks if full else [c for c in qk_chunks if c[3] % 2 == 1]
                nch = len(chs)
                for i in range(nch + 1):
                    if i < nch:
                        stA(*chs[i], it + i)
                    if i >= 1:
                        stB(*chs[i - 1], it + i - 1)
                it += nch
                for (c0, dz, cid) in (z_chunks if full else []):
                    u = it % 2
                    pu = it % 3
                    it += 1
                    zot, B_zo = zo[u]
                    for kc in range(8):
                        S.op("pe", lambda e, kc=kc, c0=c0, dz=dz, pu=pu: e.matmul(
                            psum[pu][0:dz, :], lhsT=win_bf[:, kc, c0:c0 + dz], rhs=hT[:, kc, :], start=(kc == 0), stop=(kc == 7)),
                            reads=[B_win, B_hT], writes=[PB[pu]], sig=(kc == 7))
                    S.op("act", lambda e, dz=dz, pu=pu, zot=zot: e.activation(out=zot[0:dz, :], in_=psum[pu][0:dz, :], func=AF.Silu),
                         reads=[PB[pu]], writes=[B_zo])
                    S.dma("pool", lambda e, dz=dz, cid=cid, t0=t0, zot=zot: e.dma_start(out=ZT[cid, 0:dz, t0:t0 + 512], in_=zot[0:dz, :]),
                          f"st_zo{u}", reads=[B_zo])
                for sub in range(4):
                    vot, B_vo = vo[sub % 2]
                    for gi, (c0, n, o0) in enumerate(v_groups):
                        pu = it % 3
                        it += 1
                        for kc in range(8):
                            S.op("pe", lambda e, kc=kc, c0=c0, n=n, pu=pu, sub=sub: e.matmul(
                                psum[pu][:, 0:n], lhsT=hT[:, kc, sub * 128:(sub + 1) * 128], rhs=win_bf[:, kc, c0:c0 + n],
                                start=(kc == 0), stop=(kc == 7)), reads=[B_win, B_hT], writes=[PB[pu]], sig=(kc == 7))
                        if gi % 2 == 0:
                            S.op("dve", lambda e, n=n, pu=pu, o0=o0, vot=vot: e.tensor_copy(out=vot[:, o0:o0 + n], in_=psum[pu][:, 0:n]),
                                 reads=[PB[pu]], writes=[B_vo])
                        else:
                            S.op("act", lambda e, n=n, pu=pu, o0=o0, vot=vot: e.activation(out=vot[:, o0:o0 + n], in_=psum[pu][:, 0:n],
                                                                                          func=AF.Copy), reads=[PB[pu]], writes=[B_vo])
                    S.dma("pool", lambda e, t0=t0, sub=sub, vot=vot: e.dma_start(out=VS[t0 + sub * 128:t0 + (sub + 1) * 128, :], in_=vot[:]),
                          f"st_vo{sub % 2}", reads=[B_vo])

        phase_reset()
        KT = [sb([128, SMAX + 2048], BF16) for _ in range(2)]
        QT = [sb([128, SMAX], BF16) for _ in range(2)]
        VT, B_VT = sb([128, SMAX // 128 + 1, 128], BF16)
        PT = [sb([128, 512], BF16) for _ in range(6)]
        zt, B_zt = sb([128, 512], BF16)
        e1, B_e1 = sb([128, 512], F32)
        e2, B_e2 = sb([128, 512], F32)
        e3, B_e3 = sb([128, 512], F32)
        e4, B_e4 = sb([128, 512], F32)
        esq, B_esq = sb([128, 512], BF16)
        yzo, B_yzo = sb([128, 512], BF16)
        AO, B_AO = sb([96, SMAX], F32)
        AL, B_AL = sb([96, SMAX], F32)
        sc_rot = [0]
        sc_nb = [4]
        sel64, B_sel = sb([128, 64], F32)
        S.op("pool", lambda e: e.memset(sel64[:], 0.0), writes=[B_sel])
        S.op("pool", lambda e: e.memset(sel64[64:65, :], 1.0), writes=[B_sel])

        def run_pipeline(blocks, G=2, LA=1, nb=4):
            sc_nb[0] = nb
            assert G * (LA + 1) <= nb
            groups = [blocks[i:i + G] for i in range(0, len(blocks), G)]
            n = len(groups)
            for i in range(n + LA):
                if i < n:
                    for b in groups[i]:
                        b[0]()
                if i >= LA:
                    for b in groups[i - LA]:
                        b[1]()

        def c_epilogue(h, t0):
            S.dma("sp", lambda e, h=h, t0=t0: e.dma_start(out=zt[:], in_=ZT[10 + h, :, t0:t0 + 512]), "ld_zt", writes=[B_zt])
            S.op("act", lambda e: e.activation(out=e1[:], in_=psum[5][:], func=AF.Ln), reads=[PB[5]], writes=[B_e1])
            S.op("act", lambda e: e.activation(out=e2[:], in_=psum[7][:], func=AF.Ln), reads=[PB[7]], writes=[B_e2])
            S.op("act", lambda e: e.activation(out=e1[:], in_=e1[:], func=AF.Exp, scale=-1.0), reads=[B_e1], writes=[B_e1])
            S.op("act", lambda e: e.activation(out=e2[:], in_=e2[:], func=AF.Exp, scale=-1.0), reads=[B_e2], writes=[B_e2])
            S.op("dve", lambda e: e.tensor_tensor(out=e1[:], in0=psum[4][:], in1=e1[:], op=ALU.mult), reads=[PB[4], B_e1],
                 writes=[B_e1])
            S.op("dve", lambda e: e.tensor_tensor(out=e2[:], in0=psum[6][:], in1=e2[:], op=ALU.mult), reads=[PB[6], B_e2],
                 writes=[B_e2])
            S.op("dve", lambda e: e.scalar_tensor_tensor(out=e3[:], in0=e2[:], scalar=nlam_t[:, 0:1], in1=e1[:], op0=ALU.mult,
                                                         op1=ALU.add), reads=[B_e1, B_e2, B_nlam], writes=[B_e3])
            S.op("act", lambda e: e.activation(out=esq[:], in_=e3[:], func=AF.Square), reads=[B_e3], writes=[B_esq])
            r = 5
            S.op("pe", lambda e, r=r: e.matmul(psum[r][:], lhsT=ones_bf[:], rhs=esq[:], start=True, stop=True), reads=[B_ones, B_esq],
                 writes=[PB[r]])
            S.op("act", lambda e, r=r: e.activation(out=e4[:], in_=psum[r][:], func=AF.Ln, bias=eps_t[:], scale=1.0 / 128),
                 reads=[PB[r], B_eps], writes=[B_e4])
            S.op("act", lambda e: e.activation(out=e4[:], in_=e4[:], func=AF.Exp, scale=-0.5), reads=[B_e4], writes=[B_e4])
            S.op("dve", lambda e: e.scalar_tensor_tensor(out=e3[:], in0=e3[:], scalar=sg_t[:, 0:1], in1=e4[:], op0=ALU.mult,
                                                         op1=ALU.mult), reads=[B_e3, B_sg, B_e4], writes=[B_e3])
            S.op("pool", lambda e: e.tensor_tensor(out=yzo[:], in0=e3[:], in1=zt[:], op=ALU.mult), reads=[B_e3, B_zt],
                 writes=[B_yzo])
            S.dma("pool", lambda e, h=h, t0=t0: e.dma_start(out=YZ[10 + h, :, t0:t0 + 512], in_=yzo[:]), "st_yzo", reads=[B_yzo])


        def score_exp(ktile, B_k, kcol, dh, qtile, B_q, q0, nq, scale, kstep=1, qstep=1):
            r = sc_rot[0] % sc_nb[0]
            sc_rot[0] += 1
            ptt, B_pt = PT[r]
            ksl = ktile[0:dh, kcol:kcol + 127 * kstep + 1:kstep] if kstep > 1 else ktile[0:dh, kcol:kcol + 128]
            qsl = qtile[0:dh, q0:q0 + (nq - 1) * qstep + 1:qstep] if qstep > 1 else qtile[0:dh, q0:q0 + nq]
            S.op("pe", lambda e, r=r, ksl=ksl, qsl=qsl, nq=nq: e.matmul(psum[r][:, 0:nq], lhsT=ksl, rhs=qsl, start=True, stop=True),
                 reads=[B_k, B_q], writes=[PB[r]])
            S.op("act", lambda e, r=r, nq=nq, ptt=ptt, scale=scale: e.activation(out=ptt[:, 0:nq], in_=psum[r][:, 0:nq], func=AF.Exp,
                                                                                 scale=scale), reads=[PB[r]], writes=[B_pt])
            return ptt, B_pt

        for (aq0, anq, ak0, ank, halo) in attn:
            Sq = ank
            s0 = ak0
            nkb = ank // 128
            nqt = anq // 512
            oth0 = aq0 + anq
            for kvh in range(2):
                kt, B_k = KT[0]
                S.dma("sp", lambda e, kvh=kvh, s0=s0, Sq=Sq, kt=kt: e.dma_start(out=kt[0:64, 0:Sq], in_=QK[30 + kvh, 0:64, s0:s0 + Sq]),
                      "ld_K0", writes=[B_k])
                S.dma("sp", lambda e, kvh=kvh, s0=s0, Sq=Sq, nkb=nkb: e.dma_start(
                    out=VT[:, 0:nkb, 0:64],
                    in_=VS[s0:s0 + Sq, 1152 + kvh * 64:1152 + (kvh + 1) * 64].rearrange("(b p) c -> p b c", p=128)),
                    "ld_V", writes=[B_VT])
                S.op("pool", lambda e, nkb=nkb: e.memset(VT[:, 0:nkb, 64:65], 1.0), writes=[B_VT])
                for g in range(3):
                    qh = kvh * 3 + g
                    qt_, B_q = QT[g % 2]
                    S.dma("sp", lambda e, qh=qh, aq0=aq0, anq=anq, qt_=qt_: e.dma_start(out=qt_[0:64, 0:anq],
                                                                                        in_=QK[24 + qh, 0:64, aq0:aq0 + anq]),
                          f"ld_Q{g % 2}", writes=[B_q])
                    blocks = []
                    for qt in range(nqt):
                        for kb in range(nkb):
                            st = {}

                            def s1(st=st, kb=kb, qt=qt, kt=kt, B_k=B_k, qt_=qt_, B_q=B_q):
                                st["pt"] = score_exp(kt, B_k, kb * 128, 64, qt_, B_q, qt * 512, 512, 0.125)

                            def s2(st=st, kb=kb, qt=qt, qh=qh, nkb=nkb, s0=aq0):
                                ptt, B_pt = st["pt"]
                                S.op("pe", lambda e, kb=kb, ptt=ptt, nkb=nkb: e.matmul(psum[6][0:65, :], lhsT=VT[:, kb, 0:65], rhs=ptt[:],
                                                                                      start=(kb == 0), stop=(kb == nkb - 1)),
                                     reads=[B_VT, B_pt], writes=[PB[6]])
                                if kb == nkb - 1:
                                    t0 = s0 + qt * 512
                                    S.dma("sp", lambda e, qh=qh, t0=t0: e.dma_start(out=zt[0:64, :], in_=ZT[4 + qh, 0:64, t0:t0 + 512]),
                                          "ld_zt", writes=[B_zt])
                                    S.op("act", lambda e: e.activation(out=e3[0:65, :], in_=psum[6][0:65, :], func=AF.Copy),
                                         reads=[PB[6]], writes=[B_e3])
                                    S.op("pe", lambda e: e.matmul(psum[7][0:64, :], lhsT=sel64[0:65, :], rhs=e3[0:65, :], start=True,
                                                                  stop=True), reads=[B_sel, B_e3], writes=[PB[7]])
                                    S.op("act", lambda e: e.activation(out=e1[0:64, :], in_=psum[7][0:64, :], func=AF.Ln), reads=[PB[7]],
                                         writes=[B_e1])
                                    S.op("act", lambda e: e.activation(out=e1[0:64, :], in_=e1[0:64, :], func=AF.Exp, scale=-1.0),
                                         reads=[B_e1], writes=[B_e1])
                                    S.op("dve", lambda e: e.tensor_tensor(out=e2[0:64, :], in0=e3[0:64, :], in1=e1[0:64, :], op=ALU.mult),
                                         reads=[B_e3, B_e1], writes=[B_e2])
                                    S.op("pool", lambda e: e.tensor_tensor(out=yzo[0:64, :], in0=e2[0:64, :], in1=zt[0:64, :], op=ALU.mult),
                                         reads=[B_e2, B_zt], writes=[B_yzo])
                                    S.dma("pool", lambda e, qh=qh, t0=t0: e.dma_start(out=YZ[4 + qh, 0:64, t0:t0 + 512], in_=yzo[0:64, :]),
                                          "st_yzo", reads=[B_yzo])

                            blocks.append((s1, s2))
                    run_pipeline(blocks, G=3, LA=1, nb=6)
            for h in range(4):
                for j in range(2):
                    kt, B_k = KT[j]
                    qt_, B_q = QT[j]
                    S.dma("sp", lambda e, h=h, j=j, s0=s0, Sq=Sq, kt=kt: e.dma_start(
                        out=kt[0:64, 0:Sq], in_=QK[40 + 2 * h + j, 0:64, s0:s0 + Sq]), f"ld_K{j}", writes=[B_k])
                    S.dma("sp", lambda e, h=h, j=j, aq0=aq0, anq=anq, qt_=qt_: e.dma_start(
                        out=qt_[0:64, 0:anq], in_=QK[32 + 2 * h + j, 0:64, aq0:aq0 + anq]), f"ld_Q{j}", writes=[B_q])
                S.dma("sp", lambda e, h=h, s0=s0, Sq=Sq, nkb=nkb: e.dma_start(
                    out=VT[:, 0:nkb, :],
                    in_=VS[s0:s0 + Sq, 1280 + h * 128:1280 + (h + 1) * 128].rearrange("(b p) c -> p b c", p=128)),
                    "ld_V", writes=[B_VT])
                blocks = []
                for qt in range(nqt):
                    for kb in range(nkb):
                        for j in range(2):
                            st = {}

                            def s1(st=st, kb=kb, qt=qt, j=j):
                                st["pt"] = score_exp(KT[j][0], KT[j][1], kb * 128, 64, QT[j][0], QT[j][1], qt * 512, 512, 0.125)

                            def s2(st=st, kb=kb, qt=qt, j=j, h=h, nkb=nkb, s0=aq0):
                                ptt, B_pt = st["pt"]
                                S.op("pe", lambda e, kb=kb, ptt=ptt, nkb=nkb, j=j: e.matmul(psum[4 + 2 * j][:, :], lhsT=VT[:, kb, :],
                                                                                           rhs=ptt[:], start=(kb == 0), stop=(kb == nkb - 1)),
                                     reads=[B_VT, B_pt], writes=[PB[4 + 2 * j]], sig=False)
                                S.op("pe", lambda e, kb=kb, ptt=ptt, nkb=nkb, j=j: e.matmul(psum[5 + 2 * j][:, :], lhsT=ones_bf[:, :],
                                                                                           rhs=ptt[:], start=(kb == 0), stop=(kb == nkb - 1)),
                                     reads=[B_ones, B_pt], writes=[PB[5 + 2 * j]])
                                if kb == nkb - 1 and j == 1:
                                    c_epilogue(h, s0 + qt * 512)

                            blocks.append((s1, s2))
                run_pipeline(blocks, G=2, LA=1, nb=4)
            scale_a = 96 ** -0.5
            for h in range(4):
                for g, (win, dil) in enumerate(A_GROUPS):
                    hh = g * 4 + h
                    Sq = anq
                    s0 = aq0
                    L = Sq // dil
                    T = min(512, L)
                    pad = 64 * dil
                    kt, B_k = KT[0]
                    qt_, B_q = QT[0]
                    if halo:
                        S.dma("sp", lambda e, hh=hh, kt=kt, pad=pad, oth0=oth0, anq=anq: e.dma_start(
                            out=kt[0:96, 0:pad], in_=QK[12 + hh, 0:96, oth0 + anq - pad:oth0 + anq]), "ld_K0", writes=[B_k])
                        S.dma("sp", lambda e, hh=hh, kt=kt, pad=pad, oth0=oth0, Sq=Sq: e.dma_start(
                            out=kt[0:96, pad + Sq:pad + Sq + pad], in_=QK[12 + hh, 0:96, oth0:oth0 + pad]), "ld_K0", writes=[B_k])
                    else:
                        S.op("pool", lambda e, kt=kt, pad=pad: e.memset(kt[0:96, 0:pad], 0.0), writes=[B_k])
                        S.op("pool", lambda e, kt=kt, pad=pad, Sq=Sq: e.memset(kt[0:96, pad + Sq:pad + Sq + pad], 0.0), writes=[B_k])
                    S.dma("sp", lambda e, hh=hh, s0=s0, Sq=Sq, kt=kt, pad=pad: e.dma_start(
                        out=kt[0:96, pad:pad + Sq], in_=QK[12 + hh, 0:96, s0:s0 + Sq]), "ld_K0", writes=[B_k])
                    S.dma("sp", lambda e, hh=hh, s0=s0, Sq=Sq, qt_=qt_: e.dma_start(out=qt_[0:96, 0:Sq], in_=QK[hh, 0:96, s0:s0 + Sq]),
                          "ld_Q0", writes=[B_q])
                    nblk = L // 128 + 1
                    for r in range(dil):
                        if halo:
                            lrow = oth0 + anq - 64 * dil + r
                            rrow = oth0 + r
                            vl = VS[lrow:lrow + 63 * dil + 1:dil, hh * 96:(hh + 1) * 96] if dil > 1 else VS[lrow:lrow + 64, hh * 96:(hh + 1) * 96]
                            vr = VS[rrow:rrow + 63 * dil + 1:dil, hh * 96:(hh + 1) * 96] if dil > 1 else VS[rrow:rrow + 64, hh * 96:(hh + 1) * 96]
                            S.dma("sp", lambda e, vl=vl: e.dma_start(out=VT[0:64, 0, 0:96], in_=vl), "ld_V", writes=[B_VT])
                            S.dma("sp", lambda e, vr=vr, nblk=nblk: e.dma_start(out=VT[64:128, nblk - 1, 0:96], in_=vr), "ld_V", writes=[B_VT])
                            S.op("dve", lambda e: e.tensor_scalar(out=VT[0:64, 0, 0:96], in0=VT[0:64, 0, 0:96], scalar1=flags[0:64, 0:1],
                                                                  scalar2=None, op0=ALU.mult), reads=[B_flags, B_VT], writes=[B_VT])
                            S.op("dve", lambda e, nblk=nblk: e.tensor_scalar(out=VT[64:128, nblk - 1, 0:96], in0=VT[64:128, nblk - 1, 0:96],
                                                                             scalar1=flags[64:128, 1:2], scalar2=None, op0=ALU.mult),
                                 reads=[B_flags, B_VT], writes=[B_VT])
                        else:
                            S.op("pool", lambda e: e.memset(VT[0:64, 0:1, 0:96], 0.0), writes=[B_VT])
                            S.op("pool", lambda e, nblk=nblk: e.memset(VT[64:128, nblk - 1:nblk, 0:96], 0.0), writes=[B_VT])
                        vsrc = VS[s0:s0 + Sq, hh * 96:(hh + 1) * 96].rearrange("(i d) c -> d i c", d=dil)[r]
                        vsrc = vsrc.rearrange("(b p) c -> p b c", p=128)
                        S.dma("sp", lambda e, vsrc=vsrc, nblk=nblk: e.dma_start(out=VT[64:128, 0:nblk - 1, 0:96], in_=vsrc[0:64]),
                              "ld_V", writes=[B_VT])
                        S.dma("sp", lambda e, vsrc=vsrc, nblk=nblk: e.dma_start(out=VT[0:64, 1:nblk, 0:96], in_=vsrc[64:128]),
                              "ld_V2", writes=[B_VT])
                        blocks = []
                        for qt in range(L // T):
                            i0 = qt * T
                            nb = T // 128 + 1
                            for b in range(nb):
                                st = {}
                                jb = i0 // 128 + b
                                w0 = max(i0, i0 - 128 + 128 * b)
                                w1 = min(i0 + T, i0 + 128 + 128 * b)
                                nq = w1 - w0
                                m0 = w0 - (i0 - 128 + 128 * b)
                                kcol = r + dil * 128 * jb
                                q0 = r + dil * w0

                                def s1(st=st, kcol=kcol, q0=q0, nq=nq, m0=m0, kt=kt, B_k=B_k, qt_=qt_, B_q=B_q, dil=dil):
                                    ptt, B_pt = score_exp(kt, B_k, kcol, 96, qt_, B_q, q0, nq, scale_a, kstep=dil, qstep=dil)
                                    S.op("pool", lambda e, ptt=ptt, nq=nq, m0=m0: e.tensor_tensor(
                                        out=ptt[:, 0:nq], in0=ptt[:, 0:nq], in1=mask_bf[:, m0:m0 + nq], op=ALU.mult),
                                        reads=[B_pt, B_mask], writes=[B_pt])
                                    st["pt"] = (ptt, B_pt)

                                def s2(st=st, b=b, nb=nb, jb=jb, nq=nq, w0=w0, i0=i0, T=T, nblk=nblk, g=g, r=r, dil=dil, halo=halo):
                                    ptt, B_pt = st["pt"]
                                    if b == 0:
                                        S.op("pe", lambda e, T=T: e.matmul(psum[4][0:96, 0:T], lhsT=zeros_bf[:, 0:96], rhs=zeros_w[:, 0:T],
                                                                           start=True, stop=False), reads=[B_zeros], writes=[PB[4]],
                                             sig=False)
                                        S.op("pe", lambda e, T=T: e.matmul(psum[5][0:96, 0:T], lhsT=zeros_bf[:, 0:96], rhs=zeros_w[:, 0:T],
                                                                           start=True, stop=False), reads=[B_zeros], writes=[PB[5]],
                                             sig=False)
                                    c0 = w0 - i0
                                    last = (b == nb - 1)
                                    S.op("pe", lambda e, jb=jb, ptt=ptt, nq=nq, c0=c0, last=last: e.matmul(
                                        psum[4][0:96, c0:c0 + nq], lhsT=VT[:, jb, 0:96], rhs=ptt[:, 0:nq], start=False, stop=last),
                                        reads=[B_VT, B_pt], writes=[PB[4]], sig=False)
                                    if halo:
                                        onesm = onesL if jb == 0 else (onesR if jb == nblk - 1 else ones_bf)
                                    else:
                                        onesm = ones_lo if jb == 0 else (ones_hi if jb == nblk - 1 else ones_bf)
                                    S.op("pe", lambda e, ptt=ptt, nq=nq, c0=c0, last=last, onesm=onesm: e.matmul(
                                        psum[5][0:96, c0:c0 + nq], lhsT=onesm[:, 0:96], rhs=ptt[:, 0:nq], start=False, stop=last),
                                        reads=[B_ones, B_onesL, B_pt], writes=[PB[5]])
                                    if not last:
                                        return
                                    a0 = r + dil * i0
                                    if dil > 1:
                                        ao_sl = AO[:, a0:a0 + dil * (T - 1) + 1:dil]
                                        al_sl = AL[:, a0:a0 + dil * (T - 1) + 1:dil]
                                    else:
                                        ao_sl = AO[:, a0:a0 + T]
                                        al_sl = AL[:, a0:a0 + T]
                                    if g == 0:
                                        S.op("dve", lambda e, ao_sl=ao_sl, T=T: e.tensor_copy(out=ao_sl, in_=psum[4][0:96, 0:T]),
                                             reads=[PB[4]], writes=[B_AO])
                                        S.op("act", lambda e, al_sl=al_sl, T=T: e.activation(out=al_sl, in_=psum[5][0:96, 0:T],
                                                                                            func=AF.Copy), reads=[PB[5]], writes=[B_AL])
                                    else:
                                        S.op("dve", lambda e, ao_sl=ao_sl, T=T: e.tensor_tensor(out=ao_sl, in0=psum[4][0:96, 0:T],
                                                                                               in1=ao_sl, op=ALU.add),
                                             reads=[PB[4], B_AO], writes=[B_AO])
                                        S.op("dve", lambda e, al_sl=al_sl, T=T: e.tensor_tensor(out=al_sl, in0=psum[5][0:96, 0:T],
                                                                                               in1=al_sl, op=ALU.add),
                                             reads=[PB[5], B_AL], writes=[B_AL])

                                blocks.append((s1, s2))
                        run_pipeline(blocks, G=2, LA=1, nb=4)
                for qt in range(nqt):
                    t0 = aq0 + qt * 512
                    c0 = qt * 512
                    S.dma("sp", lambda e, h=h, t0=t0: e.dma_start(out=zt[0:96, :], in_=ZT[h, 0:96, t0:t0 + 512]), "ld_zt", writes=[B_zt])
                    S.op("act", lambda e, c0=c0: e.activation(out=e1[0:96, :], in_=AL[:, c0:c0 + 512], func=AF.Ln), reads=[B_AL],
                         writes=[B_e1])
                    S.op("act", lambda e: e.activation(out=e1[0:96, :], in_=e1[0:96, :], func=AF.Exp, scale=-1.0), reads=[B_e1],
                         writes=[B_e1])
                    S.op("dve", lambda e, c0=c0: e.tensor_tensor(out=e2[0:96, :], in0=AO[:, c0:c0 + 512], in1=e1[0:96, :], op=ALU.mult),
                         reads=[B_AO, B_e1], writes=[B_e2])
                    S.op("pool", lambda e: e.tensor_tensor(out=yzo[0:96, :], in0=e2[0:96, :], in1=zt[0:96, :], op=ALU.mult),
                         reads=[B_e2, B_zt], writes=[B_yzo])
                    S.dma("pool", lambda e, h=h, t0=t0: e.dma_start(out=YZ[h, 0:96, t0:t0 + 512], in_=yzo[0:96, :]), "st_yzo",
                          reads=[B_yzo])

        phase_reset()
        wbg_bf, B_wbg = sb([128, 8, 3 * D], BF16)
        woa_bf, B_woa = sb([96, 4, D], BF16)
        wob_bf, B_wob = sb([64, 6, D], BF16)
        woc_bf, B_woc = sb([128, 4, D], BF16)
        wout_bf, B_wout = sb([128, 8, D], BF16)
        wstg, B_wstg = sb([128, 1024], F32)
        cvt = [0]

        def load_cast(dst_ap, src_ap, np_):
            S.dma("sp", lambda e: e.dma_start(out=wstg[0:np_, :], in_=src_ap), "ld_wstg", writes=[B_wstg])
            eng = ("act", "dve", "pool")[cvt[0] % 3]
            cvt[0] += 1
            if eng == "act":
                S.op("act", lambda e: e.activation(out=dst_ap, in_=wstg[0:np_, :], func=AF.Copy), reads=[B_wstg], writes=[B_wbg])
            else:
                S.op(eng, lambda e: e.tensor_copy(out=dst_ap, in_=wstg[0:np_, :]), reads=[B_wstg], writes=[B_wbg])

        for kc in range(8):
            for cg in range(3):
                load_cast(wbg_bf[:, kc, cg * 1024:(cg + 1) * 1024], w_bg[l, kc, :, cg * 1024:(cg + 1) * 1024], 128)
            load_cast(wout_bf[:, kc, :], w_out[l, kc], 128)
        for hh in range(4):
            load_cast(woa_bf[:, hh, :], w_oa[l, hh], 96)
            load_cast(woc_bf[:, hh, :], w_oc[l, hh], 128)
        for hh in range(6):
            load_cast(wob_bf[:, hh, :], w_ob[l, hh], 64)
        B_woa = B_wob = B_woc = B_wout = B_wbg

        hT, B_hT = sb([128, 8, 512], BF16)
        xT, B_xT = sb([128, 8, 512], F32)
        yz, B_yz = sb([128, 14, 512], BF16)
        gsb = [sb([128, 512], F32) for _ in range(3)]
        msb = [sb([128, 512], F32) for _ in range(3)]
        mg, B_mg = sb([128, 8, 512], BF16)
        xn, B_xn = sb([128, 8, 512], F32)
        ytok = [sb([128, D], F32) for _ in range(2)]
        last_layer = (l == depth - 1)
        for (sg0, sgn, si, full) in segs:
            if not full:
                continue
            for tt in range(sgn // 512):
                t0 = sg0 + tt * 512
                S.dma("sp", lambda e, t0=t0: e.dma_start(out=hT[:], in_=HT[:, :, t0:t0 + 512].rearrange("k p t -> p k t")), "ld_hT",
                      writes=[B_hT])
                S.dma("sp", lambda e, t0=t0, l=l: e.dma_start(out=xT[:], in_=XT[l][:, :, t0:t0 + 512].rearrange("k p t -> p k t")),
                      "ld_xT", writes=[B_xT])
                S.dma("sp", lambda e, t0=t0: e.dma_start(out=yz[0:96, 0:4, :], in_=YZ[0:4, 0:96, t0:t0 + 512].rearrange("c p t -> p c t")),
                      "ld_yz", writes=[B_yz])
                S.dma("sp", lambda e, t0=t0: e.dma_start(out=yz[0:64, 4:10, :], in_=YZ[4:10, 0:64, t0:t0 + 512].rearrange("c p t -> p c t")),
                      "ld_yz", writes=[B_yz])
                S.dma("sp", lambda e, t0=t0: e.dma_start(out=yz[:, 10:14, :], in_=YZ[10:14, :, t0:t0 + 512].rearrange("c p t -> p c t")),
                      "ld_yz", writes=[B_yz])
                for oc in range(8):
                    osl = slice(oc * 128, (oc + 1) * 128)
                    for hh in range(4):
                        S.op("pe", lambda e, hh=hh, osl=osl: e.matmul(psum[0][:], lhsT=woa_bf[0:96, hh, osl], rhs=yz[0:96, hh, :],
                                                                     start=(hh == 0), stop=(hh == 3)),
                             reads=[B_wbg, B_yz], writes=[PB[0]], sig=(hh == 3))
                    for hh in range(6):
                        S.op("pe", lambda e, hh=hh, osl=osl: e.matmul(psum[1][:], lhsT=wob_bf[0:64, hh, osl], rhs=yz[0:64, 4 + hh, :],
                                                                     start=(hh == 0), stop=(hh == 5)),
                             reads=[B_wbg, B_yz], writes=[PB[1]], sig=(hh == 5))
                    for hh in range(4):
                        S.op("pe", lambda e, hh=hh, osl=osl: e.matmul(psum[2][:], lhsT=woc_bf[:, hh, osl], rhs=yz[:, 10 + hh, :],
                                                                     start=(hh == 0), stop=(hh == 3)),
                             reads=[B_wbg, B_yz], writes=[PB[2]], sig=(hh == 3))
                    for br in range(3):
                        c0 = br * 1024 + oc * 128
                        for kc in range(8):
                            S.op("pe", lambda e, kc=kc, c0=c0, br=br: e.matmul(psum[3 + br][:], lhsT=wbg_bf[:, kc, c0:c0 + 128],
                                                                              rhs=hT[:, kc, :], start=(kc == 0), stop=(kc == 7)),
                                 reads=[B_wbg, B_hT], writes=[PB[3 + br]], sig=(kc == 7))
                        gt, B_g = gsb[br]
                        mt, B_m = msb[br]
                        ch = br * 8 + oc
                        S.op("act", lambda e, br=br, gt=gt, ch=ch: e.activation(out=gt[:], in_=psum[3 + br][:], func=AF.Sigmoid,
                                                                               bias=bbg_t[:, ch:ch + 1], scale=1.0),
                             reads=[PB[3 + br], B_bbg], writes=[B_g])
                        S.op("dve", lambda e, br=br, gt=gt, mt=mt: e.tensor_tensor(out=mt[:], in0=psum[br][:], in1=gt[:], op=ALU.mult),
                             reads=[PB[br], B_g], writes=[B_m])
                    S.op("pool", lambda e: e.tensor_tensor(out=msb[0][0][:], in0=msb[0][0][:], in1=msb[1][0][:], op=ALU.add),
                         reads=[msb[0][1], msb[1][1]], writes=[msb[0][1]])
                    S.op("pool", lambda e, oc=oc: e.tensor_tensor(out=mg[:, oc, :], in0=msb[0][0][:], in1=msb[2][0][:], op=ALU.add),
                         reads=[msb[0][1], msb[2][1]], writes=[B_mg])
                for oc in range(8):
                    osl = slice(oc * 128, (oc + 1) * 128)
                    pb = 6 + oc % 2
                    for kc in range(8):
                        S.op("pe", lambda e, kc=kc, osl=osl, pb=pb: e.matmul(psum[pb][:], lhsT=wout_bf[:, kc, osl], rhs=mg[:, kc, :],
                                                                            start=(kc == 0), stop=(kc == 7)),
                             reads=[B_wbg, B_mg], writes=[PB[pb]], sig=(kc == 7))
                    S.op("dve", lambda e, oc=oc, pb=pb, si=si: e.scalar_tensor_tensor(
                        out=xn[:, oc, :], in0=psum[pb][:], scalar=modT[:, 16 + oc, si:si + 1], in1=xT[:, oc, :], op0=ALU.mult,
                        op1=ALU.add), reads=[PB[pb], B_mod, B_xT], writes=[B_xn])
                if not last_layer:
                    S.dma("pool", lambda e, t0=t0, l=l: e.dma_start(out=XT[l + 1][:, :, t0:t0 + 512].rearrange("k p t -> p k t"),
                                                                   in_=xn[:]), "st_xn", reads=[B_xn])
                    if sg0 == SP:
                        o0 = t0 - SP
                        S.dma("pool", lambda e, o0=o0: e.dma_start(out=AGsrc[:, :, o0:o0 + 512].rearrange("k p t -> p k t"), in_=xn[:]),
                              "st_ag", reads=[B_xn], writes=[B_AGsrc])
                else:
                    for sub in range(4):
                        yt, B_y = ytok[sub % 2]
                        for hf in range(2):
                            pb = 0 + hf
                            for k4 in range(4):
                                kc = hf * 4 + k4
                                S.op("pe", lambda e, pb=pb, k4=k4, kc=kc, sub=sub: e.transpose(
                                    psum[pb][:, k4 * 128:(k4 + 1) * 128], xn[:, kc, sub * 128:(sub + 1) * 128], ident[:]),
                                    reads=[B_xn, B_ident], writes=[PB[pb]], sig=(k4 == 3))
                            if hf == 0:
                                S.op("dve", lambda e, pb=pb, yt=yt: e.tensor_copy(out=yt[:, 0:512], in_=psum[pb][:]), reads=[PB[pb]],
                                     writes=[B_y])
                            else:
                                S.op("act", lambda e, pb=pb, yt=yt: e.activation(out=yt[:, 512:1024], in_=psum[pb][:], func=AF.Copy),
                                     reads=[PB[pb]], writes=[B_y])
                        S.dma("pool", lambda e, t0=t0, sub=sub, yt=yt: e.dma_start(out=y_out[t0 + sub * 128:t0 + (sub + 1) * 128, :],
                                                                                  in_=yt[:]), f"st_y{sub % 2}", reads=[B_y])
        if not last_layer:
            for kc in range(8):
                S.cc(lambda e, kc=kc: e.collective_compute(
                    "AllGather", ALU.bypass, replica_groups=[[0, 1], [2, 3], [4, 5], [6, 7]],
                    ins=[AGsrc[kc]], outs=[AGdst[kc].rearrange("r p t -> (r p) t")]), f"cc_{l}_{kc}", reads=[B_AGsrc], writes=[B_AGdst])

    S.final_wait("sp")
    S.emit()
    return nc, S


def _prep_common(inp, depth):
    f = np.float32

    def A(x):
        return np.ascontiguousarray(np.asarray(x, dtype=f))

    gc = np.zeros((depth, 128, 6), f)
    for j, (k, n) in enumerate((("qn_a", 96), ("kn_a", 96), ("qn_b", 64), ("kn_b", 64), ("qn_c", 64), ("kn_c", 64))):
        gc[:, :n, j] = A(inp[k])
    lam = np.concatenate([A(inp["lam_q1"]), A(inp["lam_k1"]), A(inp["lam_q2"]), A(inp["lam_k2"])], axis=1)[:, None, :]
    return {
        "w_ada": A(inp["w_ada"]).reshape(depth, 8, 128, 3 * D),
        "b_adaT": A(A(inp["b_ada"]).reshape(depth, 24, 128).transpose(0, 2, 1)),
        "norm_gT": A(A(inp["norm_g"]).reshape(depth, 8, 128).transpose(0, 2, 1)),
        "w_in": A(inp["w_in"]).reshape(depth, 8, 128, DIN),
        "gcols": gc,
        "lam": A(lam),
        "subln": A(inp["subln_c"]).reshape(depth, 128, 1),
        "w_oa": A(inp["w_oa"]).reshape(depth, 4, 96, D),
        "w_ob": A(inp["w_ob"]).reshape(depth, 6, 64, D),
        "w_oc": A(inp["w_oc"]).reshape(depth, 4, 128, D),
        "w_bg": A(inp["w_bg"]).reshape(depth, 8, 128, 3 * D),
        "b_bgT": A(A(inp["b_bg"]).reshape(depth, 24, 128).transpose(0, 2, 1)),
        "w_out": A(inp["w_out"]).reshape(depth, 8, 128, D),
        "rotm": _rot_mats(),
        "bandmask": _band_mask(),
        "ident": np.eye(128, dtype=f),
    }


def _core_inputs(common, rope_g, xp_c, xs_j, cp_c, cs_j, hf, SP, H):
    m = dict(common)
    own = slice(hf * H, (hf + 1) * H)
    oth = slice((1 - hf) * H, (2 - hf) * H)
    m["x"] = np.ascontiguousarray(np.concatenate([xp_c, xs_j[own], xs_j[oth]], axis=0))
    m["rope"] = np.ascontiguousarray(np.concatenate([rope_g[..., 0:SP], rope_g[..., own], rope_g[..., oth]], axis=-1))
    cc = np.stack([cp_c, cs_j], axis=0)
    m["cT"] = np.ascontiguousarray(cc.reshape(2, 8, 128).transpose(2, 1, 0))
    fl = np.zeros((128, 2), np.float32)
    fl[:, 0] = float(hf)
    fl[:, 1] = float(1 - hf)
    m["flags"] = fl
    return m


def kernel(**inp):
    xp = np.asarray(inp["x_prompt"], np.float32)
    xs = np.asarray(inp["x_sample"], np.float32)
    cp = np.asarray(inp["c_prompt"], np.float32)
    cs = np.asarray(inp["c_sample"], np.float32)
    depth = int(np.asarray(inp["norm_g"]).shape[0])
    SP, SS = xp.shape[1], xs.shape[1]
    H = SS // 2
    lam_inits = [0.8 - 0.6 * math.exp(-0.3 * l) for l in range(depth)]
    nc, _ = build(SP, H, depth, lam_inits)
    common = _prep_common(inp, depth)
    rope_g = _rope_tables(max(SP, SS))
    in_maps = [_core_inputs(common, rope_g, xp[c], xs[c // 2], cp[c], cs[c // 2], c % 2, SP, H) for c in range(8)]
    res = run_bass_kernel_spmd(nc, in_maps, core_ids=list(range(8)))
    yp = np.stack([res.results[c]["y"][:SP] for c in range(8)], axis=0)
    ys = np.stack([np.concatenate([res.results[2 * j]["y"][SP:], res.results[2 * j + 1]["y"][SP:]], axis=0)
                   for j in range(xs.shape[0])], axis=0)
    return (yp.astype(np.float32), ys.astype(np.float32))
```

```python
import math
import types
import numpy as np
import concourse.bass as bass
import concourse.mybir as mybir
from concourse.bass_utils import run_bass_kernel_spmd

F32 = mybir.dt.float32
BF16 = mybir.dt.bfloat16
AF = mybir.ActivationFunctionType
ALU = mybir.AluOpType

D = 1024
DIN = 6912
EPS = 1e-6
A_GROUPS = ((128, 1), (512, 4), (2048, 16))
OFF = dict(qa=0, ka=1152, va=2304, za=3456, qb=3840, kb=4224, vb=4352, zb=4480, qc=4864, kc=5376, vc=5888, zc=6400)
ENGS = ("pe", "act", "dve", "pool", "sp")


def _freeze(fn):
    if fn.__closure__ is None:
        return fn
    cells = []
    for c in fn.__closure__:
        try:
            cells.append(types.CellType(c.cell_contents))
        except ValueError:
            cells.append(c)
    return types.FunctionType(fn.__code__, fn.__globals__, fn.__name__, fn.__defaults__, tuple(cells))


class Buf:
    __slots__ = ("name", "w", "r")

    def __init__(self, name):
        self.name = name
        self.w = None
        self.r = []


class Sched:
    def __init__(self, nc):
        self.nc = nc
        self.q = {e: [] for e in ENGS}
        self.sems = {}
        self.cnt = {}
        self.seen = {e: {} for e in ENGS}
        for e in ENGS:
            self._sem("E_" + e)
        self.n_ops = 0

    def _sem(self, key):
        if key not in self.sems:
            self.sems[key] = self.nc.alloc_semaphore(key)
            self.cnt[key] = 0
        return self.sems[key]

    def _need(self, eng, ev, force=False):
        if ev is None:
            return
        key, val = ev
        if val <= 0:
            return
        if eng == "pe" and key == "E_pe" and not force:
            return
        if self.seen[eng].get(key, 0) >= val:
            return
        self.seen[eng][key] = val
        self.q[eng].append(("wait", key, val))

    def _deps(self, eng, reads, writes):
        for b in reads:
            self._need(eng, b.w)
        for b in writes:
            self._need(eng, b.w)
            for ev in b.r:
                self._need(eng, ev)

    def _commit(self, ev, reads, writes):
        for b in reads:
            b.r.append(ev)
            if len(b.r) > 48:
                best = {}
                for k, v in b.r:
                    if best.get(k, 0) < v:
                        best[k] = v
                b.r = list(best.items())
        for b in writes:
            b.w = ev
            b.r = []

    def op(self, eng, fn, reads=(), writes=(), sig=True):
        self._deps(eng, reads, writes)
        key = "E_" + eng
        if sig:
            self.cnt[key] += 1
            ev = (key, self.cnt[key])
        else:
            ev = (key, self.cnt[key] + 1)
        self.q[eng].append(("op", _freeze(fn), key if sig else None))
        self._commit(ev, reads, writes)
        self.n_ops += 1

    def dma(self, eng, fn, sem_key, reads=(), writes=()):
        self._deps(eng, reads, writes)
        self._sem(sem_key)
        self.cnt[sem_key] += 16
        ev = (sem_key, self.cnt[sem_key])
        self.q[eng].append(("dma", _freeze(fn), sem_key))
        self._commit(ev, reads, writes)
        self.n_ops += 1

    def cc(self, fn, sem_key, reads=(), writes=()):
        eng = "pool"
        self._deps(eng, reads, writes)
        self._sem(sem_key)
        self.cnt[sem_key] += 1
        ev = (sem_key, self.cnt[sem_key])
        self.q[eng].append(("cc", _freeze(fn), sem_key))
        self._commit(ev, reads, writes)
        self.n_ops += 1

    def barrier(self):
        evs = [(k, v) for k, v in self.cnt.items() if v > 0]
        for e in ENGS:
            for ev in evs:
                if ev[0] != "E_" + e:
                    self._need(e, ev, force=True)

    def final_wait(self, eng="sp"):
        for k, v in self.cnt.items():
            if v > 0 and k != "E_" + eng:
                self._need(eng, (k, v), force=True)

    def emit(self):
        nc = self.nc
        engmap = {"pe": "tensor", "act": "scalar", "dve": "vector", "pool": "gpsimd", "sp": "sync"}
        sems = self.sems
        with nc.Block() as block:
            for e in ENGS:
                items = self.q[e]
                if not items:
                    continue

                def body(eng, items=items):
                    for it in items:
                        if it[0] == "wait":
                            eng.wait_ge(sems[it[1]], it[2])
                        elif it[0] == "op":
                            ins = it[1](eng)
                            if it[2] is not None:
                                ins.then_inc(sems[it[2]], 1)
                        elif it[0] == "cc":
                            it[1](eng).then_inc(sems[it[2]])
                        else:
                            it[1](eng).then_inc(sems[it[2]], 16)

                getattr(block, engmap[e])(body)


def _rope_tables(smax):
    pos = np.arange(smax)
    f32 = np.float32

    def ang(p, dim, theta):
        inv = (f32(theta) ** (-np.arange(0, dim, 2, dtype=f32) / f32(dim))).astype(f32)
        return (p.astype(f32)[:, None] * inv[None, :]).astype(f32)

    tab = np.zeros((3, 2, 128, smax), f32)
    tab[:, 0] = 1.0
    aa = ang(pos, 24, 500000.0)
    tab[0, 0, 0:12] = np.cos(aa).T
    tab[0, 0, 12:24] = np.cos(aa).T
    tab[0, 1, 0:12] = np.sin(aa).T
    tab[0, 1, 12:24] = np.sin(aa).T
    tab[0, :, 96:] = 0.0
    ar = ang(pos // 64, 32, 10000.0)
    ac = ang(pos % 64, 32, 10000.0)
    tab[1, 0, 0:16] = np.cos(ar).T
    tab[1, 0, 16:32] = np.cos(ar).T
    tab[1, 1, 0:16] = np.sin(ar).T
    tab[1, 1, 16:32] = np.sin(ar).T
    tab[1, 0, 32:48] = np.cos(ac).T
    tab[1, 0, 48:64] = np.cos(ac).T
    tab[1, 1, 32:48] = np.sin(ac).T
    tab[1, 1, 48:64] = np.sin(ac).T
    a_c = ang(pos, 16, 500000.0)
    tab[2, 0, 0:8] = np.cos(a_c).T
    tab[2, 0, 8:16] = np.cos(a_c).T
    tab[2, 1, 0:8] = np.sin(a_c).T
    tab[2, 1, 8:16] = np.sin(a_c).T
    return tab


def _rot_mats():
    R = np.zeros((3, 128, 128), np.float32)

    def fill(t, base, n):
        h = n // 2
        for i in range(h):
            R[t, base + i + h, base + i] = -1.0
            R[t, base + i, base + i + h] = 1.0

    fill(0, 0, 24)
    fill(1, 0, 32)
    fill(1, 32, 32)
    fill(2, 0, 16)
    return R


def _band_mask():
    p = np.arange(128)[:, None]
    j = np.arange(256)[None, :]
    return ((j >= p) & (j <= p + 128)).astype(np.float32)


def build(SP, H, depth, lam_inits):
    NT = SP + 2 * H
    NF = SP + H
    nseq = 2
    SMAX = max(SP, 2 * H)
    segs = [(0, SP, 0, True), (SP, H, 1, True), (SP + H, H, 1, False)]
    attn = [(0, SP, 0, SP, False), (SP, H, SP, 2 * H, True)]
    nc = bass.Bass("TRN2", target_bir_lowering=False)

    def din(name, shape, dt=F32):
        return nc.dram_tensor(name, list(shape), dt, kind="ExternalInput").ap()

    x_in = din("x", [NT, D])
    cT_in = din("cT", [128, 8, nseq])
    w_ada = din("w_ada", [depth, 8, 128, 3 * D])
    b_adaT = din("b_adaT", [depth, 128, 24])
    norm_gT = din("norm_gT", [depth, 128, 8])
    w_in = din("w_in", [depth, 8, 128, DIN])
    gcols_in = din("gcols", [depth, 128, 6])
    lam_in = din("lam", [depth, 1, 256])
    subln_in = din("subln", [depth, 128, 1])
    w_oa = din("w_oa", [depth, 4, 96, D])
    w_ob = din("w_ob", [depth, 6, 64, D])
    w_oc = din("w_oc", [depth, 4, 128, D])
    w_bg = din("w_bg", [depth, 8, 128, 3 * D])
    b_bgT = din("b_bgT", [depth, 128, 24])
    w_out = din("w_out", [depth, 8, 128, D])
    rope_in = din("rope", [3, 2, 128, NT])
    flags_in = din("flags", [128, 2])
    rot_in = din("rotm", [3, 128, 128])
    mask_in = din("bandmask", [128, 256])
    ident_in = din("ident", [128, 128])
    y_out = nc.dram_tensor("y", [NF, D], F32, kind="ExternalOutput").ap()

    def dscr(name, shape, dt):
        return nc.dram_tensor(name, list(shape), dt).ap()

    XT = [dscr(f"XT{l}", [8, 128, NT], F32) for l in range(depth)]
    HT = dscr("HT", [8, 128, NT], BF16)
    QK = dscr("QK", [48, 128, NT], BF16)
    ZT = dscr("ZT", [14, 128, NT], BF16)
    VS = dscr("VS", [NT, 1792], BF16)
    YZ = dscr("YZ", [14, 128, NT], BF16)
    AGsrc = dscr("AGsrc", [8, 128, H], F32)
    AGdst = dscr("AGdst", [8, 2, 128, H], F32)
    B_AGsrc, B_AGdst = Buf("AGsrc"), Buf("AGdst")

    S = Sched(nc)
    _bn = [0]

    def sb(shape, dt, name=None):
        _bn[0] += 1
        name = "s_" + (name or f"t{_bn[0]}")
        return nc.alloc_sbuf_tensor(name, list(shape), dt), Buf(name)

    psum = [nc.alloc_psum_tensor(f"ps{i}", [128, 512], F32) for i in range(8)]
    PB = [Buf(f"ps{i}") for i in range(8)]

    ident, B_ident = sb([128, 128], F32, "ident")
    ones_bf, B_ones = sb([128, 128], BF16, "ones")
    zeros_bf, B_zeros = sb([128, 128], BF16, "zeros")
    zeros_w, _ = sb([128, 512], BF16, "zerosw")
    ones_lo, _ = sb([128, 96], BF16, "oneslo")
    ones_hi, _ = sb([128, 96], BF16, "oneshi")
    eps_t, B_eps = sb([128, 1], F32, "eps")
    mask_f, B_maskf = sb([128, 256], F32, "maskf")
    mask_bf, B_mask = sb([128, 256], BF16, "maskbf")
    rot_f, B_rotf = sb([128, 3, 128], F32, "rotf")
    cT, B_cT = sb([128, 8, nseq], F32, "cT")
    scT, B_scT = sb([128, 8, nseq], F32, "scT")

    S.dma("sp", lambda e: e.dma_start(out=ident[:], in_=ident_in), "ld_ident", writes=[B_ident])
    S.dma("sp", lambda e: e.dma_start(out=mask_f[:], in_=mask_in), "ld_mask", writes=[B_maskf])
    S.dma("sp", lambda e: e.dma_start(out=rot_f[:], in_=rot_in.rearrange("t p m -> p t m")), "ld_rot", writes=[B_rotf])
    S.dma("sp", lambda e: e.dma_start(out=cT[:], in_=cT_in), "ld_cT", writes=[B_cT])
    S.op("pool", lambda e: e.memset(ones_bf[:], 1.0), writes=[B_ones])
    S.op("pool", lambda e: e.memset(zeros_bf[:], 0.0), writes=[B_zeros])
    S.op("pool", lambda e: e.memset(zeros_w[:], 0.0), writes=[B_zeros])
    S.op("pool", lambda e: e.memset(ones_lo[:], 1.0), writes=[B_ones])
    S.op("pool", lambda e: e.memset(ones_lo[0:64, :], 0.0), writes=[B_ones])
    S.op("pool", lambda e: e.memset(ones_hi[:], 0.0), writes=[B_ones])
    S.op("pool", lambda e: e.memset(ones_hi[0:64, :], 1.0), writes=[B_ones])
    S.op("pool", lambda e: e.memset(eps_t[:], EPS), writes=[B_eps])
    S.op("dve", lambda e: e.tensor_copy(out=mask_bf[:], in_=mask_f[:]), reads=[B_maskf], writes=[B_mask])
    S.op("act", lambda e: e.activation(out=scT[:], in_=cT[:], func=AF.Silu), reads=[B_cT], writes=[B_scT])
    flags, B_flags = sb([128, 2], F32, "flags")
    onesL, B_onesL = sb([128, 96], BF16, "onesL")
    onesR, _ = sb([128, 96], BF16, "onesR")
    S.dma("sp", lambda e: e.dma_start(out=flags[:], in_=flags_in), "ld_flags", writes=[B_flags])
    S.op("pool", lambda e: e.memset(onesL[:], 1.0), writes=[B_onesL])
    S.op("pool", lambda e: e.memset(onesR[:], 1.0), writes=[B_onesL])
    S.op("dve", lambda e: e.tensor_scalar(out=onesL[0:64, :], in0=onesL[0:64, :], scalar1=flags[0:64, 0:1], scalar2=None, op0=ALU.mult),
         reads=[B_flags, B_onesL], writes=[B_onesL])
    S.op("dve", lambda e: e.tensor_scalar(out=onesR[64:128, :], in0=onesR[64:128, :], scalar1=flags[64:128, 1:2], scalar2=None,
                                          op0=ALU.mult), reads=[B_flags, B_onesL], writes=[B_onesL])

    modT, B_mod = sb([128, 24, nseq], F32, "modT")
    gsT, B_gs = sb([128, 8, nseq], F32, "gsT")
    b_ada_t, B_bada = sb([128, 24], F32, "bada")
    ng_t, B_ng = sb([128, 8], F32, "ng")
    gcols, B_gcols = sb([128, 6], F32, "gcols")
    bbg_t, B_bbg = sb([128, 24], F32, "bbg")
    subln_t, B_subln = sb([128, 1], F32, "subln")
    sg_t, B_sg = sb([128, 1], F32, "sg")
    lam_t, B_lam = sb([1, 256], F32, "lam")
    lam_w, B_lamw = sb([1, 8], F32, "lamw")
    lam_bf, B_lambf = sb([1, 128], F32, "lambf")
    nlam_t, B_nlam = sb([128, 1], F32, "nlam")
    rotg, B_rotg = sb([128, 6, 128], BF16, "rotg")

    SB_TOP_CONST = nc.sbuf_base

    def phase_reset():
        S.barrier()
        nc.sbuf_base = SB_TOP_CONST

    for l in range(depth):
        lam_init = lam_inits[l]
        phase_reset()
        S.dma("sp", lambda e, l=l: e.dma_start(out=b_ada_t[:], in_=b_adaT[l]), "ld_bada", writes=[B_bada])
        S.dma("sp", lambda e, l=l: e.dma_start(out=ng_t[:], in_=norm_gT[l]), "ld_ng", writes=[B_ng])
        S.dma("sp", lambda e, l=l: e.dma_start(out=gcols[:], in_=gcols_in[l]), "ld_gcols", writes=[B_gcols])
        S.dma("sp", lambda e, l=l: e.dma_start(out=bbg_t[:], in_=b_bgT[l]), "ld_bbg", writes=[B_bbg])
        S.dma("sp", lambda e, l=l: e.dma_start(out=subln_t[:], in_=subln_in[l]), "ld_subln", writes=[B_subln])
        S.dma("sp", lambda e, l=l: e.dma_start(out=lam_t[:], in_=lam_in[l]), "ld_lam", writes=[B_lam])
        S.op("dve", lambda e: e.tensor_scalar(out=sg_t[:], in0=subln_t[:], scalar1=float(1.0 - lam_init), scalar2=None,
                                              op0=ALU.mult), reads=[B_subln], writes=[B_sg])
        S.op("dve", lambda e: e.tensor_tensor(out=lam_t[:, 0:64], in0=lam_t[:, 0:64], in1=lam_t[:, 64:128], op=ALU.mult),
             reads=[B_lam], writes=[B_lam])
        S.op("dve", lambda e: e.tensor_tensor(out=lam_t[:, 128:192], in0=lam_t[:, 128:192], in1=lam_t[:, 192:256], op=ALU.mult),
             reads=[B_lam], writes=[B_lam])
        S.op("dve", lambda e: e.reduce_sum(out=lam_w[:, 0:1], in_=lam_t[:, 0:64], axis=mybir.AxisListType.X),
             reads=[B_lam], writes=[B_lamw])
        S.op("dve", lambda e: e.reduce_sum(out=lam_w[:, 1:2], in_=lam_t[:, 128:192], axis=mybir.AxisListType.X),
             reads=[B_lam], writes=[B_lamw])
        S.op("act", lambda e: e.activation(out=lam_w[:, 2:4], in_=lam_w[:, 0:2], func=AF.Exp), reads=[B_lamw], writes=[B_lamw])
        S.op("dve", lambda e: e.tensor_tensor(out=lam_w[:, 4:5], in0=lam_w[:, 3:4], in1=lam_w[:, 2:3], op=ALU.subtract),
             reads=[B_lamw], writes=[B_lamw])
        S.op("dve", lambda e: e.tensor_scalar(out=lam_w[:, 5:6], in0=lam_w[:, 4:5], scalar1=float(-lam_init), scalar2=None,
                                              op0=ALU.add), reads=[B_lamw], writes=[B_lamw])
        S.op("pool", lambda e: e.memset(lam_bf[:], 1.0), writes=[B_lambf])
        S.op("pe", lambda e: e.matmul(psum[0][:, 0:1], lhsT=lam_bf[0:1, :], rhs=lam_w[0:1, 5:6], start=True, stop=True),
             reads=[B_lambf, B_lamw], writes=[PB[0]])
        S.op("dve", lambda e: e.tensor_copy(out=nlam_t[:], in_=psum[0][:, 0:1]), reads=[PB[0]], writes=[B_nlam])
        for j in range(6):
            S.op("dve", lambda e, j=j: e.tensor_scalar(out=rotg[:, j, :], in0=rot_f[:, j // 2, :], scalar1=gcols[:, j:j + 1],
                                                       scalar2=None, op0=ALU.mult),
                 reads=[B_rotf, B_gcols], writes=[B_rotg])
        wst, B_wst = sb([128, 8, 1536], F32)
        for half in range(2):
            for kc in range(8):
                S.dma("sp", lambda e, l=l, kc=kc, half=half: e.dma_start(
                    out=wst[:, kc, :], in_=w_ada[l, kc, :, half * 1536:(half + 1) * 1536]), "ld_wst", writes=[B_wst])
            for cc in range(12):
                ch = half * 12 + cc
                for kc in range(8):
                    S.op("pe", lambda e, kc=kc, cc=cc, ch=ch: e.matmul(
                        psum[1][:, ch * nseq:(ch + 1) * nseq], lhsT=wst[:, kc, cc * 128:(cc + 1) * 128], rhs=scT[:, kc, :],
                        start=(kc == 0), stop=(kc == 7)), reads=[B_wst, B_scT], writes=[PB[1]], sig=(kc == 7))
        for s in range(nseq):
            S.op("dve", lambda e, s=s: e.tensor_tensor(
                out=modT[:, :, s], in0=psum[1][:, 0:24 * nseq].rearrange("p (c s) -> p c s", s=nseq)[:, :, s], in1=b_ada_t[:],
                op=ALU.add), reads=[PB[1], B_bada], writes=[B_mod])
            S.op("dve", lambda e, s=s: e.scalar_tensor_tensor(
                out=gsT[:, :, s], in0=modT[:, 8:16, s], scalar=1.0, in1=ng_t[:], op0=ALU.add, op1=ALU.mult),
                reads=[B_mod, B_ng], writes=[B_gs])

        phase_reset()
        win_bf, B_win = sb([128, 8, DIN], BF16)
        wstg, B_wstg = sb([128, 1152], F32)
        for kc in range(8):
            for cg in range(6):
                S.dma("sp", lambda e, l=l, kc=kc, cg=cg: e.dma_start(
                    out=wstg[:], in_=w_in[l, kc, :, cg * 1152:(cg + 1) * 1152]), "ld_wstg", writes=[B_wstg])
                eng = ("act", "dve", "pool")[(kc * 6 + cg) % 3]
                if eng == "act":
                    S.op("act", lambda e, kc=kc, cg=cg: e.activation(out=win_bf[:, kc, cg * 1152:(cg + 1) * 1152], in_=wstg[:],
                                                                     func=AF.Copy), reads=[B_wstg], writes=[B_win])
                else:
                    S.op(eng, lambda e, kc=kc, cg=cg: e.tensor_copy(out=win_bf[:, kc, cg * 1152:(cg + 1) * 1152], in_=wstg[:]),
                         reads=[B_wstg], writes=[B_win])
        xT, B_xT = sb([128, 8, 512], F32)
        hT, B_hT = sb([128, 8, 512], BF16)
        if l == 0:
            xtok, B_xtok = sb([128, D], F32)
        tabs, B_tabs = sb([128, 3, 2, 512], F32)
        sq = [sb([128, 512], BF16) for _ in range(2)]
        ubf = [sb([128, 512], BF16) for _ in range(2)]
        rs = [sb([128, 512], F32) for _ in range(2)]
        t1 = [sb([128, 512], F32) for _ in range(3)]
        t2 = [sb([128, 512], F32) for _ in range(2)]
        qo = [sb([128, 512], BF16) for _ in range(2)]
        zo = [sb([128, 512], BF16) for _ in range(2)]
        vo = [sb([128, 1792], BF16) for _ in range(2)]
        tmpn, B_tmpn = sb([128, 512], F32)

        qk_chunks = []
        for h in range(12):
            qk_chunks.append((OFF["qa"] + 96 * h, 96, 0, 0, h))
        for h in range(12):
            qk_chunks.append((OFF["ka"] + 96 * h, 96, 0, 1, 12 + h))
        for h in range(6):
            qk_chunks.append((OFF["qb"] + 64 * h, 64, 1, 2, 24 + h))
        for h in range(2):
            qk_chunks.append((OFF["kb"] + 64 * h, 64, 1, 3, 30 + h))
        for h in range(8):
            qk_chunks.append((OFF["qc"] + 64 * h, 64, 2, 4, 32 + h))
        for h in range(8):
            qk_chunks.append((OFF["kc"] + 64 * h, 64, 2, 5, 40 + h))
        z_chunks = []
        for h in range(4):
            z_chunks.append((OFF["za"] + 96 * h, 96, h))
        for h in range(6):
            z_chunks.append((OFF["zb"] + 64 * h, 64, 4 + h))
        for h in range(4):
            z_chunks.append((OFF["zc"] + 128 * h, 128, 10 + h))
        v_groups = [(OFF["va"], 512, 0), (OFF["va"] + 512, 512, 512), (OFF["va"] + 1024, 128, 1024),
                    (OFF["vb"], 128, 1152), (OFF["vc"], 512, 1280)]

        it = 0
        if l > 0:
            xa, B_xa = sb([128, 8, 512], F32)
        for (sg0, sgn, si, full) in segs:
            for tt in range(sgn // 512):
                t0 = sg0 + tt * 512
                if l > 0 and not full:
                    o0 = t0 - (SP + H)
                    S.dma("sp", lambda e, o0=o0: e.dma_start(out=xa[:], in_=AGdst[:, 0, :, o0:o0 + 512].rearrange("k p t -> p k t")),
                          "ld_xa", reads=[B_AGdst], writes=[B_xa])
                    S.dma("sp", lambda e, o0=o0: e.dma_start(out=xT[:], in_=AGdst[:, 1, :, o0:o0 + 512].rearrange("k p t -> p k t")),
                          "ld_xT", reads=[B_AGdst], writes=[B_xT])
                    S.op("pool", lambda e: e.tensor_scalar(out=xa[:], in0=xa[:], scalar1=flags[:, 0:1], scalar2=None, op0=ALU.mult),
                         reads=[B_xa, B_flags], writes=[B_xa])
                    S.op("dve", lambda e: e.scalar_tensor_tensor(out=xT[:], in0=xT[:], scalar=flags[:, 1:2], in1=xa[:], op0=ALU.mult,
                                                                 op1=ALU.add), reads=[B_xT, B_xa, B_flags], writes=[B_xT])
                elif l == 0:
                    for sub in range(4):
                        S.dma("sp", lambda e, t0=t0, sub=sub: e.dma_start(out=xtok[:], in_=x_in[t0 + sub * 128:t0 + (sub + 1) * 128, :]),
                              "ld_xtok", writes=[B_xtok])
                        for hf in range(2):
                            pb = 6 + hf
                            for k4 in range(4):
                                kc = hf * 4 + k4
                                S.op("pe", lambda e, pb=pb, k4=k4, kc=kc: e.transpose(
                                    psum[pb][:, k4 * 128:(k4 + 1) * 128], xtok[:, kc * 128:(kc + 1) * 128], ident[:]),
                                    reads=[B_xtok, B_ident], writes=[PB[pb]], sig=(k4 == 3))
                            S.op("dve" if hf == 0 else "act",
                                 (lambda e, pb=pb, hf=hf, sub=sub: e.tensor_copy(
                                     out=xT[:, hf * 4:(hf + 1) * 4, sub * 128:(sub + 1) * 128],
                                     in_=psum[pb][:].rearrange("p (k t) -> p k t", k=4))) if hf == 0 else
                                 (lambda e, pb=pb, hf=hf, sub=sub: e.activation(
                                     out=xT[:, hf * 4:(hf + 1) * 4, sub * 128:(sub + 1) * 128],
                                     in_=psum[pb][:].rearrange("p (k t) -> p k t", k=4), func=AF.Copy)),
                                 reads=[PB[pb]], writes=[B_xT])
                    if full:
                        S.dma("pool", lambda e, t0=t0, l=l: e.dma_start(
                            out=XT[l][:, :, t0:t0 + 512].rearrange("k p t -> p k t"), in_=xT[:]), "st_xT", reads=[B_xT])
                else:
                    S.dma("sp", lambda e, t0=t0, l=l: e.dma_start(
                        out=xT[:], in_=XT[l][:, :, t0:t0 + 512].rearrange("k p t -> p k t")), "ld_xT", writes=[B_xT])
                for kc in range(8):
                    sqt, B_sq = sq[kc % 2]
                    S.op("act", lambda e, kc=kc, sqt=sqt: e.activation(out=sqt[:], in_=xT[:, kc, :], func=AF.Square),
                         reads=[B_xT], writes=[B_sq])
                    S.op("pe", lambda e, kc=kc, sqt=sqt: e.matmul(psum[7][:], lhsT=ones_bf[:], rhs=sqt[:], start=(kc == 0),
                                                                  stop=(kc == 7)), reads=[B_ones, B_sq], writes=[PB[7]])
                rst, B_rs = rs[0]
                S.op("act", lambda e, rst=rst: e.activation(out=rst[:], in_=psum[7][:], func=AF.Sqrt, bias=eps_t[:], scale=1.0 / D),
                     reads=[PB[7], B_eps], writes=[B_rs])
                S.op("dve", lambda e, rst=rst: e.reciprocal(out=rst[:], in_=rst[:]), reads=[B_rs], writes=[B_rs])
                for kc in range(8):
                    S.op("dve", lambda e, kc=kc, rst=rst: e.tensor_tensor(out=tmpn[:], in0=xT[:, kc, :], in1=rst[:], op=ALU.mult),
                         reads=[B_xT, B_rs], writes=[B_tmpn])
                    S.op("act", lambda e, kc=kc, si=si: e.activation(out=hT[:, kc, :], in_=tmpn[:], func=AF.Identity,
                                                                     bias=modT[:, kc, si:si + 1], scale=gsT[:, kc, si:si + 1]),
                         reads=[B_tmpn, B_mod, B_gs], writes=[B_hT])
                if full:
                    S.dma("pool", lambda e, t0=t0: e.dma_start(out=HT[:, :, t0:t0 + 512].rearrange("k p t -> p k t"), in_=hT[:]),
                          "st_hT", reads=[B_hT])
                S.dma("sp", lambda e, t0=t0: e.dma_start(
                    out=tabs[:], in_=rope_in[:, :, :, t0:t0 + 512].rearrange("t c p s -> p t c s")), "ld_tabs", writes=[B_tabs])
                def stA(c0, dh, ty, gj, cid, u):
                    pu = u % 3
                    sqt, B_sq = sq[u % len(sq)]
                    ubt, B_ub = ubf[u % len(ubf)]
                    for kc in range(8):
                        S.op("pe", lambda e, kc=kc, c0=c0, dh=dh, pu=pu: e.matmul(
                            psum[pu][0:dh, :], lhsT=win_bf[:, kc, c0:c0 + dh], rhs=hT[:, kc, :], start=(kc == 0), stop=(kc == 7)),
                            reads=[B_win, B_hT], writes=[PB[pu]], sig=(kc == 7))
                    S.op("act", lambda e, dh=dh, pu=pu, sqt=sqt: e.activation(out=sqt[0:dh, :], in_=psum[pu][0:dh, :], func=AF.Square),
                         reads=[PB[pu]], writes=[B_sq])
                    S.op("act", lambda e, dh=dh, pu=pu, ubt=ubt: e.activation(out=ubt[0:dh, :], in_=psum[pu][0:dh, :], func=AF.Copy),
                         reads=[PB[pu]], writes=[B_ub])

                def stB(c0, dh, ty, gj, cid, u, t0=t0):
                    pu, pss, prt = u % 3, 3 + u % 2, 5 + u % 2
                    sqt, B_sq = sq[u % len(sq)]
                    ubt, B_ub = ubf[u % len(ubf)]
                    rst, B_rs = rs[u % len(rs)]
                    t1t, B_t1 = t1[u % len(t1)]
                    t2t, B_t2 = t2[u % len(t2)]
                    qot, B_qo = qo[u % len(qo)]
                    S.op("pe", lambda e, dh=dh, pss=pss, sqt=sqt: e.matmul(psum[pss][0:dh, :], lhsT=ones_bf[0:dh, 0:dh], rhs=sqt[0:dh, :],
                                                                          start=True, stop=True),
                         reads=[B_ones, B_sq], writes=[PB[pss]])
                    S.op("pe", lambda e, dh=dh, prt=prt, ubt=ubt, gj=gj: e.matmul(psum[prt][0:dh, :], lhsT=rotg[0:dh, gj, 0:dh],
                                                                                 rhs=ubt[0:dh, :], start=True, stop=True),
                         reads=[B_rotg, B_ub], writes=[PB[prt]])
                    S.op("act", lambda e, dh=dh, pss=pss, rst=rst: e.activation(out=rst[0:dh, :], in_=psum[pss][0:dh, :], func=AF.Sqrt,
                                                                               bias=eps_t[0:dh, :], scale=1.0 / dh),
                         reads=[PB[pss], B_eps], writes=[B_rs])
                    S.op("dve", lambda e, dh=dh, rst=rst: e.reciprocal(out=rst[0:dh, :], in_=rst[0:dh, :]), reads=[B_rs], writes=[B_rs])
                    S.op("dve", lambda e, dh=dh, pu=pu, t1t=t1t, gj=gj, ty=ty: e.scalar_tensor_tensor(
                        out=t1t[0:dh, :], in0=psum[pu][0:dh, :], scalar=gcols[0:dh, gj:gj + 1], in1=tabs[0:dh, ty, 0, :],
                        op0=ALU.mult, op1=ALU.mult), reads=[PB[pu], B_gcols, B_tabs], writes=[B_t1])
                    S.op("dve", lambda e, dh=dh, prt=prt, t2t=t2t, ty=ty: e.tensor_tensor(
                        out=t2t[0:dh, :], in0=psum[prt][0:dh, :], in1=tabs[0:dh, ty, 1, :], op=ALU.mult),
                        reads=[PB[prt], B_tabs], writes=[B_t2])
                    S.op("pool", lambda e, dh=dh, t1t=t1t, t2t=t2t: e.tensor_tensor(out=t1t[0:dh, :], in0=t1t[0:dh, :], in1=t2t[0:dh, :],
                                                                                   op=ALU.add), reads=[B_t1, B_t2], writes=[B_t1])
                    S.op("pool", lambda e, dh=dh, t1t=t1t, rst=rst, qot=qot: e.tensor_tensor(out=qot[0:dh, :], in0=t1t[0:dh, :],
                                                                                            in1=rst[0:dh, :], op=ALU.mult),
                         reads=[B_t1, B_rs], writes=[B_qo])
                    S.dma("pool", lambda e, dh=dh, cid=cid, t0=t0, qot=qot: e.dma_start(out=QK[cid, 0:dh, t0:t0 + 512], in_=qot[0:dh, :]),
                          f"st_qo{u % len(qo)}", reads=[B_qo])

                chs = qk_chunks if full else [c for c in qk_chunks if c[3] % 2 == 1]
                nch = len(chs)
                for i in range(nch + 1):
                    if i < nch:
                        stA(*chs[i], it + i)
                    if i >= 1:
                        stB(*chs[i - 1], it + i - 1)
                it += nch
                for (c0, dz, cid) in (z_chunks if full else []):
                    u = it % 2
                    pu = it % 3
                    it += 1
                    zot, B_zo = zo[u]
                    for kc in range(8):
                        S.op("pe", lambda e, kc=kc, c0=c0, dz=dz, pu=pu: e.matmul(
                            psum[pu][0:dz, :], lhsT=win_bf[:, kc, c0:c0 + dz], rhs=hT[:, kc, :], start=(kc == 0), stop=(kc == 7)),
                            reads=[B_win, B_hT], writes=[PB[pu]], sig=(kc == 7))
                    S.op("act", lambda e, dz=dz, pu=pu, zot=zot: e.activation(out=zot[0:dz, :], in_=psum[pu][0:dz, :], func=AF.Silu),
                         reads=[PB[pu]], writes=[B_zo])
                    S.dma("pool", lambda e, dz=dz, cid=cid, t0=t0, zot=zot: e.dma_start(out=ZT[cid, 0:dz, t0:t0 + 512], in_=zot[0:dz, :]),
                          f"st_zo{u}", reads=[B_zo])
                for sub in range(4):
                    vot, B_vo = vo[sub % 2]
                    for gi, (c0, n, o0) in enumerate(v_groups):
                        pu = it % 3
                        it += 1
                        for kc in range(8):
                            S.op("pe", lambda e, kc=kc, c0=c0, n=n, pu=pu, sub=sub: e.matmul(
                                psum[pu][:, 0:n], lhsT=hT[:, kc, sub * 128:(sub + 1) * 128], rhs=win_bf[:, kc, c0:c0 + n],
                                start=(kc == 0), stop=(kc == 7)), reads=[B_win, B_hT], writes=[PB[pu]], sig=(kc == 7))
                        if gi % 2 == 0:
                            S.op("dve", lambda e, n=n, pu=pu, o0=o0, vot=vot: e.tensor_copy(out=vot[:, o0:o0 + n], in_=psum[pu][:, 0:n]),
                                 reads=[PB[pu]], writes=[B_vo])
                        else:
                            S.op("act", lambda e, n=n, pu=pu, o0=o0, vot=vot: e.activation(out=vot[:, o0:o0 + n], in_=psum[pu][:, 0:n],
                                                                                          func=AF.Copy), reads=[PB[pu]], writes=[B_vo])
                    S.dma("pool", lambda e, t0=t0, sub=sub, vot=vot: e.dma_start(out=VS[t0 + sub * 128:t0 + (sub + 1) * 128, :], in_=vot[:]),
                          f"st_vo{sub % 2}", reads=[B_vo])

        phase_reset()
        KW = max(SMAX, H + 2048)
        QW = max(SP, H)
        KTs = [[sb([128, KW], BF16) for _ in range(2)] for _ in range(2)]
        QTs = [[sb([128, QW], BF16) for _ in range(2)] for _ in range(2)]
        VTs = [sb([128, SMAX // 128 + 1, 128], BF16) for _ in range(2)]
        VTA = [VTs[0], sb([128, H // 128 + 1, 96], BF16)]
        vta_i = [0]
        PT = [sb([128, 512], BF16) for _ in range(6)]
        zt, B_zt = sb([128, 512], BF16)
        e1, B_e1 = sb([128, 512], F32)
        e2, B_e2 = sb([128, 512], F32)
        e3, B_e3 = sb([128, 512], F32)
        e4, B_e4 = sb([128, 512], F32)
        esq, B_esq = sb([128, 512], BF16)
        yzo, B_yzo = sb([128, 512], BF16)
        AO, B_AO = sb([96, QW], F32)
        AL, B_AL = sb([96, QW], F32)
        sc_rot = [0]
        sc_nb = [4]
        sel64, B_sel = sb([128, 64], F32)
        S.op("pool", lambda e: e.memset(sel64[:], 0.0), writes=[B_sel])
        S.op("pool", lambda e: e.memset(sel64[64:65, :], 1.0), writes=[B_sel])

        def run_pipeline(blocks, G=2, LA=1, nb=4):
            sc_nb[0] = nb
            assert G * (LA + 1) <= nb
            groups = [blocks[i:i + G] for i in range(0, len(blocks), G)]
            n = len(groups)
            for i in range(n + LA):
                if i < n:
                    for b in groups[i]:
                        b[0]()
                if i >= LA:
                    for b in groups[i - LA]:
                        b[1]()

        def c_epilogue(h, t0):
            S.dma("sp", lambda e, h=h, t0=t0: e.dma_start(out=zt[:], in_=ZT[10 + h, :, t0:t0 + 512]), "ld_zt", writes=[B_zt])
            S.op("act", lambda e: e.activation(out=e1[:], in_=psum[5][:], func=AF.Ln), reads=[PB[5]], writes=[B_e1])
            S.op("act", lambda e: e.activation(out=e2[:], in_=psum[7][:], func=AF.Ln), reads=[PB[7]], writes=[B_e2])
            S.op("act", lambda e: e.activation(out=e1[:], in_=e1[:], func=AF.Exp, scale=-1.0), reads=[B_e1], writes=[B_e1])
            S.op("act", lambda e: e.activation(out=e2[:], in_=e2[:], func=AF.Exp, scale=-1.0), reads=[B_e2], writes=[B_e2])
            S.op("dve", lambda e: e.tensor_tensor(out=e1[:], in0=psum[4][:], in1=e1[:], op=ALU.mult), reads=[PB[4], B_e1],
                 writes=[B_e1])
            S.op("dve", lambda e: e.tensor_tensor(out=e2[:], in0=psum[6][:], in1=e2[:], op=ALU.mult), reads=[PB[6], B_e2],
                 writes=[B_e2])
            S.op("dve", lambda e: e.scalar_tensor_tensor(out=e3[:], in0=e2[:], scalar=nlam_t[:, 0:1], in1=e1[:], op0=ALU.mult,
                                                         op1=ALU.add), reads=[B_e1, B_e2, B_nlam], writes=[B_e3])
            S.op("act", lambda e: e.activation(out=esq[:], in_=e3[:], func=AF.Square), reads=[B_e3], writes=[B_esq])
            r = 5
            S.op("pe", lambda e, r=r: e.matmul(psum[r][:], lhsT=ones_bf[:], rhs=esq[:], start=True, stop=True), reads=[B_ones, B_esq],
                 writes=[PB[r]])
            S.op("act", lambda e, r=r: e.activation(out=e4[:], in_=psum[r][:], func=AF.Ln, bias=eps_t[:], scale=1.0 / 128),
                 reads=[PB[r], B_eps], writes=[B_e4])
            S.op("act", lambda e: e.activation(out=e4[:], in_=e4[:], func=AF.Exp, scale=-0.5), reads=[B_e4], writes=[B_e4])
            S.op("dve", lambda e: e.scalar_tensor_tensor(out=e3[:], in0=e3[:], scalar=sg_t[:, 0:1], in1=e4[:], op0=ALU.mult,
                                                         op1=ALU.mult), reads=[B_e3, B_sg, B_e4], writes=[B_e3])
            S.op("pool", lambda e: e.tensor_tensor(out=yzo[:], in0=e3[:], in1=zt[:], op=ALU.mult), reads=[B_e3, B_zt],
                 writes=[B_yzo])
            S.dma("pool", lambda e, h=h, t0=t0: e.dma_start(out=YZ[10 + h, :, t0:t0 + 512], in_=yzo[:]), "st_yzo", reads=[B_yzo])


        def score_exp(ktile, B_k, kcol, dh, qtile, B_q, q0, nq, scale, kstep=1, qstep=1):
            r = sc_rot[0] % sc_nb[0]
            sc_rot[0] += 1
            ptt, B_pt = PT[r]
            ksl = ktile[0:dh, kcol:kcol + 127 * kstep + 1:kstep] if kstep > 1 else ktile[0:dh, kcol:kcol + 128]
            qsl = qtile[0:dh, q0:q0 + (nq - 1) * qstep + 1:qstep] if qstep > 1 else qtile[0:dh, q0:q0 + nq]
            S.op("pe", lambda e, r=r, ksl=ksl, qsl=qsl, nq=nq: e.matmul(psum[r][:, 0:nq], lhsT=ksl, rhs=qsl, start=True, stop=True),
                 reads=[B_k, B_q], writes=[PB[r]])
            S.op("act", lambda e, r=r, nq=nq, ptt=ptt, scale=scale: e.activation(out=ptt[:, 0:nq], in_=psum[r][:, 0:nq], func=AF.Exp,
                                                                                 scale=scale), reads=[PB[r]], writes=[B_pt])
            return ptt, B_pt

        for (aq0, anq, ak0, ank, halo) in attn:
            Sq = ank
            s0 = ak0
            nkb = ank // 128
            nqt = anq // 512
            oth0 = aq0 + anq
            def b_load_kv(kvh):
                bs = kvh % 2
                kt, B_k = KTs[bs][0]
                vt_, B_vt = VTs[bs]
                S.dma("sp", lambda e, kvh=kvh, s0=s0, Sq=Sq, kt=kt: e.dma_start(out=kt[0:64, 0:Sq], in_=QK[30 + kvh, 0:64, s0:s0 + Sq]),
                      f"ld_K{bs}0", writes=[B_k])
                S.dma("sp", lambda e, kvh=kvh, s0=s0, Sq=Sq, nkb=nkb, vt_=vt_: e.dma_start(
                    out=vt_[:, 0:nkb, 0:64],
                    in_=VS[s0:s0 + Sq, 1152 + kvh * 64:1152 + (kvh + 1) * 64].rearrange("(b p) c -> p b c", p=128)),
                    f"ld_V{bs}", writes=[B_vt])
                S.op("pool", lambda e, nkb=nkb, vt_=vt_: e.memset(vt_[:, 0:nkb, 64:65], 1.0), writes=[B_vt])

            def b_load_q(kvh, g):
                bs = kvh % 2
                qh = kvh * 3 + g
                qt_, B_q = QTs[bs][g % 2]
                S.dma("sp", lambda e, qh=qh, aq0=aq0, anq=anq, qt_=qt_: e.dma_start(out=qt_[0:64, 0:anq],
                                                                                    in_=QK[24 + qh, 0:64, aq0:aq0 + anq]),
                      f"ld_Q{bs}{g % 2}", writes=[B_q])

            b_units = [(kvh, g) for kvh in range(2) for g in range(3)]
            b_load_kv(0)
            b_load_q(0, 0)
            for kvh in range(2):
                bs = kvh % 2
                kt, B_k = KTs[bs][0]
                vt_, B_vt = VTs[bs]
                for g in range(3):
                    qh = kvh * 3 + g
                    qt_, B_q = QTs[bs][g % 2]
                    ui = b_units.index((kvh, g))
                    if ui + 1 < len(b_units):
                        nk, ng = b_units[ui + 1]
                        if nk != kvh:
                            b_load_kv(nk)
                        b_load_q(nk, ng)
                    blocks = []
                    for qt in range(nqt):
                        for kb in range(nkb):
                            st = {}

                            def s1(st=st, kb=kb, qt=qt, kt=kt, B_k=B_k, qt_=qt_, B_q=B_q):
                                st["pt"] = score_exp(kt, B_k, kb * 128, 64, qt_, B_q, qt * 512, 512, 0.125)

                            def s2(st=st, kb=kb, qt=qt, qh=qh, nkb=nkb, s0=aq0, vt_=vt_, B_vt=B_vt):
                                ptt, B_pt = st["pt"]
                                S.op("pe", lambda e, kb=kb, ptt=ptt, nkb=nkb, vt_=vt_: e.matmul(psum[6][0:65, :], lhsT=vt_[:, kb, 0:65],
                                                                                               rhs=ptt[:], start=(kb == 0),
                                                                                               stop=(kb == nkb - 1)),
                                     reads=[B_vt, B_pt], writes=[PB[6]])
                                if kb == nkb - 1:
                                    t0 = s0 + qt * 512
                                    S.dma("sp", lambda e, qh=qh, t0=t0: e.dma_start(out=zt[0:64, :], in_=ZT[4 + qh, 0:64, t0:t0 + 512]),
                                          "ld_zt", writes=[B_zt])
                                    S.op("act", lambda e: e.activation(out=e3[0:65, :], in_=psum[6][0:65, :], func=AF.Copy),
                                         reads=[PB[6]], writes=[B_e3])
                                    S.op("pe", lambda e: e.matmul(psum[7][0:64, :], lhsT=sel64[0:65, :], rhs=e3[0:65, :], start=True,
                                                                  stop=True), reads=[B_sel, B_e3], writes=[PB[7]])
                                    S.op("act", lambda e: e.activation(out=e1[0:64, :], in_=psum[7][0:64, :], func=AF.Ln), reads=[PB[7]],
                                         writes=[B_e1])
                                    S.op("act", lambda e: e.activation(out=e1[0:64, :], in_=e1[0:64, :], func=AF.Exp, scale=-1.0),
                                         reads=[B_e1], writes=[B_e1])
                                    S.op("dve", lambda e: e.tensor_tensor(out=e2[0:64, :], in0=e3[0:64, :], in1=e1[0:64, :], op=ALU.mult),
                                         reads=[B_e3, B_e1], writes=[B_e2])
                                    S.op("pool", lambda e: e.tensor_tensor(out=yzo[0:64, :], in0=e2[0:64, :], in1=zt[0:64, :], op=ALU.mult),
                                         reads=[B_e2, B_zt], writes=[B_yzo])
                                    S.dma("pool", lambda e, qh=qh, t0=t0: e.dma_start(out=YZ[4 + qh, 0:64, t0:t0 + 512], in_=yzo[0:64, :]),
                                          "st_yzo", reads=[B_yzo])

                            blocks.append((s1, s2))
                    run_pipeline(blocks, G=3, LA=1, nb=6)
            def c_load(h):
                bs = h % 2
                vt_, B_vt = VTs[bs]
                for j in range(2):
                    kt, B_k = KTs[bs][j]
                    qt_, B_q = QTs[bs][j]
                    S.dma("sp", lambda e, h=h, j=j, s0=s0, Sq=Sq, kt=kt: e.dma_start(
                        out=kt[0:64, 0:Sq], in_=QK[40 + 2 * h + j, 0:64, s0:s0 + Sq]), f"ld_K{bs}{j}", writes=[B_k])
                    S.dma("sp", lambda e, h=h, j=j, aq0=aq0, anq=anq, qt_=qt_: e.dma_start(
                        out=qt_[0:64, 0:anq], in_=QK[32 + 2 * h + j, 0:64, aq0:aq0 + anq]), f"ld_Q{bs}{j}", writes=[B_q])
                S.dma("sp", lambda e, h=h, s0=s0, Sq=Sq, nkb=nkb, vt_=vt_: e.dma_start(
                    out=vt_[:, 0:nkb, :],
                    in_=VS[s0:s0 + Sq, 1280 + h * 128:1280 + (h + 1) * 128].rearrange("(b p) c -> p b c", p=128)),
                    f"ld_V{bs}", writes=[B_vt])

            c_load(0)
            for h in range(4):
                bs = h % 2
                vt_, B_vt = VTs[bs]
                if h + 1 < 4:
                    c_load(h + 1)
                blocks = []
                for qt in range(nqt):
                    for kb in range(nkb):
                        for j in range(2):
                            st = {}

                            def s1(st=st, kb=kb, qt=qt, j=j, bs=bs):
                                st["pt"] = score_exp(KTs[bs][j][0], KTs[bs][j][1], kb * 128, 64, QTs[bs][j][0], QTs[bs][j][1], qt * 512, 512,
                                                     0.125)

                            def s2(st=st, kb=kb, qt=qt, j=j, h=h, nkb=nkb, s0=aq0, vt_=vt_, B_vt=B_vt):
                                ptt, B_pt = st["pt"]
                                S.op("pe", lambda e, kb=kb, ptt=ptt, nkb=nkb, j=j, vt_=vt_: e.matmul(psum[4 + 2 * j][:, :], lhsT=vt_[:, kb, :],
                                                                                                    rhs=ptt[:], start=(kb == 0),
                                                                                                    stop=(kb == nkb - 1)),
                                     reads=[B_vt, B_pt], writes=[PB[4 + 2 * j]], sig=False)
                                S.op("pe", lambda e, kb=kb, ptt=ptt, nkb=nkb, j=j: e.matmul(psum[5 + 2 * j][:, :], lhsT=ones_bf[:, :],
                                                                                           rhs=ptt[:], start=(kb == 0), stop=(kb == nkb - 1)),
                                     reads=[B_ones, B_pt], writes=[PB[5 + 2 * j]])
                                if kb == nkb - 1 and j == 1:
                                    c_epilogue(h, s0 + qt * 512)

                            blocks.append((s1, s2))
                run_pipeline(blocks, G=2, LA=1, nb=4)
            scale_a = 96 ** -0.5
            for h in range(4):
                for g, (win, dil) in enumerate(A_GROUPS):
                    hh = g * 4 + h
                    Sq = anq
                    s0 = aq0
                    L = Sq // dil
                    T = min(512, L)
                    pad = 64 * dil
                    kt, B_k = KTs[0][0]
                    qt_, B_q = QTs[0][0]
                    if halo:
                        S.dma("sp", lambda e, hh=hh, kt=kt, pad=pad, oth0=oth0, anq=anq: e.dma_start(
                            out=kt[0:96, 0:pad], in_=QK[12 + hh, 0:96, oth0 + anq - pad:oth0 + anq]), "ld_K00", writes=[B_k])
                        S.dma("sp", lambda e, hh=hh, kt=kt, pad=pad, oth0=oth0, Sq=Sq: e.dma_start(
                            out=kt[0:96, pad + Sq:pad + Sq + pad], in_=QK[12 + hh, 0:96, oth0:oth0 + pad]), "ld_K00", writes=[B_k])
                    else:
                        S.op("pool", lambda e, kt=kt, pad=pad: e.memset(kt[0:96, 0:pad], 0.0), writes=[B_k])
                        S.op("pool", lambda e, kt=kt, pad=pad, Sq=Sq: e.memset(kt[0:96, pad + Sq:pad + Sq + pad], 0.0), writes=[B_k])
                    S.dma("sp", lambda e, hh=hh, s0=s0, Sq=Sq, kt=kt, pad=pad: e.dma_start(
                        out=kt[0:96, pad:pad + Sq], in_=QK[12 + hh, 0:96, s0:s0 + Sq]), "ld_K00", writes=[B_k])
                    S.dma("sp", lambda e, hh=hh, s0=s0, Sq=Sq, qt_=qt_: e.dma_start(out=qt_[0:96, 0:Sq], in_=QK[hh, 0:96, s0:s0 + Sq]),
                          "ld_Q00", writes=[B_q])
                    nblk = L // 128 + 1
                    for r in range(dil):
                        bi = vta_i[0] % 2
                        VTa, B_VTa = VTA[bi]
                        vta_i[0] += 1
                        if halo:
                            lrow = oth0 + anq - 64 * dil + r
                            rrow = oth0 + r
                            vl = VS[lrow:lrow + 63 * dil + 1:dil, hh * 96:(hh + 1) * 96] if dil > 1 else VS[lrow:lrow + 64, hh * 96:(hh + 1) * 96]
                            vr = VS[rrow:rrow + 63 * dil + 1:dil, hh * 96:(hh + 1) * 96] if dil > 1 else VS[rrow:rrow + 64, hh * 96:(hh + 1) * 96]
                            S.dma("sp", lambda e, vl=vl: e.dma_start(out=VTa[0:64, 0, 0:96], in_=vl), f"ld_Va{bi}", writes=[B_VTa])
                            S.dma("sp", lambda e, vr=vr, nblk=nblk: e.dma_start(out=VTa[64:128, nblk - 1, 0:96], in_=vr), f"ld_Va{bi}", writes=[B_VTa])
                            S.op("dve", lambda e: e.tensor_scalar(out=VTa[0:64, 0, 0:96], in0=VTa[0:64, 0, 0:96], scalar1=flags[0:64, 0:1],
                                                                  scalar2=None, op0=ALU.mult), reads=[B_flags, B_VTa], writes=[B_VTa])
                            S.op("dve", lambda e, nblk=nblk: e.tensor_scalar(out=VTa[64:128, nblk - 1, 0:96], in0=VTa[64:128, nblk - 1, 0:96],
                                                                             scalar1=flags[64:128, 1:2], scalar2=None, op0=ALU.mult),
                                 reads=[B_flags, B_VTa], writes=[B_VTa])
                        else:
                            S.op("pool", lambda e: e.memset(VTa[0:64, 0:1, 0:96], 0.0), writes=[B_VTa])
                            S.op("pool", lambda e, nblk=nblk: e.memset(VTa[64:128, nblk - 1:nblk, 0:96], 0.0), writes=[B_VTa])
                        vsrc = VS[s0:s0 + Sq, hh * 96:(hh + 1) * 96].rearrange("(i d) c -> d i c", d=dil)[r]
                        vsrc = vsrc.rearrange("(b p) c -> p b c", p=128)
                        S.dma("sp", lambda e, vsrc=vsrc, nblk=nblk: e.dma_start(out=VTa[64:128, 0:nblk - 1, 0:96], in_=vsrc[0:64]),
                              f"ld_Va{bi}", writes=[B_VTa])
                        S.dma("sp", lambda e, vsrc=vsrc, nblk=nblk: e.dma_start(out=VTa[0:64, 1:nblk, 0:96], in_=vsrc[64:128]),
                              f"ld_Vb{bi}", writes=[B_VTa])
                        blocks = []
                        for qt in range(L // T):
                            i0 = qt * T
                            nb = T // 128 + 1
                            for b in range(nb):
                                st = {}
                                jb = i0 // 128 + b
                                w0 = max(i0, i0 - 128 + 128 * b)
                                w1 = min(i0 + T, i0 + 128 + 128 * b)
                                nq = w1 - w0
                                m0 = w0 - (i0 - 128 + 128 * b)
                                kcol = r + dil * 128 * jb
                                q0 = r + dil * w0

                                def s1(st=st, kcol=kcol, q0=q0, nq=nq, m0=m0, kt=kt, B_k=B_k, qt_=qt_, B_q=B_q, dil=dil):
                                    ptt, B_pt = score_exp(kt, B_k, kcol, 96, qt_, B_q, q0, nq, scale_a, kstep=dil, qstep=dil)
                                    S.op("pool", lambda e, ptt=ptt, nq=nq, m0=m0: e.tensor_tensor(
                                        out=ptt[:, 0:nq], in0=ptt[:, 0:nq], in1=mask_bf[:, m0:m0 + nq], op=ALU.mult),
                                        reads=[B_pt, B_mask], writes=[B_pt])
                                    st["pt"] = (ptt, B_pt)

                                def s2(st=st, b=b, nb=nb, jb=jb, nq=nq, w0=w0, i0=i0, T=T, nblk=nblk, g=g, r=r, dil=dil, halo=halo, VTa=VTa, B_VTa=B_VTa):
                                    ptt, B_pt = st["pt"]
                                    if b == 0:
                                        S.op("pe", lambda e, T=T: e.matmul(psum[4][0:96, 0:T], lhsT=zeros_bf[:, 0:96], rhs=zeros_w[:, 0:T],
                                                                           start=True, stop=False), reads=[B_zeros], writes=[PB[4]],
                                             sig=False)
                                        S.op("pe", lambda e, T=T: e.matmul(psum[5][0:96, 0:T], lhsT=zeros_bf[:, 0:96], rhs=zeros_w[:, 0:T],
                                                                           start=True, stop=False), reads=[B_zeros], writes=[PB[5]],
                                             sig=False)
                                    c0 = w0 - i0
                                    last = (b == nb - 1)
                                    S.op("pe", lambda e, jb=jb, ptt=ptt, nq=nq, c0=c0, last=last: e.matmul(
                                        psum[4][0:96, c0:c0 + nq], lhsT=VTa[:, jb, 0:96], rhs=ptt[:, 0:nq], start=False, stop=last),
                                        reads=[B_VTa, B_pt], writes=[PB[4]], sig=False)
                                    if halo:
                                        onesm = onesL if jb == 0 else (onesR if jb == nblk - 1 else ones_bf)
                                    else:
                                        onesm = ones_lo if jb == 0 else (ones_hi if jb == nblk - 1 else ones_bf)
                                    S.op("pe", lambda e, ptt=ptt, nq=nq, c0=c0, last=last, onesm=onesm: e.matmul(
                                        psum[5][0:96, c0:c0 + nq], lhsT=onesm[:, 0:96], rhs=ptt[:, 0:nq], start=False, stop=last),
                                        reads=[B_ones, B_onesL, B_pt], writes=[PB[5]])
                                    if not last:
                                        return
                                    a0 = r + dil * i0
                                    if dil > 1:
                                        ao_sl = AO[:, a0:a0 + dil * (T - 1) + 1:dil]
                                        al_sl = AL[:, a0:a0 + dil * (T - 1) + 1:dil]
                                    else:
                                        ao_sl = AO[:, a0:a0 + T]
                                        al_sl = AL[:, a0:a0 + T]
                                    if g == 0:
                                        S.op("dve", lambda e, ao_sl=ao_sl, T=T: e.tensor_copy(out=ao_sl, in_=psum[4][0:96, 0:T]),
                                             reads=[PB[4]], writes=[B_AO])
                                        S.op("act", lambda e, al_sl=al_sl, T=T: e.activation(out=al_sl, in_=psum[5][0:96, 0:T],
                                                                                            func=AF.Copy), reads=[PB[5]], writes=[B_AL])
                                    else:
                                        S.op("dve", lambda e, ao_sl=ao_sl, T=T: e.tensor_tensor(out=ao_sl, in0=psum[4][0:96, 0:T],
                                                                                               in1=ao_sl, op=ALU.add),
                                             reads=[PB[4], B_AO], writes=[B_AO])
                                        S.op("dve", lambda e, al_sl=al_sl, T=T: e.tensor_tensor(out=al_sl, in0=psum[5][0:96, 0:T],
                                                                                               in1=al_sl, op=ALU.add),
                                             reads=[PB[5], B_AL], writes=[B_AL])

                                blocks.append((s1, s2))
                        run_pipeline(blocks, G=2, LA=1, nb=4)
                for qt in range(nqt):
                    t0 = aq0 + qt * 512
                    c0 = qt * 512
                    S.dma("sp", lambda e, h=h, t0=t0: e.dma_start(out=zt[0:96, :], in_=ZT[h, 0:96, t0:t0 + 512]), "ld_zt", writes=[B_zt])
                    S.op("act", lambda e, c0=c0: e.activation(out=e1[0:96, :], in_=AL[:, c0:c0 + 512], func=AF.Ln), reads=[B_AL],
                         writes=[B_e1])
                    S.op("act", lambda e: e.activation(out=e1[0:96, :], in_=e1[0:96, :], func=AF.Exp, scale=-1.0), reads=[B_e1],
                         writes=[B_e1])
                    S.op("dve", lambda e, c0=c0: e.tensor_tensor(out=e2[0:96, :], in0=AO[:, c0:c0 + 512], in1=e1[0:96, :], op=ALU.mult),
                         reads=[B_AO, B_e1], writes=[B_e2])
                    S.op("pool", lambda e: e.tensor_tensor(out=yzo[0:96, :], in0=e2[0:96, :], in1=zt[0:96, :], op=ALU.mult),
                         reads=[B_e2, B_zt], writes=[B_yzo])
                    S.dma("pool", lambda e, h=h, t0=t0: e.dma_start(out=YZ[h, 0:96, t0:t0 + 512], in_=yzo[0:96, :]), "st_yzo",
                          reads=[B_yzo])

        phase_reset()
        wbg_bf, B_wbg = sb([128, 8, 3 * D], BF16)
        woa_bf, B_woa = sb([96, 4, D], BF16)
        wob_bf, B_wob = sb([64, 6, D], BF16)
        woc_bf, B_woc = sb([128, 4, D], BF16)
        wout_bf, B_wout = sb([128, 8, D], BF16)
        wstg, B_wstg = sb([128, 1024], F32)
        cvt = [0]

        def load_cast(dst_ap, src_ap, np_):
            S.dma("sp", lambda e: e.dma_start(out=wstg[0:np_, :], in_=src_ap), "ld_wstg", writes=[B_wstg])
            eng = ("act", "dve", "pool")[cvt[0] % 3]
            cvt[0] += 1
            if eng == "act":
                S.op("act", lambda e: e.activation(out=dst_ap, in_=wstg[0:np_, :], func=AF.Copy), reads=[B_wstg], writes=[B_wbg])
            else:
                S.op(eng, lambda e: e.tensor_copy(out=dst_ap, in_=wstg[0:np_, :]), reads=[B_wstg], writes=[B_wbg])

        for kc in range(8):
            for cg in range(3):
                load_cast(wbg_bf[:, kc, cg * 1024:(cg + 1) * 1024], w_bg[l, kc, :, cg * 1024:(cg + 1) * 1024], 128)
            load_cast(wout_bf[:, kc, :], w_out[l, kc], 128)
        for hh in range(4):
            load_cast(woa_bf[:, hh, :], w_oa[l, hh], 96)
            load_cast(woc_bf[:, hh, :], w_oc[l, hh], 128)
        for hh in range(6):
            load_cast(wob_bf[:, hh, :], w_ob[l, hh], 64)
        B_woa = B_wob = B_woc = B_wout = B_wbg

        hT, B_hT = sb([128, 8, 512], BF16)
        xT, B_xT = sb([128, 8, 512], F32)
        yz, B_yz = sb([128, 14, 512], BF16)
        gsb = [sb([128, 512], F32) for _ in range(3)]
        msb = [sb([128, 512], F32) for _ in range(3)]
        mg, B_mg = sb([128, 8, 512], BF16)
        xn, B_xn = sb([128, 8, 512], F32)
        ytok = [sb([128, D], F32) for _ in range(2)]
        last_layer = (l == depth - 1)
        for (sg0, sgn, si, full) in segs:
            if not full:
                continue
            for tt in range(sgn // 512):
                t0 = sg0 + tt * 512
                S.dma("sp", lambda e, t0=t0: e.dma_start(out=hT[:], in_=HT[:, :, t0:t0 + 512].rearrange("k p t -> p k t")), "ld_hT",
                      writes=[B_hT])
                S.dma("sp", lambda e, t0=t0, l=l: e.dma_start(out=xT[:], in_=XT[l][:, :, t0:t0 + 512].rearrange("k p t -> p k t")),
                      "ld_xT", writes=[B_xT])
                S.dma("sp", lambda e, t0=t0: e.dma_start(out=yz[0:96, 0:4, :], in_=YZ[0:4, 0:96, t0:t0 + 512].rearrange("c p t -> p c t")),
                      "ld_yz", writes=[B_yz])
                S.dma("sp", lambda e, t0=t0: e.dma_start(out=yz[0:64, 4:10, :], in_=YZ[4:10, 0:64, t0:t0 + 512].rearrange("c p t -> p c t")),
                      "ld_yz", writes=[B_yz])
                S.dma("sp", lambda e, t0=t0: e.dma_start(out=yz[:, 10:14, :], in_=YZ[10:14, :, t0:t0 + 512].rearrange("c p t -> p c t")),
                      "ld_yz", writes=[B_yz])
                for oc in range(8):
                    osl = slice(oc * 128, (oc + 1) * 128)
                    for hh in range(4):
                        S.op("pe", lambda e, hh=hh, osl=osl: e.matmul(psum[0][:], lhsT=woa_bf[0:96, hh, osl], rhs=yz[0:96, hh, :],
                                                                     start=(hh == 0), stop=(hh == 3)),
                             reads=[B_wbg, B_yz], writes=[PB[0]], sig=(hh == 3))
                    for hh in range(6):
                        S.op("pe", lambda e, hh=hh, osl=osl: e.matmul(psum[1][:], lhsT=wob_bf[0:64, hh, osl], rhs=yz[0:64, 4 + hh, :],
                                                                     start=(hh == 0), stop=(hh == 5)),
                             reads=[B_wbg, B_yz], writes=[PB[1]], sig=(hh == 5))
                    for hh in range(4):
                        S.op("pe", lambda e, hh=hh, osl=osl: e.matmul(psum[2][:], lhsT=woc_bf[:, hh, osl], rhs=yz[:, 10 + hh, :],
                                                                     start=(hh == 0), stop=(hh == 3)),
                             reads=[B_wbg, B_yz], writes=[PB[2]], sig=(hh == 3))
                    for br in range(3):
                        c0 = br * 1024 + oc * 128
                        for kc in range(8):
                            S.op("pe", lambda e, kc=kc, c0=c0, br=br: e.matmul(psum[3 + br][:], lhsT=wbg_bf[:, kc, c0:c0 + 128],
                                                                              rhs=hT[:, kc, :], start=(kc == 0), stop=(kc == 7)),
                                 reads=[B_wbg, B_hT], writes=[PB[3 + br]], sig=(kc == 7))
                        gt, B_g = gsb[br]
                        mt, B_m = msb[br]
                        ch = br * 8 + oc
                        S.op("act", lambda e, br=br, gt=gt, ch=ch: e.activation(out=gt[:], in_=psum[3 + br][:], func=AF.Sigmoid,
                                                                               bias=bbg_t[:, ch:ch + 1], scale=1.0),
                             reads=[PB[3 + br], B_bbg], writes=[B_g])
                        S.op("dve", lambda e, br=br, gt=gt, mt=mt: e.tensor_tensor(out=mt[:], in0=psum[br][:], in1=gt[:], op=ALU.mult),
                             reads=[PB[br], B_g], writes=[B_m])
                    S.op("pool", lambda e: e.tensor_tensor(out=msb[0][0][:], in0=msb[0][0][:], in1=msb[1][0][:], op=ALU.add),
                         reads=[msb[0][1], msb[1][1]], writes=[msb[0][1]])
                    S.op("pool", lambda e, oc=oc: e.tensor_tensor(out=mg[:, oc, :], in0=msb[0][0][:], in1=msb[2][0][:], op=ALU.add),
                         reads=[msb[0][1], msb[2][1]], writes=[B_mg])
                for oc in range(8):
                    osl = slice(oc * 128, (oc + 1) * 128)
                    pb = 6 + oc % 2
                    for kc in range(8):
                        S.op("pe", lambda e, kc=kc, osl=osl, pb=pb: e.matmul(psum[pb][:], lhsT=wout_bf[:, kc, osl], rhs=mg[:, kc, :],
                                                                            start=(kc == 0), stop=(kc == 7)),
                             reads=[B_wbg, B_mg], writes=[PB[pb]], sig=(kc == 7))
                    S.op("dve", lambda e, oc=oc, pb=pb, si=si: e.scalar_tensor_tensor(
                        out=xn[:, oc, :], in0=psum[pb][:], scalar=modT[:, 16 + oc, si:si + 1], in1=xT[:, oc, :], op0=ALU.mult,
                        op1=ALU.add), reads=[PB[pb], B_mod, B_xT], writes=[B_xn])
                if not last_layer:
                    S.dma("pool", lambda e, t0=t0, l=l: e.dma_start(out=XT[l + 1][:, :, t0:t0 + 512].rearrange("k p t -> p k t"),
                                                                   in_=xn[:]), "st_xn", reads=[B_xn])
                    if sg0 == SP:
                        o0 = t0 - SP
                        S.dma("pool", lambda e, o0=o0: e.dma_start(out=AGsrc[:, :, o0:o0 + 512].rearrange("k p t -> p k t"), in_=xn[:]),
                              "st_ag", reads=[B_xn], writes=[B_AGsrc])
                else:
                    for sub in range(4):
                        yt, B_y = ytok[sub % 2]
                        for hf in range(2):
                            pb = 0 + hf
                            for k4 in range(4):
                                kc = hf * 4 + k4
                                S.op("pe", lambda e, pb=pb, k4=k4, kc=kc, sub=sub: e.transpose(
                                    psum[pb][:, k4 * 128:(k4 + 1) * 128], xn[:, kc, sub * 128:(sub + 1) * 128], ident[:]),
                                    reads=[B_xn, B_ident], writes=[PB[pb]], sig=(k4 == 3))
                            if hf == 0:
                                S.op("dve", lambda e, pb=pb, yt=yt: e.tensor_copy(out=yt[:, 0:512], in_=psum[pb][:]), reads=[PB[pb]],
                                     writes=[B_y])
                            else:
                                S.op("act", lambda e, pb=pb, yt=yt: e.activation(out=yt[:, 512:1024], in_=psum[pb][:], func=AF.Copy),
                                     reads=[PB[pb]], writes=[B_y])
                        S.dma("pool", lambda e, t0=t0, sub=sub, yt=yt: e.dma_start(out=y_out[t0 + sub * 128:t0 + (sub + 1) * 128, :],
                                                                                  in_=yt[:]), f"st_y{sub % 2}", reads=[B_y])
        if not last_layer:
            for kc in range(8):
                S.cc(lambda e, kc=kc: e.collective_compute(
                    "AllGather", ALU.bypass, replica_groups=[[0, 1], [2, 3], [4, 5], [6, 7]],
                    ins=[AGsrc[kc]], outs=[AGdst[kc].rearrange("r p t -> (r p) t")]), f"cc_{l}_{kc}", reads=[B_AGsrc], writes=[B_AGdst])

    S.final_wait("sp")
    S.emit()
    return nc, S


def _prep_common(inp, depth):
    f = np.float32

    def A(x):
        return np.ascontiguousarray(np.asarray(x, dtype=f))

    gc = np.zeros((depth, 128, 6), f)
    for j, (k, n) in enumerate((("qn_a", 96), ("kn_a", 96), ("qn_b", 64), ("kn_b", 64), ("qn_c", 64), ("kn_c", 64))):
        gc[:, :n, j] = A(inp[k])
    lam = np.concatenate([A(inp["lam_q1"]), A(inp["lam_k1"]), A(inp["lam_q2"]), A(inp["lam_k2"])], axis=1)[:, None, :]
    return {
        "w_ada": A(inp["w_ada"]).reshape(depth, 8, 128, 3 * D),
        "b_adaT": A(A(inp["b_ada"]).reshape(depth, 24, 128).transpose(0, 2, 1)),
        "norm_gT": A(A(inp["norm_g"]).reshape(depth, 8, 128).transpose(0, 2, 1)),
        "w_in": A(inp["w_in"]).reshape(depth, 8, 128, DIN),
        "gcols": gc,
        "lam": A(lam),
        "subln": A(inp["subln_c"]).reshape(depth, 128, 1),
        "w_oa": A(inp["w_oa"]).reshape(depth, 4, 96, D),
        "w_ob": A(inp["w_ob"]).reshape(depth, 6, 64, D),
        "w_oc": A(inp["w_oc"]).reshape(depth, 4, 128, D),
        "w_bg": A(inp["w_bg"]).reshape(depth, 8, 128, 3 * D),
        "b_bgT": A(A(inp["b_bg"]).reshape(depth, 24, 128).transpose(0, 2, 1)),
        "w_out": A(inp["w_out"]).reshape(depth, 8, 128, D),
        "rotm": _rot_mats(),
        "bandmask": _band_mask(),
        "ident": np.eye(128, dtype=f),
    }


def _core_inputs(common, rope_g, xp_c, xs_j, cp_c, cs_j, hf, SP, H):
    m = dict(common)
    own = slice(hf * H, (hf + 1) * H)
    oth = slice((1 - hf) * H, (2 - hf) * H)
    m["x"] = np.ascontiguousarray(np.concatenate([xp_c, xs_j[own], xs_j[oth]], axis=0))
    m["rope"] = np.ascontiguousarray(np.concatenate([rope_g[..., 0:SP], rope_g[..., own], rope_g[..., oth]], axis=-1))
    cc = np.stack([cp_c, cs_j], axis=0)
    m["cT"] = np.ascontiguousarray(cc.reshape(2, 8, 128).transpose(2, 1, 0))
    fl = np.zeros((128, 2), np.float32)
    fl[:, 0] = float(hf)
    fl[:, 1] = float(1 - hf)
    m["flags"] = fl
    return m


def kernel(**inp):
    xp = np.asarray(inp["x_prompt"], np.float32)
    xs = np.asarray(inp["x_sample"], np.float32)
    cp = np.asarray(inp["c_prompt"], np.float32)
    cs = np.asarray(inp["c_sample"], np.float32)
    depth = int(np.asarray(inp["norm_g"]).shape[0])
    SP, SS = xp.shape[1], xs.shape[1]
    H = SS // 2
    lam_inits = [0.8 - 0.6 * math.exp(-0.3 * l) for l in range(depth)]
    nc, _ = build(SP, H, depth, lam_inits)
    common = _prep_common(inp, depth)
    rope_g = _rope_tables(max(SP, SS))
    in_maps = [_core_inputs(common, rope_g, xp[c], xs[c // 2], cp[c], cs[c // 2], c % 2, SP, H) for c in range(8)]
    res = run_bass_kernel_spmd(nc, in_maps, core_ids=list(range(8)))
    yp = np.stack([res.results[c]["y"][:SP] for c in range(8)], axis=0)
    ys = np.stack([np.concatenate([res.results[2 * j]["y"][SP:], res.results[2 * j + 1]["y"][SP:]], axis=0)
                   for j in range(xs.shape[0])], axis=0)
    return (yp.astype(np.float32), ys.astype(np.float32))
```

```python
import math
import types
import numpy as np
import concourse.bass as bass
import concourse.mybir as mybir
from concourse.bass_utils import run_bass_kernel_spmd

F32 = mybir.dt.float32
BF16 = mybir.dt.bfloat16
AF = mybir.ActivationFunctionType
ALU = mybir.AluOpType

D = 1024
DIN = 6912
EPS = 1e-6
A_GROUPS = ((128, 1), (512, 4), (2048, 16))
OFF = dict(qa=0, ka=1152, va=2304, za=3456, qb=3840, kb=4224, vb=4352, zb=4480, qc=4864, kc=5376, vc=5888, zc=6400)
ENGS = ("pe", "act", "dve", "pool", "sp")


def _freeze(fn):
    if fn.__closure__ is None:
        return fn
    cells = []
    for c in fn.__closure__:
        try:
            cells.append(types.CellType(c.cell_contents))
        except ValueError:
            cells.append(c)
    return types.FunctionType(fn.__code__, fn.__globals__, fn.__name__, fn.__defaults__, tuple(cells))


class Buf:
    __slots__ = ("name", "w", "r")

    def __init__(self, name):
        self.name = name
        self.w = None
        self.r = []


class Sched:
    def __init__(self, nc):
        self.nc = nc
        self.q = {e: [] for e in ENGS}
        self.sems = {}
        self.cnt = {}
        self.seen = {e: {} for e in ENGS}
        for e in ENGS:
            self._sem("E_" + e)
        self.n_ops = 0

    def _sem(self, key):
        if key not in self.sems:
            self.sems[key] = self.nc.alloc_semaphore(key)
            self.cnt[key] = 0
        return self.sems[key]

    def _need(self, eng, ev, force=False):
        if ev is None:
            return
        key, val = ev
        if val <= 0:
            return
        if eng == "pe" and key == "E_pe" and not force:
            return
        if self.seen[eng].get(key, 0) >= val:
            return
        self.seen[eng][key] = val
        self.q[eng].append(("wait", key, val))

    def _deps(self, eng, reads, writes):
        for b in reads:
            self._need(eng, b.w)
        for b in writes:
            self._need(eng, b.w)
            for ev in b.r:
                self._need(eng, ev)

    def _commit(self, ev, reads, writes):
        for b in reads:
            b.r.append(ev)
            if len(b.r) > 48:
                best = {}
                for k, v in b.r:
                    if best.get(k, 0) < v:
                        best[k] = v
                b.r = list(best.items())
        for b in writes:
            b.w = ev
            b.r = []

    def op(self, eng, fn, reads=(), writes=(), sig=True):
        self._deps(eng, reads, writes)
        key = "E_" + eng
        if sig:
            self.cnt[key] += 1
            ev = (key, self.cnt[key])
        else:
            ev = (key, self.cnt[key] + 1)
        self.q[eng].append(("op", _freeze(fn), key if sig else None))
        self._commit(ev, reads, writes)
        self.n_ops += 1

    def dma(self, eng, fn, sem_key, reads=(), writes=()):
        self._deps(eng, reads, writes)
        self._sem(sem_key)
        self.cnt[sem_key] += 16
        ev = (sem_key, self.cnt[sem_key])
        self.q[eng].append(("dma", _freeze(fn), sem_key))
        self._commit(ev, reads, writes)
        self.n_ops += 1

    def cc(self, fn, sem_key, reads=(), writes=()):
        eng = "pool"
        self._deps(eng, reads, writes)
        self._sem(sem_key)
        self.cnt[sem_key] += 1
        ev = (sem_key, self.cnt[sem_key])
        self.q[eng].append(("cc", _freeze(fn), sem_key))
        self._commit(ev, reads, writes)
        self.n_ops += 1

    def barrier(self):
        evs = [(k, v) for k, v in self.cnt.items() if v > 0]
        for e in ENGS:
            for ev in evs:
                if ev[0] != "E_" + e:
                    self._need(e, ev, force=True)

    def final_wait(self, eng="sp"):
        for k, v in self.cnt.items():
            if v > 0 and k != "E_" + eng:
                self._need(eng, (k, v), force=True)

    def emit(self):
        nc = self.nc
        engmap = {"pe": "tensor", "act": "scalar", "dve": "vector", "pool": "gpsimd", "sp": "sync"}
        sems = self.sems
        with nc.Block() as block:
            for e in ENGS:
                items = self.q[e]
                if not items:
                    continue

                def body(eng, items=items):
                    for it in items:
                        if it[0] == "wait":
                            eng.wait_ge(sems[it[1]], it[2])
                        elif it[0] == "op":
                            ins = it[1](eng)
                            if it[2] is not None:
                                ins.then_inc(sems[it[2]], 1)
                        elif it[0] == "cc":
                            it[1](eng).then_inc(sems[it[2]])
                        else:
                            it[1](eng).then_inc(sems[it[2]], 16)

                getattr(block, engmap[e])(body)


def _rope_tables(smax):
    pos = np.arange(smax)
    f32 = np.float32

    def ang(p, dim, theta):
        inv = (f32(theta) ** (-np.arange(0, dim, 2, dtype=f32) / f32(dim))).astype(f32)
        return (p.astype(f32)[:, None] * inv[None, :]).astype(f32)

    tab = np.zeros((3, 2, 128, smax), f32)
    tab[:, 0] = 1.0
    aa = ang(pos, 24, 500000.0)
    tab[0, 0, 0:12] = np.cos(aa).T
    tab[0, 0, 12:24] = np.cos(aa).T
    tab[0, 1, 0:12] = np.sin(aa).T
    tab[0, 1, 12:24] = np.sin(aa).T
    tab[0, :, 96:] = 0.0
    ar = ang(pos // 64, 32, 10000.0)
    ac = ang(pos % 64, 32, 10000.0)
    tab[1, 0, 0:16] = np.cos(ar).T
    tab[1, 0, 16:32] = np.cos(ar).T
    tab[1, 1, 0:16] = np.sin(ar).T
    tab[1, 1, 16:32] = np.sin(ar).T
    tab[1, 0, 32:48] = np.cos(ac).T
    tab[1, 0, 48:64] = np.cos(ac).T
    tab[1, 1, 32:48] = np.sin(ac).T
    tab[1, 1, 48:64] = np.sin(ac).T
    a_c = ang(pos, 16, 500000.0)
    tab[2, 0, 0:8] = np.cos(a_c).T
    tab[2, 0, 8:16] = np.cos(a_c).T
    tab[2, 1, 0:8] = np.sin(a_c).T
    tab[2, 1, 8:16] = np.sin(a_c).T
    tab[1, :, 64:128] = tab[1, :, 0:64]
    tab[2, :, 64:128] = tab[2, :, 0:64]
    return tab


def _rot_mats():
    R = np.zeros((3, 128, 128), np.float32)

    def fill(t, base, n):
        h = n // 2
        for i in range(h):
            R[t, base + i + h, base + i] = -1.0
            R[t, base + i, base + i + h] = 1.0

    fill(0, 0, 24)
    for hb in (0, 64):
        fill(1, hb + 0, 32)
        fill(1, hb + 32, 32)
        fill(2, hb + 0, 16)
    return R


def _band_mask():
    p = np.arange(128)[:, None]
    j = np.arange(256)[None, :]
    return ((j >= p) & (j <= p + 128)).astype(np.float32)


def build(SP, H, depth, lam_inits):
    NT = SP + 2 * H
    NF = SP + H
    nseq = 2
    SMAX = max(SP, 2 * H)
    segs = [(0, SP, 0, True), (SP, H, 1, True), (SP + H, H, 1, False)]
    attn = [(0, SP, 0, SP, False), (SP, H, SP, 2 * H, True)]
    nc = bass.Bass("TRN2", target_bir_lowering=False)

    def din(name, shape, dt=F32):
        return nc.dram_tensor(name, list(shape), dt, kind="ExternalInput").ap()

    x_in = din("x", [NT, D])
    cT_in = din("cT", [128, 8, nseq])
    w_ada = din("w_ada", [depth, 8, 128, 3 * D])
    b_adaT = din("b_adaT", [depth, 128, 24])
    norm_gT = din("norm_gT", [depth, 128, 8])
    w_in = din("w_in", [depth, 8, 128, DIN])
    gcols_in = din("gcols", [depth, 128, 6])
    lam_in = din("lam", [depth, 1, 256])
    subln_in = din("subln", [depth, 128, 1])
    w_oa = din("w_oa", [depth, 4, 96, D])
    w_ob = din("w_ob", [depth, 6, 64, D])
    w_oc = din("w_oc", [depth, 4, 128, D])
    w_bg = din("w_bg", [depth, 8, 128, 3 * D])
    b_bgT = din("b_bgT", [depth, 128, 24])
    w_out = din("w_out", [depth, 8, 128, D])
    rope_in = din("rope", [3, 2, 128, NT])
    flags_in = din("flags", [128, 2])
    rot_in = din("rotm", [3, 128, 128])
    mask_in = din("bandmask", [128, 256])
    ident_in = din("ident", [128, 128])
    y_out = nc.dram_tensor("y", [NF, D], F32, kind="ExternalOutput").ap()

    def dscr(name, shape, dt):
        return nc.dram_tensor(name, list(shape), dt).ap()

    XT = [dscr(f"XT{l}", [8, 128, NT], F32) for l in range(depth)]
    HT = dscr("HT", [8, 128, NT], BF16)
    QK = dscr("QK", [48, 128, NT], BF16)
    ZT = dscr("ZT", [14, 128, NT], BF16)
    VS = dscr("VS", [NT, 1792], BF16)
    YZ = dscr("YZ", [14, 128, NT], BF16)
    AGsrc = dscr("AGsrc", [8, 128, H], F32)
    AGdst = dscr("AGdst", [8, 2, 128, H], F32)
    B_AGsrc, B_AGdst = Buf("AGsrc"), Buf("AGdst")

    S = Sched(nc)
    _bn = [0]

    def sb(shape, dt, name=None):
        _bn[0] += 1
        name = "s_" + (name or f"t{_bn[0]}")
        return nc.alloc_sbuf_tensor(name, list(shape), dt), Buf(name)

    psum = [nc.alloc_psum_tensor(f"ps{i}", [128, 512], F32) for i in range(8)]
    PB = [Buf(f"ps{i}") for i in range(8)]

    ident, B_ident = sb([128, 128], F32, "ident")
    ones_bf, B_ones = sb([128, 128], BF16, "ones")
    zeros_bf, B_zeros = sb([128, 128], BF16, "zeros")
    zeros_w, _ = sb([128, 512], BF16, "zerosw")
    bd64, _ = sb([128, 128], BF16, "bd64")
    ones_lo, _ = sb([128, 96], BF16, "oneslo")
    ones_hi, _ = sb([128, 96], BF16, "oneshi")
    eps_t, B_eps = sb([128, 1], F32, "eps")
    mask_f, B_maskf = sb([128, 256], F32, "maskf")
    mask_bf, B_mask = sb([128, 256], BF16, "maskbf")
    rot_f, B_rotf = sb([128, 3, 128], F32, "rotf")
    cT, B_cT = sb([128, 8, nseq], F32, "cT")
    scT, B_scT = sb([128, 8, nseq], F32, "scT")

    S.dma("sp", lambda e: e.dma_start(out=ident[:], in_=ident_in), "ld_ident", writes=[B_ident])
    S.dma("sp", lambda e: e.dma_start(out=mask_f[:], in_=mask_in), "ld_mask", writes=[B_maskf])
    S.dma("sp", lambda e: e.dma_start(out=rot_f[:], in_=rot_in.rearrange("t p m -> p t m")), "ld_rot", writes=[B_rotf])
    S.dma("sp", lambda e: e.dma_start(out=cT[:], in_=cT_in), "ld_cT", writes=[B_cT])
    S.op("pool", lambda e: e.memset(ones_bf[:], 1.0), writes=[B_ones])
    S.op("pool", lambda e: e.memset(zeros_bf[:], 0.0), writes=[B_zeros])
    S.op("pool", lambda e: e.memset(bd64[:], 0.0), writes=[B_ones])
    S.op("pool", lambda e: e.memset(bd64[0:64, 0:64], 1.0), writes=[B_ones])
    S.op("pool", lambda e: e.memset(bd64[64:128, 64:128], 1.0), writes=[B_ones])
    S.op("pool", lambda e: e.memset(zeros_w[:], 0.0), writes=[B_zeros])
    S.op("pool", lambda e: e.memset(ones_lo[:], 1.0), writes=[B_ones])
    S.op("pool", lambda e: e.memset(ones_lo[0:64, :], 0.0), writes=[B_ones])
    S.op("pool", lambda e: e.memset(ones_hi[:], 0.0), writes=[B_ones])
    S.op("pool", lambda e: e.memset(ones_hi[0:64, :], 1.0), writes=[B_ones])
    S.op("pool", lambda e: e.memset(eps_t[:], EPS), writes=[B_eps])
    S.op("dve", lambda e: e.tensor_copy(out=mask_bf[:], in_=mask_f[:]), reads=[B_maskf], writes=[B_mask])
    S.op("act", lambda e: e.activation(out=scT[:], in_=cT[:], func=AF.Silu), reads=[B_cT], writes=[B_scT])
    flags, B_flags = sb([128, 2], F32, "flags")
    onesL, B_onesL = sb([128, 96], BF16, "onesL")
    onesR, _ = sb([128, 96], BF16, "onesR")
    S.dma("sp", lambda e: e.dma_start(out=flags[:], in_=flags_in), "ld_flags", writes=[B_flags])
    S.op("pool", lambda e: e.memset(onesL[:], 1.0), writes=[B_onesL])
    S.op("pool", lambda e: e.memset(onesR[:], 1.0), writes=[B_onesL])
    S.op("dve", lambda e: e.tensor_scalar(out=onesL[0:64, :], in0=onesL[0:64, :], scalar1=flags[0:64, 0:1], scalar2=None, op0=ALU.mult),
         reads=[B_flags, B_onesL], writes=[B_onesL])
    S.op("dve", lambda e: e.tensor_scalar(out=onesR[64:128, :], in0=onesR[64:128, :], scalar1=flags[64:128, 1:2], scalar2=None,
                                          op0=ALU.mult), reads=[B_flags, B_onesL], writes=[B_onesL])

    modT, B_mod = sb([128, 24, nseq], F32, "modT")
    gsT, B_gs = sb([128, 8, nseq], F32, "gsT")
    b_ada_t, B_bada = sb([128, 24], F32, "bada")
    ng_t, B_ng = sb([128, 8], F32, "ng")
    gcols, B_gcols = sb([128, 6], F32, "gcols")
    bbg_t, B_bbg = sb([128, 24], F32, "bbg")
    subln_t, B_subln = sb([128, 1], F32, "subln")
    sg_t, B_sg = sb([128, 1], F32, "sg")
    lam_t, B_lam = sb([1, 256], F32, "lam")
    lam_w, B_lamw = sb([1, 8], F32, "lamw")
    lam_bf, B_lambf = sb([1, 128], F32, "lambf")
    nlam_t, B_nlam = sb([128, 1], F32, "nlam")
    rotg, B_rotg = sb([128, 6, 128], BF16, "rotg")

    SB_TOP_CONST = nc.sbuf_base

    def phase_reset():
        S.barrier()
        nc.sbuf_base = SB_TOP_CONST

    for l in range(depth):
        lam_init = lam_inits[l]
        phase_reset()
        S.dma("sp", lambda e, l=l: e.dma_start(out=b_ada_t[:], in_=b_adaT[l]), "ld_bada", writes=[B_bada])
        S.dma("sp", lambda e, l=l: e.dma_start(out=ng_t[:], in_=norm_gT[l]), "ld_ng", writes=[B_ng])
        S.dma("sp", lambda e, l=l: e.dma_start(out=gcols[:], in_=gcols_in[l]), "ld_gcols", writes=[B_gcols])
        S.dma("sp", lambda e, l=l: e.dma_start(out=bbg_t[:], in_=b_bgT[l]), "ld_bbg", writes=[B_bbg])
        S.dma("sp", lambda e, l=l: e.dma_start(out=subln_t[:], in_=subln_in[l]), "ld_subln", writes=[B_subln])
        S.dma("sp", lambda e, l=l: e.dma_start(out=lam_t[:], in_=lam_in[l]), "ld_lam", writes=[B_lam])
        S.op("dve", lambda e: e.tensor_scalar(out=sg_t[:], in0=subln_t[:], scalar1=float(1.0 - lam_init), scalar2=None,
                                              op0=ALU.mult), reads=[B_subln], writes=[B_sg])
        S.op("dve", lambda e: e.tensor_tensor(out=lam_t[:, 0:64], in0=lam_t[:, 0:64], in1=lam_t[:, 64:128], op=ALU.mult),
             reads=[B_lam], writes=[B_lam])
        S.op("dve", lambda e: e.tensor_tensor(out=lam_t[:, 128:192], in0=lam_t[:, 128:192], in1=lam_t[:, 192:256], op=ALU.mult),
             reads=[B_lam], writes=[B_lam])
        S.op("dve", lambda e: e.reduce_sum(out=lam_w[:, 0:1], in_=lam_t[:, 0:64], axis=mybir.AxisListType.X),
             reads=[B_lam], writes=[B_lamw])
        S.op("dve", lambda e: e.reduce_sum(out=lam_w[:, 1:2], in_=lam_t[:, 128:192], axis=mybir.AxisListType.X),
             reads=[B_lam], writes=[B_lamw])
        S.op("act", lambda e: e.activation(out=lam_w[:, 2:4], in_=lam_w[:, 0:2], func=AF.Exp), reads=[B_lamw], writes=[B_lamw])
        S.op("dve", lambda e: e.tensor_tensor(out=lam_w[:, 4:5], in0=lam_w[:, 3:4], in1=lam_w[:, 2:3], op=ALU.subtract),
             reads=[B_lamw], writes=[B_lamw])
        S.op("dve", lambda e: e.tensor_scalar(out=lam_w[:, 5:6], in0=lam_w[:, 4:5], scalar1=float(-lam_init), scalar2=None,
                                              op0=ALU.add), reads=[B_lamw], writes=[B_lamw])
        S.op("pool", lambda e: e.memset(lam_bf[:], 1.0), writes=[B_lambf])
        S.op("pe", lambda e: e.matmul(psum[0][:, 0:1], lhsT=lam_bf[0:1, :], rhs=lam_w[0:1, 5:6], start=True, stop=True),
             reads=[B_lambf, B_lamw], writes=[PB[0]])
        S.op("dve", lambda e: e.tensor_copy(out=nlam_t[:], in_=psum[0][:, 0:1]), reads=[PB[0]], writes=[B_nlam])
        for j in range(6):
            S.op("dve", lambda e, j=j: e.tensor_scalar(out=rotg[:, j, :], in0=rot_f[:, j // 2, :], scalar1=gcols[:, j:j + 1],
                                                       scalar2=None, op0=ALU.mult),
                 reads=[B_rotf, B_gcols], writes=[B_rotg])
        wst, B_wst = sb([128, 8, 1536], F32)
        for half in range(2):
            for kc in range(8):
                S.dma("sp", lambda e, l=l, kc=kc, half=half: e.dma_start(
                    out=wst[:, kc, :], in_=w_ada[l, kc, :, half * 1536:(half + 1) * 1536]), "ld_wst", writes=[B_wst])
            for cc in range(12):
                ch = half * 12 + cc
                for kc in range(8):
                    S.op("pe", lambda e, kc=kc, cc=cc, ch=ch: e.matmul(
                        psum[1][:, ch * nseq:(ch + 1) * nseq], lhsT=wst[:, kc, cc * 128:(cc + 1) * 128], rhs=scT[:, kc, :],
                        start=(kc == 0), stop=(kc == 7)), reads=[B_wst, B_scT], writes=[PB[1]], sig=(kc == 7))
        for s in range(nseq):
            S.op("dve", lambda e, s=s: e.tensor_tensor(
                out=modT[:, :, s], in0=psum[1][:, 0:24 * nseq].rearrange("p (c s) -> p c s", s=nseq)[:, :, s], in1=b_ada_t[:],
                op=ALU.add), reads=[PB[1], B_bada], writes=[B_mod])
            S.op("dve", lambda e, s=s: e.scalar_tensor_tensor(
                out=gsT[:, :, s], in0=modT[:, 8:16, s], scalar=1.0, in1=ng_t[:], op0=ALU.add, op1=ALU.mult),
                reads=[B_mod, B_ng], writes=[B_gs])

        phase_reset()
        win_bf, B_win = sb([128, 8, DIN], BF16)
        wstg, B_wstg = sb([128, 1152], F32)
        for kc in range(8):
            for cg in range(6):
                S.dma("sp", lambda e, l=l, kc=kc, cg=cg: e.dma_start(
                    out=wstg[:], in_=w_in[l, kc, :, cg * 1152:(cg + 1) * 1152]), "ld_wstg", writes=[B_wstg])
                eng = ("act", "dve", "pool")[(kc * 6 + cg) % 3]
                if eng == "act":
                    S.op("act", lambda e, kc=kc, cg=cg: e.activation(out=win_bf[:, kc, cg * 1152:(cg + 1) * 1152], in_=wstg[:],
                                                                     func=AF.Copy), reads=[B_wstg], writes=[B_win])
                else:
                    S.op(eng, lambda e, kc=kc, cg=cg: e.tensor_copy(out=win_bf[:, kc, cg * 1152:(cg + 1) * 1152], in_=wstg[:]),
                         reads=[B_wstg], writes=[B_win])
        xT, B_xT = sb([128, 8, 512], F32)
        hT, B_hT = sb([128, 8, 512], BF16)
        if l == 0:
            xtok, B_xtok = sb([128, D], F32)
        tabs, B_tabs = sb([128, 3, 2, 512], F32)
        sq = [sb([128, 512], BF16) for _ in range(2)]
        ubf = [sb([128, 512], BF16) for _ in range(2)]
        rs = [sb([128, 512], F32) for _ in range(2)]
        t1 = [sb([128, 512], F32) for _ in range(3)]
        t2 = [sb([128, 512], F32) for _ in range(2)]
        qo = [sb([128, 512], BF16) for _ in range(2)]
        zo = [sb([128, 512], BF16) for _ in range(2)]
        vo = [sb([128, 1792], BF16) for _ in range(2)]
        tmpn, B_tmpn = sb([128, 512], F32)

        qk_chunks = []
        for h in range(12):
            qk_chunks.append((OFF["qa"] + 96 * h, 96, 0, 0, h, 96))
        for h in range(12):
            qk_chunks.append((OFF["ka"] + 96 * h, 96, 0, 1, 12 + h, 96))
        for h in range(0, 6, 2):
            qk_chunks.append((OFF["qb"] + 64 * h, 128, 1, 2, 24 + h, 64))
        for h in range(0, 2, 2):
            qk_chunks.append((OFF["kb"] + 64 * h, 128, 1, 3, 30 + h, 64))
        for h in range(0, 8, 2):
            qk_chunks.append((OFF["qc"] + 64 * h, 128, 2, 4, 32 + h, 64))
        for h in range(0, 8, 2):
            qk_chunks.append((OFF["kc"] + 64 * h, 128, 2, 5, 40 + h, 64))
        z_chunks = []
        for h in range(4):
            z_chunks.append((OFF["za"] + 96 * h, 96, h))
        for h in range(6):
            z_chunks.append((OFF["zb"] + 64 * h, 64, 4 + h))
        for h in range(4):
            z_chunks.append((OFF["zc"] + 128 * h, 128, 10 + h))
        v_groups = [(OFF["va"], 512, 0), (OFF["va"] + 512, 512, 512), (OFF["va"] + 1024, 128, 1024),
                    (OFF["vb"], 128, 1152), (OFF["vc"], 512, 1280)]

        it = 0
        if l > 0:
            xa, B_xa = sb([128, 8, 512], F32)
        for (sg0, sgn, si, full) in segs:
            for tt in range(sgn // 512):
                t0 = sg0 + tt * 512
                if l > 0 and not full:
                    o0 = t0 - (SP + H)
                    S.dma("sp", lambda e, o0=o0: e.dma_start(out=xa[:], in_=AGdst[:, 0, :, o0:o0 + 512].rearrange("k p t -> p k t")),
                          "ld_xa", reads=[B_AGdst], writes=[B_xa])
                    S.dma("sp", lambda e, o0=o0: e.dma_start(out=xT[:], in_=AGdst[:, 1, :, o0:o0 + 512].rearrange("k p t -> p k t")),
                          "ld_xT", reads=[B_AGdst], writes=[B_xT])
                    S.op("pool", lambda e: e.tensor_scalar(out=xa[:], in0=xa[:], scalar1=flags[:, 0:1], scalar2=None, op0=ALU.mult),
                         reads=[B_xa, B_flags], writes=[B_xa])
                    S.op("dve", lambda e: e.scalar_tensor_tensor(out=xT[:], in0=xT[:], scalar=flags[:, 1:2], in1=xa[:], op0=ALU.mult,
                                                                 op1=ALU.add), reads=[B_xT, B_xa, B_flags], writes=[B_xT])
                elif l == 0:
                    for sub in range(4):
                        S.dma("sp", lambda e, t0=t0, sub=sub: e.dma_start(out=xtok[:], in_=x_in[t0 + sub * 128:t0 + (sub + 1) * 128, :]),
                              "ld_xtok", writes=[B_xtok])
                        for hf in range(2):
                            pb = 6 + hf
                            for k4 in range(4):
                                kc = hf * 4 + k4
                                S.op("pe", lambda e, pb=pb, k4=k4, kc=kc: e.transpose(
                                    psum[pb][:, k4 * 128:(k4 + 1) * 128], xtok[:, kc * 128:(kc + 1) * 128], ident[:]),
                                    reads=[B_xtok, B_ident], writes=[PB[pb]], sig=(k4 == 3))
                            S.op("dve" if hf == 0 else "act",
                                 (lambda e, pb=pb, hf=hf, sub=sub: e.tensor_copy(
                                     out=xT[:, hf * 4:(hf + 1) * 4, sub * 128:(sub + 1) * 128],
                                     in_=psum[pb][:].rearrange("p (k t) -> p k t", k=4))) if hf == 0 else
                                 (lambda e, pb=pb, hf=hf, sub=sub: e.activation(
                                     out=xT[:, hf * 4:(hf + 1) * 4, sub * 128:(sub + 1) * 128],
                                     in_=psum[pb][:].rearrange("p (k t) -> p k t", k=4), func=AF.Copy)),
                                 reads=[PB[pb]], writes=[B_xT])
                    if full:
                        S.dma("pool", lambda e, t0=t0, l=l: e.dma_start(
                            out=XT[l][:, :, t0:t0 + 512].rearrange("k p t -> p k t"), in_=xT[:]), "st_xT", reads=[B_xT])
                else:
                    S.dma("sp", lambda e, t0=t0, l=l: e.dma_start(
                        out=xT[:], in_=XT[l][:, :, t0:t0 + 512].rearrange("k p t -> p k t")), "ld_xT", writes=[B_xT])
                for kc in range(8):
                    sqt, B_sq = sq[kc % 2]
                    S.op("act", lambda e, kc=kc, sqt=sqt: e.activation(out=sqt[:], in_=xT[:, kc, :], func=AF.Square),
                         reads=[B_xT], writes=[B_sq])
                    S.op("pe", lambda e, kc=kc, sqt=sqt: e.matmul(psum[7][:], lhsT=ones_bf[:], rhs=sqt[:], start=(kc == 0),
                                                                  stop=(kc == 7)), reads=[B_ones, B_sq], writes=[PB[7]])
                rst, B_rs = rs[0]
                S.op("act", lambda e, rst=rst: e.activation(out=rst[:], in_=psum[7][:], func=AF.Sqrt, bias=eps_t[:], scale=1.0 / D),
                     reads=[PB[7], B_eps], writes=[B_rs])
                S.op("dve", lambda e, rst=rst: e.reciprocal(out=rst[:], in_=rst[:]), reads=[B_rs], writes=[B_rs])
                for kc in range(8):
                    S.op("dve", lambda e, kc=kc, rst=rst: e.tensor_tensor(out=tmpn[:], in0=xT[:, kc, :], in1=rst[:], op=ALU.mult),
                         reads=[B_xT, B_rs], writes=[B_tmpn])
                    S.op("act", lambda e, kc=kc, si=si: e.activation(out=hT[:, kc, :], in_=tmpn[:], func=AF.Identity,
                                                                     bias=modT[:, kc, si:si + 1], scale=gsT[:, kc, si:si + 1]),
                         reads=[B_tmpn, B_mod, B_gs], writes=[B_hT])
                if full:
                    S.dma("pool", lambda e, t0=t0: e.dma_start(out=HT[:, :, t0:t0 + 512].rearrange("k p t -> p k t"), in_=hT[:]),
                          "st_hT", reads=[B_hT])
                S.dma("sp", lambda e, t0=t0: e.dma_start(
                    out=tabs[:], in_=rope_in[:, :, :, t0:t0 + 512].rearrange("t c p s -> p t c s")), "ld_tabs", writes=[B_tabs])
                def stA(c0, dh, ty, gj, cid, nd, u):
                    pu = u % 3
                    sqt, B_sq = sq[u % len(sq)]
                    ubt, B_ub = ubf[u % len(ubf)]
                    for kc in range(8):
                        S.op("pe", lambda e, kc=kc, c0=c0, dh=dh, pu=pu: e.matmul(
                            psum[pu][0:dh, :], lhsT=win_bf[:, kc, c0:c0 + dh], rhs=hT[:, kc, :], start=(kc == 0), stop=(kc == 7)),
                            reads=[B_win, B_hT], writes=[PB[pu]], sig=(kc == 7))
                    S.op("act", lambda e, dh=dh, pu=pu, sqt=sqt: e.activation(out=sqt[0:dh, :], in_=psum[pu][0:dh, :], func=AF.Square),
                         reads=[PB[pu]], writes=[B_sq])
                    S.op("act", lambda e, dh=dh, pu=pu, ubt=ubt: e.activation(out=ubt[0:dh, :], in_=psum[pu][0:dh, :], func=AF.Copy),
                         reads=[PB[pu]], writes=[B_ub])

                def stB(c0, dh, ty, gj, cid, nd, u, t0=t0):
                    onesm = ones_bf if nd == dh else bd64
                    pu, pss, prt = u % 3, 3 + u % 2, 5 + u % 2
                    sqt, B_sq = sq[u % len(sq)]
                    ubt, B_ub = ubf[u % len(ubf)]
                    rst, B_rs = rs[u % len(rs)]
                    t1t, B_t1 = t1[u % len(t1)]
                    t2t, B_t2 = t2[u % len(t2)]
                    qot, B_qo = qo[u % len(qo)]
                    S.op("pe", lambda e, dh=dh, pss=pss, sqt=sqt, onesm=onesm: e.matmul(psum[pss][0:dh, :], lhsT=onesm[0:dh, 0:dh],
                                                                                       rhs=sqt[0:dh, :], start=True, stop=True),
                         reads=[B_ones, B_sq], writes=[PB[pss]])
                    S.op("pe", lambda e, dh=dh, prt=prt, ubt=ubt, gj=gj: e.matmul(psum[prt][0:dh, :], lhsT=rotg[0:dh, gj, 0:dh],
                                                                                 rhs=ubt[0:dh, :], start=True, stop=True),
                         reads=[B_rotg, B_ub], writes=[PB[prt]])
                    S.op("act", lambda e, dh=dh, pss=pss, rst=rst: e.activation(out=rst[0:dh, :], in_=psum[pss][0:dh, :], func=AF.Sqrt,
                                                                               bias=eps_t[0:dh, :], scale=1.0 / nd),
                         reads=[PB[pss], B_eps], writes=[B_rs])
                    S.op("dve", lambda e, dh=dh, rst=rst: e.reciprocal(out=rst[0:dh, :], in_=rst[0:dh, :]), reads=[B_rs], writes=[B_rs])
                    S.op("dve", lambda e, dh=dh, pu=pu, t1t=t1t, gj=gj, ty=ty: e.scalar_tensor_tensor(
                        out=t1t[0:dh, :], in0=psum[pu][0:dh, :], scalar=gcols[0:dh, gj:gj + 1], in1=tabs[0:dh, ty, 0, :],
                        op0=ALU.mult, op1=ALU.mult), reads=[PB[pu], B_gcols, B_tabs], writes=[B_t1])
                    S.op("dve", lambda e, dh=dh, prt=prt, t2t=t2t, ty=ty: e.tensor_tensor(
                        out=t2t[0:dh, :], in0=psum[prt][0:dh, :], in1=tabs[0:dh, ty, 1, :], op=ALU.mult),
                        reads=[PB[prt], B_tabs], writes=[B_t2])
                    S.op("pool", lambda e, dh=dh, t1t=t1t, t2t=t2t: e.tensor_tensor(out=t1t[0:dh, :], in0=t1t[0:dh, :], in1=t2t[0:dh, :],
                                                                                   op=ALU.add), reads=[B_t1, B_t2], writes=[B_t1])
                    S.op("pool", lambda e, dh=dh, t1t=t1t, rst=rst, qot=qot: e.tensor_tensor(out=qot[0:dh, :], in0=t1t[0:dh, :],
                                                                                            in1=rst[0:dh, :], op=ALU.mult),
                         reads=[B_t1, B_rs], writes=[B_qo])
                    if nd == dh:
                        S.dma("pool", lambda e, dh=dh, cid=cid, t0=t0, qot=qot: e.dma_start(out=QK[cid, 0:dh, t0:t0 + 512], in_=qot[0:dh, :]),
                              f"st_qo{u % len(qo)}", reads=[B_qo])
                    else:
                        for hb in range(2):
                            S.dma("pool", lambda e, cid=cid, t0=t0, qot=qot, hb=hb: e.dma_start(
                                out=QK[cid + hb, 0:64, t0:t0 + 512], in_=qot[hb * 64:(hb + 1) * 64, :]), f"st_qo{u % len(qo)}h{hb}",
                                reads=[B_qo])

                chs = qk_chunks if full else [c for c in qk_chunks if c[3] % 2 == 1]
                nch = len(chs)
                for i in range(nch + 1):
                    if i < nch:
                        stA(*chs[i], it + i)
                    if i >= 1:
                        stB(*chs[i - 1], it + i - 1)
                it += nch
                for (c0, dz, cid) in (z_chunks if full else []):
                    u = it % 2
                    pu = it % 3
                    it += 1
                    zot, B_zo = zo[u]
                    for kc in range(8):
                        S.op("pe", lambda e, kc=kc, c0=c0, dz=dz, pu=pu: e.matmul(
                            psum[pu][0:dz, :], lhsT=win_bf[:, kc, c0:c0 + dz], rhs=hT[:, kc, :], start=(kc == 0), stop=(kc == 7)),
                            reads=[B_win, B_hT], writes=[PB[pu]], sig=(kc == 7))
                    S.op("act", lambda e, dz=dz, pu=pu, zot=zot: e.activation(out=zot[0:dz, :], in_=psum[pu][0:dz, :], func=AF.Silu),
                         reads=[PB[pu]], writes=[B_zo])
                    S.dma("pool", lambda e, dz=dz, cid=cid, t0=t0, zot=zot: e.dma_start(out=ZT[cid, 0:dz, t0:t0 + 512], in_=zot[0:dz, :]),
                          f"st_zo{u}", reads=[B_zo])
                for sub in range(4):
                    vot, B_vo = vo[sub % 2]
                    for gi, (c0, n, o0) in enumerate(v_groups):
                        pu = it % 3
                        it += 1
                        for kc in range(8):
                            S.op("pe", lambda e, kc=kc, c0=c0, n=n, pu=pu, sub=sub: e.matmul(
                                psum[pu][:, 0:n], lhsT=hT[:, kc, sub * 128:(sub + 1) * 128], rhs=win_bf[:, kc, c0:c0 + n],
                                start=(kc == 0), stop=(kc == 7)), reads=[B_win, B_hT], writes=[PB[pu]], sig=(kc == 7))
                        if gi % 2 == 0:
                            S.op("dve", lambda e, n=n, pu=pu, o0=o0, vot=vot: e.tensor_copy(out=vot[:, o0:o0 + n], in_=psum[pu][:, 0:n]),
                                 reads=[PB[pu]], writes=[B_vo])
                        else:
                            S.op("act", lambda e, n=n, pu=pu, o0=o0, vot=vot: e.activation(out=vot[:, o0:o0 + n], in_=psum[pu][:, 0:n],
                                                                                          func=AF.Copy), reads=[PB[pu]], writes=[B_vo])
                    S.dma("pool", lambda e, t0=t0, sub=sub, vot=vot: e.dma_start(out=VS[t0 + sub * 128:t0 + (sub + 1) * 128, :], in_=vot[:]),
                          f"st_vo{sub % 2}", reads=[B_vo])

        phase_reset()
        KW = max(SMAX, H + 2048)
        QW = max(SP, H)
        KTs = [[sb([128, KW], BF16) for _ in range(2)] for _ in range(2)]
        QTs = [[sb([128, QW], BF16) for _ in range(2)] for _ in range(2)]
        VTs = [sb([128, SMAX // 128 + 1, 128], BF16) for _ in range(2)]
        VTA = [VTs[0], sb([128, H // 128 + 1, 96], BF16)]
        vta_i = [0]
        PT = [sb([128, 512], BF16) for _ in range(6)]
        zt, B_zt = sb([128, 512], BF16)
        e1, B_e1 = sb([128, 512], F32)
        e2, B_e2 = sb([128, 512], F32)
        e3, B_e3 = sb([128, 512], F32)
        e4, B_e4 = sb([128, 512], F32)
        esq, B_esq = sb([128, 512], BF16)
        yzo, B_yzo = sb([128, 512], BF16)
        AO, B_AO = sb([96, QW], F32)
        AL, B_AL = sb([96, QW], F32)
        sc_rot = [0]
        sc_nb = [4]
        sel64, B_sel = sb([128, 64], F32)
        S.op("pool", lambda e: e.memset(sel64[:], 0.0), writes=[B_sel])
        S.op("pool", lambda e: e.memset(sel64[64:65, :], 1.0), writes=[B_sel])

        def run_pipeline(blocks, G=2, LA=1, nb=4):
            sc_nb[0] = nb
            assert G * (LA + 1) <= nb
            groups = [blocks[i:i + G] for i in range(0, len(blocks), G)]
            n = len(groups)
            for i in range(n + LA):
                if i < n:
                    for b in groups[i]:
                        b[0]()
                if i >= LA:
                    for b in groups[i - LA]:
                        b[1]()

        def c_epilogue(h, t0):
            S.dma("sp", lambda e, h=h, t0=t0: e.dma_start(out=zt[:], in_=ZT[10 + h, :, t0:t0 + 512]), "ld_zt", writes=[B_zt])
            S.op("act", lambda e: e.activation(out=e1[:], in_=psum[5][:], func=AF.Ln), reads=[PB[5]], writes=[B_e1])
            S.op("act", lambda e: e.activation(out=e2[:], in_=psum[7][:], func=AF.Ln), reads=[PB[7]], writes=[B_e2])
            S.op("act", lambda e: e.activation(out=e1[:], in_=e1[:], func=AF.Exp, scale=-1.0), reads=[B_e1], writes=[B_e1])
            S.op("act", lambda e: e.activation(out=e2[:], in_=e2[:], func=AF.Exp, scale=-1.0), reads=[B_e2], writes=[B_e2])
            S.op("dve", lambda e: e.tensor_tensor(out=e1[:], in0=psum[4][:], in1=e1[:], op=ALU.mult), reads=[PB[4], B_e1],
                 writes=[B_e1])
            S.op("dve", lambda e: e.tensor_tensor(out=e2[:], in0=psum[6][:], in1=e2[:], op=ALU.mult), reads=[PB[6], B_e2],
                 writes=[B_e2])
            S.op("dve", lambda e: e.scalar_tensor_tensor(out=e3[:], in0=e2[:], scalar=nlam_t[:, 0:1], in1=e1[:], op0=ALU.mult,
                                                         op1=ALU.add), reads=[B_e1, B_e2, B_nlam], writes=[B_e3])
            S.op("act", lambda e: e.activation(out=esq[:], in_=e3[:], func=AF.Square), reads=[B_e3], writes=[B_esq])
            r = 5
            S.op("pe", lambda e, r=r: e.matmul(psum[r][:], lhsT=ones_bf[:], rhs=esq[:], start=True, stop=True), reads=[B_ones, B_esq],
                 writes=[PB[r]])
            S.op("act", lambda e, r=r: e.activation(out=e4[:], in_=psum[r][:], func=AF.Ln, bias=eps_t[:], scale=1.0 / 128),
                 reads=[PB[r], B_eps], writes=[B_e4])
            S.op("act", lambda e: e.activation(out=e4[:], in_=e4[:], func=AF.Exp, scale=-0.5), reads=[B_e4], writes=[B_e4])
            S.op("dve", lambda e: e.scalar_tensor_tensor(out=e3[:], in0=e3[:], scalar=sg_t[:, 0:1], in1=e4[:], op0=ALU.mult,
                                                         op1=ALU.mult), reads=[B_e3, B_sg, B_e4], writes=[B_e3])
            S.op("pool", lambda e: e.tensor_tensor(out=yzo[:], in0=e3[:], in1=zt[:], op=ALU.mult), reads=[B_e3, B_zt],
                 writes=[B_yzo])
            S.dma("pool", lambda e, h=h, t0=t0: e.dma_start(out=YZ[10 + h, :, t0:t0 + 512], in_=yzo[:]), "st_yzo", reads=[B_yzo])


        def score_exp(ktile, B_k, kcol, dh, qtile, B_q, q0, nq, scale, kstep=1, qstep=1):
            r = sc_rot[0] % sc_nb[0]
            sc_rot[0] += 1
            ptt, B_pt = PT[r]
            ksl = ktile[0:dh, kcol:kcol + 127 * kstep + 1:kstep] if kstep > 1 else ktile[0:dh, kcol:kcol + 128]
            qsl = qtile[0:dh, q0:q0 + (nq - 1) * qstep + 1:qstep] if qstep > 1 else qtile[0:dh, q0:q0 + nq]
            S.op("pe", lambda e, r=r, ksl=ksl, qsl=qsl, nq=nq: e.matmul(psum[r][:, 0:nq], lhsT=ksl, rhs=qsl, start=True, stop=True),
                 reads=[B_k, B_q], writes=[PB[r]])
            S.op("act", lambda e, r=r, nq=nq, ptt=ptt, scale=scale: e.activation(out=ptt[:, 0:nq], in_=psum[r][:, 0:nq], func=AF.Exp,
                                                                                 scale=scale), reads=[PB[r]], writes=[B_pt])
            return ptt, B_pt

        for (aq0, anq, ak0, ank, halo) in attn:
            Sq = ank
            s0 = ak0
            nkb = ank // 128
            nqt = anq // 512
            oth0 = aq0 + anq
            def b_load_kv(kvh):
                bs = kvh % 2
                kt, B_k = KTs[bs][0]
                vt_, B_vt = VTs[bs]
                S.dma("sp", lambda e, kvh=kvh, s0=s0, Sq=Sq, kt=kt: e.dma_start(out=kt[0:64, 0:Sq], in_=QK[30 + kvh, 0:64, s0:s0 + Sq]),
                      f"ld_K{bs}0", writes=[B_k])
                S.dma("sp", lambda e, kvh=kvh, s0=s0, Sq=Sq, nkb=nkb, vt_=vt_: e.dma_start(
                    out=vt_[:, 0:nkb, 0:64],
                    in_=VS[s0:s0 + Sq, 1152 + kvh * 64:1152 + (kvh + 1) * 64].rearrange("(b p) c -> p b c", p=128)),
                    f"ld_V{bs}", writes=[B_vt])
                S.op("pool", lambda e, nkb=nkb, vt_=vt_: e.memset(vt_[:, 0:nkb, 64:65], 1.0), writes=[B_vt])

            def b_load_q(kvh, g):
                bs = kvh % 2
                qh = kvh * 3 + g
                qt_, B_q = QTs[bs][g % 2]
                S.dma("sp", lambda e, qh=qh, aq0=aq0, anq=anq, qt_=qt_: e.dma_start(out=qt_[0:64, 0:anq],
                                                                                    in_=QK[24 + qh, 0:64, aq0:aq0 + anq]),
                      f"ld_Q{bs}{g % 2}", writes=[B_q])

            b_units = [(kvh, g) for kvh in range(2) for g in range(3)]
            b_load_kv(0)
            b_load_q(0, 0)
            for kvh in range(2):
                bs = kvh % 2
                kt, B_k = KTs[bs][0]
                vt_, B_vt = VTs[bs]
                for g in range(3):
                    qh = kvh * 3 + g
                    qt_, B_q = QTs[bs][g % 2]
                    ui = b_units.index((kvh, g))
                    if ui + 1 < len(b_units):
                        nk, ng = b_units[ui + 1]
                        if nk != kvh:
                            b_load_kv(nk)
                        b_load_q(nk, ng)
                    blocks = []
                    for qt in range(nqt):
                        for kb in range(nkb):
                            st = {}

                            def s1(st=st, kb=kb, qt=qt, kt=kt, B_k=B_k, qt_=qt_, B_q=B_q):
                                st["pt"] = score_exp(kt, B_k, kb * 128, 64, qt_, B_q, qt * 512, 512, 0.125)

                            def s2(st=st, kb=kb, qt=qt, qh=qh, nkb=nkb, s0=aq0, vt_=vt_, B_vt=B_vt):
                                ptt, B_pt = st["pt"]
                                S.op("pe", lambda e, kb=kb, ptt=ptt, nkb=nkb, vt_=vt_: e.matmul(psum[6][0:65, :], lhsT=vt_[:, kb, 0:65],
                                                                                               rhs=ptt[:], start=(kb == 0),
                                                                                               stop=(kb == nkb - 1)),
                                     reads=[B_vt, B_pt], writes=[PB[6]])
                                if kb == nkb - 1:
                                    t0 = s0 + qt * 512
                                    S.dma("sp", lambda e, qh=qh, t0=t0: e.dma_start(out=zt[0:64, :], in_=ZT[4 + qh, 0:64, t0:t0 + 512]),
                                          "ld_zt", writes=[B_zt])
                                    S.op("act", lambda e: e.activation(out=e3[0:65, :], in_=psum[6][0:65, :], func=AF.Copy),
                                         reads=[PB[6]], writes=[B_e3])
                                    S.op("pe", lambda e: e.matmul(psum[7][0:64, :], lhsT=sel64[0:65, :], rhs=e3[0:65, :], start=True,
                                                                  stop=True), reads=[B_sel, B_e3], writes=[PB[7]])
                                    S.op("act", lambda e: e.activation(out=e1[0:64, :], in_=psum[7][0:64, :], func=AF.Ln), reads=[PB[7]],
                                         writes=[B_e1])
                                    S.op("act", lambda e: e.activation(out=e1[0:64, :], in_=e1[0:64, :], func=AF.Exp, scale=-1.0),
                                         reads=[B_e1], writes=[B_e1])
                                    S.op("dve", lambda e: e.tensor_tensor(out=e2[0:64, :], in0=e3[0:64, :], in1=e1[0:64, :], op=ALU.mult),
                                         reads=[B_e3, B_e1], writes=[B_e2])
                                    S.op("pool", lambda e: e.tensor_tensor(out=yzo[0:64, :], in0=e2[0:64, :], in1=zt[0:64, :], op=ALU.mult),
                                         reads=[B_e2, B_zt], writes=[B_yzo])
                                    S.dma("pool", lambda e, qh=qh, t0=t0: e.dma_start(out=YZ[4 + qh, 0:64, t0:t0 + 512], in_=yzo[0:64, :]),
                                          "st_yzo", reads=[B_yzo])

                            blocks.append((s1, s2))
                    run_pipeline(blocks, G=3, LA=1, nb=6)
            def c_load(h):
                bs = h % 2
                vt_, B_vt = VTs[bs]
                for j in range(2):
                    kt, B_k = KTs[bs][j]
                    qt_, B_q = QTs[bs][j]
                    S.dma("sp", lambda e, h=h, j=j, s0=s0, Sq=Sq, kt=kt: e.dma_start(
                        out=kt[0:64, 0:Sq], in_=QK[40 + 2 * h + j, 0:64, s0:s0 + Sq]), f"ld_K{bs}{j}", writes=[B_k])
                    S.dma("sp", lambda e, h=h, j=j, aq0=aq0, anq=anq, qt_=qt_: e.dma_start(
                        out=qt_[0:64, 0:anq], in_=QK[32 + 2 * h + j, 0:64, aq0:aq0 + anq]), f"ld_Q{bs}{j}", writes=[B_q])
                S.dma("sp", lambda e, h=h, s0=s0, Sq=Sq, nkb=nkb, vt_=vt_: e.dma_start(
                    out=vt_[:, 0:nkb, :],
                    in_=VS[s0:s0 + Sq, 1280 + h * 128:1280 + (h + 1) * 128].rearrange("(b p) c -> p b c", p=128)),
                    f"ld_V{bs}", writes=[B_vt])

            c_load(0)
            for h in range(4):
                bs = h % 2
                vt_, B_vt = VTs[bs]
                if h + 1 < 4:
                    c_load(h + 1)
                blocks = []
                for qt in range(nqt):
                    for kb in range(nkb):
                        for j in range(2):
                            st = {}

                            def s1(st=st, kb=kb, qt=qt, j=j, bs=bs):
                                st["pt"] = score_exp(KTs[bs][j][0], KTs[bs][j][1], kb * 128, 64, QTs[bs][j][0], QTs[bs][j][1], qt * 512, 512,
                                                     0.125)

                            def s2(st=st, kb=kb, qt=qt, j=j, h=h, nkb=nkb, s0=aq0, vt_=vt_, B_vt=B_vt):
                                ptt, B_pt = st["pt"]
                                S.op("pe", lambda e, kb=kb, ptt=ptt, nkb=nkb, j=j, vt_=vt_: e.matmul(psum[4 + 2 * j][:, :], lhsT=vt_[:, kb, :],
                                                                                                    rhs=ptt[:], start=(kb == 0),
                                                                                                    stop=(kb == nkb - 1)),
                                     reads=[B_vt, B_pt], writes=[PB[4 + 2 * j]], sig=False)
                                S.op("pe", lambda e, kb=kb, ptt=ptt, nkb=nkb, j=j: e.matmul(psum[5 + 2 * j][:, :], lhsT=ones_bf[:, :],
                                                                                           rhs=ptt[:], start=(kb == 0), stop=(kb == nkb - 1)),
                                     reads=[B_ones, B_pt], writes=[PB[5 + 2 * j]])
                                if kb == nkb - 1 and j == 1:
                                    c_epilogue(h, s0 + qt * 512)

                            blocks.append((s1, s2))
                run_pipeline(blocks, G=2, LA=1, nb=4)
            scale_a = 96 ** -0.5
            for h in range(4):
                for g, (win, dil) in enumerate(A_GROUPS):
                    hh = g * 4 + h
                    Sq = anq
                    s0 = aq0
                    L = Sq // dil
                    T = min(512, L)
                    pad = 64 * dil
                    kt, B_k = KTs[0][0]
                    qt_, B_q = QTs[0][0]
                    if halo:
                        S.dma("sp", lambda e, hh=hh, kt=kt, pad=pad, oth0=oth0, anq=anq: e.dma_start(
                            out=kt[0:96, 0:pad], in_=QK[12 + hh, 0:96, oth0 + anq - pad:oth0 + anq]), "ld_K00", writes=[B_k])
                        S.dma("sp", lambda e, hh=hh, kt=kt, pad=pad, oth0=oth0, Sq=Sq: e.dma_start(
                            out=kt[0:96, pad + Sq:pad + Sq + pad], in_=QK[12 + hh, 0:96, oth0:oth0 + pad]), "ld_K00", writes=[B_k])
                    else:
                        S.op("pool", lambda e, kt=kt, pad=pad: e.memset(kt[0:96, 0:pad], 0.0), writes=[B_k])
                        S.op("pool", lambda e, kt=kt, pad=pad, Sq=Sq: e.memset(kt[0:96, pad + Sq:pad + Sq + pad], 0.0), writes=[B_k])
                    S.dma("sp", lambda e, hh=hh, s0=s0, Sq=Sq, kt=kt, pad=pad: e.dma_start(
                        out=kt[0:96, pad:pad + Sq], in_=QK[12 + hh, 0:96, s0:s0 + Sq]), "ld_K00", writes=[B_k])
                    S.dma("sp", lambda e, hh=hh, s0=s0, Sq=Sq, qt_=qt_: e.dma_start(out=qt_[0:96, 0:Sq], in_=QK[hh, 0:96, s0:s0 + Sq]),
                          "ld_Q00", writes=[B_q])
                    nblk = L // 128 + 1
                    for r in range(dil):
                        bi = vta_i[0] % 2
                        VTa, B_VTa = VTA[bi]
                        vta_i[0] += 1
                        if halo:
                            lrow = oth0 + anq - 64 * dil + r
                            rrow = oth0 + r
                            vl = VS[lrow:lrow + 63 * dil + 1:dil, hh * 96:(hh + 1) * 96] if dil > 1 else VS[lrow:lrow + 64, hh * 96:(hh + 1) * 96]
                            vr = VS[rrow:rrow + 63 * dil + 1:dil, hh * 96:(hh + 1) * 96] if dil > 1 else VS[rrow:rrow + 64, hh * 96:(hh + 1) * 96]
                            S.dma("sp", lambda e, vl=vl: e.dma_start(out=VTa[0:64, 0, 0:96], in_=vl), f"ld_Va{bi}", writes=[B_VTa])
                            S.dma("sp", lambda e, vr=vr, nblk=nblk: e.dma_start(out=VTa[64:128, nblk - 1, 0:96], in_=vr), f"ld_Va{bi}", writes=[B_VTa])
                            S.op("dve", lambda e: e.tensor_scalar(out=VTa[0:64, 0, 0:96], in0=VTa[0:64, 0, 0:96], scalar1=flags[0:64, 0:1],
                                                                  scalar2=None, op0=ALU.mult), reads=[B_flags, B_VTa], writes=[B_VTa])
                            S.op("dve", lambda e, nblk=nblk: e.tensor_scalar(out=VTa[64:128, nblk - 1, 0:96], in0=VTa[64:128, nblk - 1, 0:96],
                                                                             scalar1=flags[64:128, 1:2], scalar2=None, op0=ALU.mult),
                                 reads=[B_flags, B_VTa], writes=[B_VTa])
                        else:
                            S.op("pool", lambda e: e.memset(VTa[0:64, 0:1, 0:96], 0.0), writes=[B_VTa])
                            S.op("pool", lambda e, nblk=nblk: e.memset(VTa[64:128, nblk - 1:nblk, 0:96], 0.0), writes=[B_VTa])
                        vsrc = VS[s0:s0 + Sq, hh * 96:(hh + 1) * 96].rearrange("(i d) c -> d i c", d=dil)[r]
                        vsrc = vsrc.rearrange("(b p) c -> p b c", p=128)
                        S.dma("sp", lambda e, vsrc=vsrc, nblk=nblk: e.dma_start(out=VTa[64:128, 0:nblk - 1, 0:96], in_=vsrc[0:64]),
                              f"ld_Va{bi}", writes=[B_VTa])
                        S.dma("sp", lambda e, vsrc=vsrc, nblk=nblk: e.dma_start(out=VTa[0:64, 1:nblk, 0:96], in_=vsrc[64:128]),
                              f"ld_Vb{bi}", writes=[B_VTa])
                        blocks = []
                        for qt in range(L // T):
                            i0 = qt * T
                            nb = T // 128 + 1
                            for b in range(nb):
                                st = {}
                                jb = i0 // 128 + b
                                w0 = max(i0, i0 - 128 + 128 * b)
                                w1 = min(i0 + T, i0 + 128 + 128 * b)
                                nq = w1 - w0
                                m0 = w0 - (i0 - 128 + 128 * b)
                                kcol = r + dil * 128 * jb
                                q0 = r + dil * w0

                                def s1(st=st, kcol=kcol, q0=q0, nq=nq, m0=m0, kt=kt, B_k=B_k, qt_=qt_, B_q=B_q, dil=dil):
                                    ptt, B_pt = score_exp(kt, B_k, kcol, 96, qt_, B_q, q0, nq, scale_a, kstep=dil, qstep=dil)
                                    S.op("dve", lambda e, ptt=ptt, nq=nq, m0=m0: e.tensor_tensor(
                                        out=ptt[:, 0:nq], in0=ptt[:, 0:nq], in1=mask_bf[:, m0:m0 + nq], op=ALU.mult),
                                        reads=[B_pt, B_mask], writes=[B_pt])
                                    st["pt"] = (ptt, B_pt)

                                def s2(st=st, b=b, nb=nb, jb=jb, nq=nq, w0=w0, i0=i0, T=T, nblk=nblk, g=g, r=r, dil=dil, halo=halo, VTa=VTa, B_VTa=B_VTa):
                                    ptt, B_pt = st["pt"]
                                    if b == 0:
                                        S.op("pe", lambda e, T=T: e.matmul(psum[4][0:96, 0:T], lhsT=zeros_bf[:, 0:96], rhs=zeros_w[:, 0:T],
                                                                           start=True, stop=False), reads=[B_zeros], writes=[PB[4]],
                                             sig=False)
                                        S.op("pe", lambda e, T=T: e.matmul(psum[5][0:96, 0:T], lhsT=zeros_bf[:, 0:96], rhs=zeros_w[:, 0:T],
                                                                           start=True, stop=False), reads=[B_zeros], writes=[PB[5]],
                                             sig=False)
                                    c0 = w0 - i0
                                    last = (b == nb - 1)
                                    S.op("pe", lambda e, jb=jb, ptt=ptt, nq=nq, c0=c0, last=last: e.matmul(
                                        psum[4][0:96, c0:c0 + nq], lhsT=VTa[:, jb, 0:96], rhs=ptt[:, 0:nq], start=False, stop=last),
                                        reads=[B_VTa, B_pt], writes=[PB[4]], sig=False)
                                    if halo:
                                        onesm = onesL if jb == 0 else (onesR if jb == nblk - 1 else ones_bf)
                                    else:
                                        onesm = ones_lo if jb == 0 else (ones_hi if jb == nblk - 1 else ones_bf)
                                    S.op("pe", lambda e, ptt=ptt, nq=nq, c0=c0, last=last, onesm=onesm: e.matmul(
                                        psum[5][0:96, c0:c0 + nq], lhsT=onesm[:, 0:96], rhs=ptt[:, 0:nq], start=False, stop=last),
                                        reads=[B_ones, B_onesL, B_pt], writes=[PB[5]])
                                    if not last:
                                        return
                                    a0 = r + dil * i0
                                    if dil > 1:
                                        ao_sl = AO[:, a0:a0 + dil * (T - 1) + 1:dil]
                                        al_sl = AL[:, a0:a0 + dil * (T - 1) + 1:dil]
                                    else:
                                        ao_sl = AO[:, a0:a0 + T]
                                        al_sl = AL[:, a0:a0 + T]
                                    if g == 0:
                                        S.op("dve", lambda e, ao_sl=ao_sl, T=T: e.tensor_copy(out=ao_sl, in_=psum[4][0:96, 0:T]),
                                             reads=[PB[4]], writes=[B_AO])
                                        S.op("act", lambda e, al_sl=al_sl, T=T: e.activation(out=al_sl, in_=psum[5][0:96, 0:T],
                                                                                            func=AF.Copy), reads=[PB[5]], writes=[B_AL])
                                    else:
                                        S.op("dve", lambda e, ao_sl=ao_sl, T=T: e.tensor_tensor(out=ao_sl, in0=psum[4][0:96, 0:T],
                                                                                               in1=ao_sl, op=ALU.add),
                                             reads=[PB[4], B_AO], writes=[B_AO])
                                        S.op("dve", lambda e, al_sl=al_sl, T=T: e.tensor_tensor(out=al_sl, in0=psum[5][0:96, 0:T],
                                                                                               in1=al_sl, op=ALU.add),
                                             reads=[PB[5], B_AL], writes=[B_AL])

                                blocks.append((s1, s2))
                        run_pipeline(blocks, G=2, LA=1, nb=4)
                for qt in range(nqt):
                    t0 = aq0 + qt * 512
                    c0 = qt * 512
                    S.dma("sp", lambda e, h=h, t0=t0: e.dma_start(out=zt[0:96, :], in_=ZT[h, 0:96, t0:t0 + 512]), "ld_zt", writes=[B_zt])
                    S.op("act", lambda e, c0=c0: e.activation(out=e1[0:96, :], in_=AL[:, c0:c0 + 512], func=AF.Ln), reads=[B_AL],
                         writes=[B_e1])
                    S.op("act", lambda e: e.activation(out=e1[0:96, :], in_=e1[0:96, :], func=AF.Exp, scale=-1.0), reads=[B_e1],
                         writes=[B_e1])
                    S.op("dve", lambda e, c0=c0: e.tensor_tensor(out=e2[0:96, :], in0=AO[:, c0:c0 + 512], in1=e1[0:96, :], op=ALU.mult),
                         reads=[B_AO, B_e1], writes=[B_e2])
                    S.op("pool", lambda e: e.tensor_tensor(out=yzo[0:96, :], in0=e2[0:96, :], in1=zt[0:96, :], op=ALU.mult),
                         reads=[B_e2, B_zt], writes=[B_yzo])
                    S.dma("pool", lambda e, h=h, t0=t0: e.dma_start(out=YZ[h, 0:96, t0:t0 + 512], in_=yzo[0:96, :]), "st_yzo",
                          reads=[B_yzo])

        phase_reset()
        wbg_bf, B_wbg = sb([128, 8, 3 * D], BF16)
        woa_bf, B_woa = sb([96, 4, D], BF16)
        wob_bf, B_wob = sb([64, 6, D], BF16)
        woc_bf, B_woc = sb([128, 4, D], BF16)
        wout_bf, B_wout = sb([128, 8, D], BF16)
        wstg, B_wstg = sb([128, 1024], F32)
        cvt = [0]

        def load_cast(dst_ap, src_ap, np_):
            S.dma("sp", lambda e: e.dma_start(out=wstg[0:np_, :], in_=src_ap), "ld_wstg", writes=[B_wstg])
            eng = ("act", "dve", "pool")[cvt[0] % 3]
            cvt[0] += 1
            if eng == "act":
                S.op("act", lambda e: e.activation(out=dst_ap, in_=wstg[0:np_, :], func=AF.Copy), reads=[B_wstg], writes=[B_wbg])
            else:
                S.op(eng, lambda e: e.tensor_copy(out=dst_ap, in_=wstg[0:np_, :]), reads=[B_wstg], writes=[B_wbg])

        for kc in range(8):
            for cg in range(3):
                load_cast(wbg_bf[:, kc, cg * 1024:(cg + 1) * 1024], w_bg[l, kc, :, cg * 1024:(cg + 1) * 1024], 128)
            load_cast(wout_bf[:, kc, :], w_out[l, kc], 128)
        for hh in range(4):
            load_cast(woa_bf[:, hh, :], w_oa[l, hh], 96)
            load_cast(woc_bf[:, hh, :], w_oc[l, hh], 128)
        for hh in range(6):
            load_cast(wob_bf[:, hh, :], w_ob[l, hh], 64)
        B_woa = B_wob = B_woc = B_wout = B_wbg

        hT, B_hT = sb([128, 8, 512], BF16)
        xT, B_xT = sb([128, 8, 512], F32)
        yz, B_yz = sb([128, 14, 512], BF16)
        gsb = [sb([128, 512], F32) for _ in range(3)]
        msb = [sb([128, 512], F32) for _ in range(3)]
        mg, B_mg = sb([128, 8, 512], BF16)
        xn, B_xn = sb([128, 8, 512], F32)
        ytok = [sb([128, D], F32) for _ in range(2)]
        last_layer = (l == depth - 1)
        for (sg0, sgn, si, full) in segs:
            if not full:
                continue
            for tt in range(sgn // 512):
                t0 = sg0 + tt * 512
                S.dma("sp", lambda e, t0=t0: e.dma_start(out=hT[:], in_=HT[:, :, t0:t0 + 512].rearrange("k p t -> p k t")), "ld_hT",
                      writes=[B_hT])
                S.dma("sp", lambda e, t0=t0, l=l: e.dma_start(out=xT[:], in_=XT[l][:, :, t0:t0 + 512].rearrange("k p t -> p k t")),
                      "ld_xT", writes=[B_xT])
                S.dma("sp", lambda e, t0=t0: e.dma_start(out=yz[0:96, 0:4, :], in_=YZ[0:4, 0:96, t0:t0 + 512].rearrange("c p t -> p c t")),
                      "ld_yz", writes=[B_yz])
                S.dma("sp", lambda e, t0=t0: e.dma_start(out=yz[0:64, 4:10, :], in_=YZ[4:10, 0:64, t0:t0 + 512].rearrange("c p t -> p c t")),
                      "ld_yz", writes=[B_yz])
                S.dma("sp", lambda e, t0=t0: e.dma_start(out=yz[:, 10:14, :], in_=YZ[10:14, :, t0:t0 + 512].rearrange("c p t -> p c t")),
                      "ld_yz", writes=[B_yz])
                for oc in range(8):
                    osl = slice(oc * 128, (oc + 1) * 128)
                    for hh in range(4):
                        S.op("pe", lambda e, hh=hh, osl=osl: e.matmul(psum[0][:], lhsT=woa_bf[0:96, hh, osl], rhs=yz[0:96, hh, :],
                                                                     start=(hh == 0), stop=(hh == 3)),
                             reads=[B_wbg, B_yz], writes=[PB[0]], sig=(hh == 3))
                    for hh in range(6):
                        S.op("pe", lambda e, hh=hh, osl=osl: e.matmul(psum[1][:], lhsT=wob_bf[0:64, hh, osl], rhs=yz[0:64, 4 + hh, :],
                                                                     start=(hh == 0), stop=(hh == 5)),
                             reads=[B_wbg, B_yz], writes=[PB[1]], sig=(hh == 5))
                    for hh in range(4):
                        S.op("pe", lambda e, hh=hh, osl=osl: e.matmul(psum[2][:], lhsT=woc_bf[:, hh, osl], rhs=yz[:, 10 + hh, :],
                                                                     start=(hh == 0), stop=(hh == 3)),
                             reads=[B_wbg, B_yz], writes=[PB[2]], sig=(hh == 3))
                    for br in range(3):
                        c0 = br * 1024 + oc * 128
                        for kc in range(8):
                            S.op("pe", lambda e, kc=kc, c0=c0, br=br: e.matmul(psum[3 + br][:], lhsT=wbg_bf[:, kc, c0:c0 + 128],
                                                                              rhs=hT[:, kc, :], start=(kc == 0), stop=(kc == 7)),
                                 reads=[B_wbg, B_hT], writes=[PB[3 + br]], sig=(kc == 7))
                        gt, B_g = gsb[br]
                        mt, B_m = msb[br]
                        ch = br * 8 + oc
                        S.op("act", lambda e, br=br, gt=gt, ch=ch: e.activation(out=gt[:], in_=psum[3 + br][:], func=AF.Sigmoid,
                                                                               bias=bbg_t[:, ch:ch + 1], scale=1.0),
                             reads=[PB[3 + br], B_bbg], writes=[B_g])
                        S.op("dve", lambda e, br=br, gt=gt, mt=mt: e.tensor_tensor(out=mt[:], in0=psum[br][:], in1=gt[:], op=ALU.mult),
                             reads=[PB[br], B_g], writes=[B_m])
                    S.op("pool", lambda e: e.tensor_tensor(out=msb[0][0][:], in0=msb[0][0][:], in1=msb[1][0][:], op=ALU.add),
                         reads=[msb[0][1], msb[1][1]], writes=[msb[0][1]])
                    S.op("pool", lambda e, oc=oc: e.tensor_tensor(out=mg[:, oc, :], in0=msb[0][0][:], in1=msb[2][0][:], op=ALU.add),
                         reads=[msb[0][1], msb[2][1]], writes=[B_mg])
                for oc in range(8):
                    osl = slice(oc * 128, (oc + 1) * 128)
                    pb = 6 + oc % 2
                    for kc in range(8):
                        S.op("pe", lambda e, kc=kc, osl=osl, pb=pb: e.matmul(psum[pb][:], lhsT=wout_bf[:, kc, osl], rhs=mg[:, kc, :],
                                                                            start=(kc == 0), stop=(kc == 7)),
                             reads=[B_wbg, B_mg], writes=[PB[pb]], sig=(kc == 7))
                    S.op("dve", lambda e, oc=oc, pb=pb, si=si: e.scalar_tensor_tensor(
                        out=xn[:, oc, :], in0=psum[pb][:], scalar=modT[:, 16 + oc, si:si + 1], in1=xT[:, oc, :], op0=ALU.mult,
                        op1=ALU.add), reads=[PB[pb], B_mod, B_xT], writes=[B_xn])
                if not last_layer:
                    S.dma("pool", lambda e, t0=t0, l=l: e.dma_start(out=XT[l + 1][:, :, t0:t0 + 512].rearrange("k p t -> p k t"),
                                                                   in_=xn[:]), "st_xn", reads=[B_xn])
                    if sg0 == SP:
                        o0 = t0 - SP
                        S.dma("pool", lambda e, o0=o0: e.dma_start(out=AGsrc[:, :, o0:o0 + 512].rearrange("k p t -> p k t"), in_=xn[:]),
                              "st_ag", reads=[B_xn], writes=[B_AGsrc])
                else:
                    for sub in range(4):
                        yt, B_y = ytok[sub % 2]
                        for hf in range(2):
                            pb = 0 + hf
                            for k4 in range(4):
                                kc = hf * 4 + k4
                                S.op("pe", lambda e, pb=pb, k4=k4, kc=kc, sub=sub: e.transpose(
                                    psum[pb][:, k4 * 128:(k4 + 1) * 128], xn[:, kc, sub * 128:(sub + 1) * 128], ident[:]),
                                    reads=[B_xn, B_ident], writes=[PB[pb]], sig=(k4 == 3))
                            if hf == 0:
                                S.op("dve", lambda e, pb=pb, yt=yt: e.tensor_copy(out=yt[:, 0:512], in_=psum[pb][:]), reads=[PB[pb]],
                                     writes=[B_y])
                            else:
                                S.op("act", lambda e, pb=pb, yt=yt: e.activation(out=yt[:, 512:1024], in_=psum[pb][:], func=AF.Copy),
                                     reads=[PB[pb]], writes=[B_y])
                        S.dma("pool", lambda e, t0=t0, sub=sub, yt=yt: e.dma_start(out=y_out[t0 + sub * 128:t0 + (sub + 1) * 128, :],
                                                                                  in_=yt[:]), f"st_y{sub % 2}", reads=[B_y])
        if not last_layer:
            for kc in range(8):
                S.cc(lambda e, kc=kc: e.collective_compute(
                    "AllGather", ALU.bypass, replica_groups=[[0, 1], [2, 3], [4, 5], [6, 7]],
                    ins=[AGsrc[kc]], outs=[AGdst[kc].rearrange("r p t -> (r p) t")]), f"cc_{l}_{kc}", reads=[B_AGsrc], writes=[B_AGdst])

    S.final_wait("sp")
    S.emit()
    return nc, S


def _prep_common(inp, depth):
    f = np.float32

    def A(x):
        return np.ascontiguousarray(np.asarray(x, dtype=f))

    gc = np.zeros((depth, 128, 6), f)
    for j, (k, n) in enumerate((("qn_a", 96), ("kn_a", 96), ("qn_b", 64), ("kn_b", 64), ("qn_c", 64), ("kn_c", 64))):
        gc[:, :n, j] = A(inp[k])
        if n == 64:
            gc[:, 64:128, j] = A(inp[k])
    lam = np.concatenate([A(inp["lam_q1"]), A(inp["lam_k1"]), A(inp["lam_q2"]), A(inp["lam_k2"])], axis=1)[:, None, :]
    return {
        "w_ada": A(inp["w_ada"]).reshape(depth, 8, 128, 3 * D),
        "b_adaT": A(A(inp["b_ada"]).reshape(depth, 24, 128).transpose(0, 2, 1)),
        "norm_gT": A(A(inp["norm_g"]).reshape(depth, 8, 128).transpose(0, 2, 1)),
        "w_in": A(inp["w_in"]).reshape(depth, 8, 128, DIN),
        "gcols": gc,
        "lam": A(lam),
        "subln": A(inp["subln_c"]).reshape(depth, 128, 1),
        "w_oa": A(inp["w_oa"]).reshape(depth, 4, 96, D),
        "w_ob": A(inp["w_ob"]).reshape(depth, 6, 64, D),
        "w_oc": A(inp["w_oc"]).reshape(depth, 4, 128, D),
        "w_bg": A(inp["w_bg"]).reshape(depth, 8, 128, 3 * D),
        "b_bgT": A(A(inp["b_bg"]).reshape(depth, 24, 128).transpose(0, 2, 1)),
        "w_out": A(inp["w_out"]).reshape(depth, 8, 128, D),
        "rotm": _rot_mats(),
        "bandmask": _band_mask(),
        "ident": np.eye(128, dtype=f),
    }


def _core_inputs(common, rope_g, xp_c, xs_j, cp_c, cs_j, hf, SP, H):
    m = dict(common)
    own = slice(hf * H, (hf + 1) * H)
    oth = slice((1 - hf) * H, (2 - hf) * H)
    m["x"] = np.ascontiguousarray(np.concatenate([xp_c, xs_j[own], xs_j[oth]], axis=0))
    m["rope"] = np.ascontiguousarray(np.concatenate([rope_g[..., 0:SP], rope_g[..., own], rope_g[..., oth]], axis=-1))
    cc = np.stack([cp_c, cs_j], axis=0)
    m["cT"] = np.ascontiguousarray(cc.reshape(2, 8, 128).transpose(2, 1, 0))
    fl = np.zeros((128, 2), np.float32)
    fl[:, 0] = float(hf)
    fl[:, 1] = float(1 - hf)
    m["flags"] = fl
    return m


def kernel(**inp):
    xp = np.asarray(inp["x_prompt"], np.float32)
    xs = np.asarray(inp["x_sample"], np.float32)
    cp = np.asarray(inp["c_prompt"], np.float32)
    cs = np.asarray(inp["c_sample"], np.float32)
    depth = int(np.asarray(inp["norm_g"]).shape[0])
    SP, SS = xp.shape[1], xs.shape[1]
    H = SS // 2
    lam_inits = [0.8 - 0.6 * math.exp(-0.3 * l) for l in range(depth)]
    nc, _ = build(SP, H, depth, lam_inits)
    common = _prep_common(inp, depth)
    rope_g = _rope_tables(max(SP, SS))
    in_maps = [_core_inputs(common, rope_g, xp[c], xs[c // 2], cp[c], cs[c // 2], c % 2, SP, H) for c in range(8)]
    res = run_bass_kernel_spmd(nc, in_maps, core_ids=list(range(8)))
    yp = np.stack([res.results[c]["y"][:SP] for c in range(8)], axis=0)
    ys = np.stack([np.concatenate([res.results[2 * j]["y"][SP:], res.results[2 * j + 1]["y"][SP:]], axis=0)
                   for j in range(xs.shape[0])], axis=0)
    return (yp.astype(np.float32), ys.astype(np.float32))
```

```python
import math
import types
import numpy as np
import concourse.bass as bass
import concourse.mybir as mybir
from concourse.bass_utils import run_bass_kernel_spmd

F32 = mybir.dt.float32
BF16 = mybir.dt.bfloat16
AF = mybir.ActivationFunctionType
ALU = mybir.AluOpType

D = 1024
DIN = 6912
EPS = 1e-6
A_GROUPS = ((128, 1), (512, 4), (2048, 16))
OFF = dict(qa=0, ka=1152, va=2304, za=3456, qb=3840, kb=4224, vb=4352, zb=4480, qc=4864, kc=5376, vc=5888, zc=6400)
ENGS = ("pe", "act", "dve", "pool", "sp")


def _freeze(fn):
    if fn.__closure__ is None:
        return fn
    cells = []
    for c in fn.__closure__:
        try:
            cells.append(types.CellType(c.cell_contents))
        except ValueError:
            cells.append(c)
    return types.FunctionType(fn.__code__, fn.__globals__, fn.__name__, fn.__defaults__, tuple(cells))


class Buf:
    __slots__ = ("name", "w", "r")

    def __init__(self, name):
        self.name = name
        self.w = None
        self.r = []


class Sched:
    def __init__(self, nc):
        self.nc = nc
        self.q = {e: [] for e in ENGS}
        self.sems = {}
        self.cnt = {}
        self.seen = {e: {} for e in ENGS}
        for e in ENGS:
            self._sem("E_" + e)
        self.n_ops = 0

    def _sem(self, key):
        if key not in self.sems:
            self.sems[key] = self.nc.alloc_semaphore(key)
            self.cnt[key] = 0
        return self.sems[key]

    def _need(self, eng, ev, force=False):
        if ev is None:
            return
        key, val = ev
        if val <= 0:
            return
        if eng == "pe" and key == "E_pe" and not force:
            return
        if self.seen[eng].get(key, 0) >= val:
            return
        self.seen[eng][key] = val
        self.q[eng].append(("wait", key, val))

    def _deps(self, eng, reads, writes):
        for b in reads:
            self._need(eng, b.w)
        for b in writes:
            self._need(eng, b.w)
            for ev in b.r:
                self._need(eng, ev)

    def _commit(self, ev, reads, writes):
        for b in reads:
            b.r.append(ev)
            if len(b.r) > 48:
                best = {}
                for k, v in b.r:
                    if best.get(k, 0) < v:
                        best[k] = v
                b.r = list(best.items())
        for b in writes:
            b.w = ev
            b.r = []

    def op(self, eng, fn, reads=(), writes=(), sig=True):
        self._deps(eng, reads, writes)
        key = "E_" + eng
        if sig:
            self.cnt[key] += 1
            ev = (key, self.cnt[key])
        else:
            ev = (key, self.cnt[key] + 1)
        self.q[eng].append(("op", _freeze(fn), key if sig else None))
        self._commit(ev, reads, writes)
        self.n_ops += 1

    def dma(self, eng, fn, sem_key, reads=(), writes=()):
        self._deps(eng, reads, writes)
        self._sem(sem_key)
        self.cnt[sem_key] += 16
        ev = (sem_key, self.cnt[sem_key])
        self.q[eng].append(("dma", _freeze(fn), sem_key))
        self._commit(ev, reads, writes)
        self.n_ops += 1

    def cc(self, fn, sem_key, reads=(), writes=()):
        eng = "pool"
        self._deps(eng, reads, writes)
        self._sem(sem_key)
        self.cnt[sem_key] += 1
        ev = (sem_key, self.cnt[sem_key])
        self.q[eng].append(("cc", _freeze(fn), sem_key))
        self._commit(ev, reads, writes)
        self.n_ops += 1

    def barrier(self):
        evs = [(k, v) for k, v in self.cnt.items() if v > 0]
        for e in ENGS:
            for ev in evs:
                if ev[0] != "E_" + e:
                    self._need(e, ev, force=True)

    def final_wait(self, eng="sp"):
        for k, v in self.cnt.items():
            if v > 0 and k != "E_" + eng:
                self._need(eng, (k, v), force=True)

    def emit(self):
        nc = self.nc
        engmap = {"pe": "tensor", "act": "scalar", "dve": "vector", "pool": "gpsimd", "sp": "sync"}
        sems = self.sems
        with nc.Block() as block:
            for e in ENGS:
                items = self.q[e]
                if not items:
                    continue

                def body(eng, items=items):
                    for it in items:
                        if it[0] == "wait":
                            eng.wait_ge(sems[it[1]], it[2])
                        elif it[0] == "op":
                            ins = it[1](eng)
                            if it[2] is not None:
                                ins.then_inc(sems[it[2]], 1)
                        elif it[0] == "cc":
                            it[1](eng).then_inc(sems[it[2]])
                        else:
                            it[1](eng).then_inc(sems[it[2]], 16)

                getattr(block, engmap[e])(body)


def _rope_tables(smax):
    pos = np.arange(smax)
    f32 = np.float32

    def ang(p, dim, theta):
        inv = (f32(theta) ** (-np.arange(0, dim, 2, dtype=f32) / f32(dim))).astype(f32)
        return (p.astype(f32)[:, None] * inv[None, :]).astype(f32)

    tab = np.zeros((3, 2, 128, smax), f32)
    tab[:, 0] = 1.0
    aa = ang(pos, 24, 500000.0)
    tab[0, 0, 0:12] = np.cos(aa).T
    tab[0, 0, 12:24] = np.cos(aa).T
    tab[0, 1, 0:12] = np.sin(aa).T
    tab[0, 1, 12:24] = np.sin(aa).T
    tab[0, :, 96:] = 0.0
    ar = ang(pos // 64, 32, 10000.0)
    ac = ang(pos % 64, 32, 10000.0)
    tab[1, 0, 0:16] = np.cos(ar).T
    tab[1, 0, 16:32] = np.cos(ar).T
    tab[1, 1, 0:16] = np.sin(ar).T
    tab[1, 1, 16:32] = np.sin(ar).T
    tab[1, 0, 32:48] = np.cos(ac).T
    tab[1, 0, 48:64] = np.cos(ac).T
    tab[1, 1, 32:48] = np.sin(ac).T
    tab[1, 1, 48:64] = np.sin(ac).T
    a_c = ang(pos, 16, 500000.0)
    tab[2, 0, 0:8] = np.cos(a_c).T
    tab[2, 0, 8:16] = np.cos(a_c).T
    tab[2, 1, 0:8] = np.sin(a_c).T
    tab[2, 1, 8:16] = np.sin(a_c).T
    tab[1, :, 64:128] = tab[1, :, 0:64]
    tab[2, :, 64:128] = tab[2, :, 0:64]
    return tab


def _rot_mats():
    R = np.zeros((3, 128, 128), np.float32)

    def fill(t, base, n):
        h = n // 2
        for i in range(h):
            R[t, base + i + h, base + i] = -1.0
            R[t, base + i, base + i + h] = 1.0

    fill(0, 0, 24)
    for hb in (0, 64):
        fill(1, hb + 0, 32)
        fill(1, hb + 32, 32)
        fill(2, hb + 0, 16)
    return R


def _band_mask():
    p = np.arange(128)[:, None]
    j = np.arange(256)[None, :]
    return ((j >= p) & (j <= p + 128)).astype(np.float32)


def build(SP, H, depth, lam_inits):
    NT = SP + 2 * H
    NF = SP + H
    nseq = 2
    SMAX = max(SP, 2 * H)
    segs = [(0, SP, 0, True), (SP, H, 1, True), (SP + H, H, 1, False)]
    attn = [(0, SP, 0, SP, False), (SP, H, SP, 2 * H, True)]
    nc = bass.Bass("TRN2", target_bir_lowering=False)

    def din(name, shape, dt=F32):
        return nc.dram_tensor(name, list(shape), dt, kind="ExternalInput").ap()

    x_in = din("x", [NT, D])
    cT_in = din("cT", [128, 8, nseq])
    w_ada = din("w_ada", [depth, 8, 128, 3 * D])
    b_adaT = din("b_adaT", [depth, 128, 24])
    norm_gT = din("norm_gT", [depth, 128, 8])
    w_in = din("w_in", [depth, 8, 128, DIN])
    gcols_in = din("gcols", [depth, 128, 6])
    lam_in = din("lam", [depth, 1, 256])
    subln_in = din("subln", [depth, 128, 1])
    w_oa = din("w_oa", [depth, 4, 96, D])
    w_ob = din("w_ob", [depth, 6, 64, D])
    w_oc = din("w_oc", [depth, 4, 128, D])
    w_bg = din("w_bg", [depth, 8, 128, 3 * D])
    b_bgT = din("b_bgT", [depth, 128, 24])
    w_out = din("w_out", [depth, 8, 128, D])
    rope_in = din("rope", [3, 2, 128, NT])
    flags_in = din("flags", [128, 2])
    rot_in = din("rotm", [3, 128, 128])
    mask_in = din("bandmask", [128, 256])
    ident_in = din("ident", [128, 128])
    y_out = nc.dram_tensor("y", [NF, D], F32, kind="ExternalOutput").ap()

    def dscr(name, shape, dt):
        return nc.dram_tensor(name, list(shape), dt).ap()

    XT = [dscr(f"XT{l}", [8, 128, NT], F32) for l in range(depth)]
    HT = dscr("HT", [8, 128, NT], BF16)
    QK = dscr("QK", [48, 128, NT], BF16)
    ZT = dscr("ZT", [14, 128, NT], BF16)
    VS = dscr("VS", [NT, 1792], BF16)
    YZ = dscr("YZ", [14, 128, NT], BF16)
    AGsrc = dscr("AGsrc", [8, 128, H], F32)
    AGdst = dscr("AGdst", [8, 2, 128, H], F32)
    B_AGsrc, B_AGdst = Buf("AGsrc"), Buf("AGdst")

    S = Sched(nc)
    _bn = [0]

    def sb(shape, dt, name=None):
        _bn[0] += 1
        name = "s_" + (name or f"t{_bn[0]}")
        return nc.alloc_sbuf_tensor(name, list(shape), dt), Buf(name)

    psum = [nc.alloc_psum_tensor(f"ps{i}", [128, 512], F32) for i in range(8)]
    PB = [Buf(f"ps{i}") for i in range(8)]

    ident, B_ident = sb([128, 128], F32, "ident")
    ones_bf, B_ones = sb([128, 128], BF16, "ones")
    zeros_bf, B_zeros = sb([128, 128], BF16, "zeros")
    zeros_w, _ = sb([128, 512], BF16, "zerosw")
    bd64, _ = sb([128, 128], BF16, "bd64")
    ones_lo, _ = sb([128, 96], BF16, "oneslo")
    ones_hi, _ = sb([128, 96], BF16, "oneshi")
    eps_t, B_eps = sb([128, 1], F32, "eps")
    mask_f, B_maskf = sb([128, 256], F32, "maskf")
    mask_bf, B_mask = sb([128, 256], BF16, "maskbf")
    rot_f, B_rotf = sb([128, 3, 128], F32, "rotf")
    cT, B_cT = sb([128, 8, nseq], F32, "cT")
    scT, B_scT = sb([128, 8, nseq], F32, "scT")

    S.dma("sp", lambda e: e.dma_start(out=ident[:], in_=ident_in), "ld_ident", writes=[B_ident])
    S.dma("sp", lambda e: e.dma_start(out=mask_f[:], in_=mask_in), "ld_mask", writes=[B_maskf])
    S.dma("sp", lambda e: e.dma_start(out=rot_f[:], in_=rot_in.rearrange("t p m -> p t m")), "ld_rot", writes=[B_rotf])
    S.dma("sp", lambda e: e.dma_start(out=cT[:], in_=cT_in), "ld_cT", writes=[B_cT])
    S.op("pool", lambda e: e.memset(ones_bf[:], 1.0), writes=[B_ones])
    S.op("pool", lambda e: e.memset(zeros_bf[:], 0.0), writes=[B_zeros])
    S.op("pool", lambda e: e.memset(bd64[:], 0.0), writes=[B_ones])
    S.op("pool", lambda e: e.memset(bd64[0:64, 0:64], 1.0), writes=[B_ones])
    S.op("pool", lambda e: e.memset(bd64[64:128, 64:128], 1.0), writes=[B_ones])
    S.op("pool", lambda e: e.memset(zeros_w[:], 0.0), writes=[B_zeros])
    S.op("pool", lambda e: e.memset(ones_lo[:], 1.0), writes=[B_ones])
    S.op("pool", lambda e: e.memset(ones_lo[0:64, :], 0.0), writes=[B_ones])
    S.op("pool", lambda e: e.memset(ones_hi[:], 0.0), writes=[B_ones])
    S.op("pool", lambda e: e.memset(ones_hi[0:64, :], 1.0), writes=[B_ones])
    S.op("pool", lambda e: e.memset(eps_t[:], EPS), writes=[B_eps])
    S.op("dve", lambda e: e.tensor_copy(out=mask_bf[:], in_=mask_f[:]), reads=[B_maskf], writes=[B_mask])
    S.op("act", lambda e: e.activation(out=scT[:], in_=cT[:], func=AF.Silu), reads=[B_cT], writes=[B_scT])
    flags, B_flags = sb([128, 2], F32, "flags")
    onesL, B_onesL = sb([128, 96], BF16, "onesL")
    onesR, _ = sb([128, 96], BF16, "onesR")
    S.dma("sp", lambda e: e.dma_start(out=flags[:], in_=flags_in), "ld_flags", writes=[B_flags])
    S.op("pool", lambda e: e.memset(onesL[:], 1.0), writes=[B_onesL])
    S.op("pool", lambda e: e.memset(onesR[:], 1.0), writes=[B_onesL])
    S.op("dve", lambda e: e.tensor_scalar(out=onesL[0:64, :], in0=onesL[0:64, :], scalar1=flags[0:64, 0:1], scalar2=None, op0=ALU.mult),
         reads=[B_flags, B_onesL], writes=[B_onesL])
    S.op("dve", lambda e: e.tensor_scalar(out=onesR[64:128, :], in0=onesR[64:128, :], scalar1=flags[64:128, 1:2], scalar2=None,
                                          op0=ALU.mult), reads=[B_flags, B_onesL], writes=[B_onesL])

    modT, B_mod = sb([128, 24, nseq], F32, "modT")
    gsT, B_gs = sb([128, 8, nseq], F32, "gsT")
    b_ada_t, B_bada = sb([128, 24], F32, "bada")
    ng_t, B_ng = sb([128, 8], F32, "ng")
    gcols, B_gcols = sb([128, 6], F32, "gcols")
    bbg_t, B_bbg = sb([128, 24], F32, "bbg")
    subln_t, B_subln = sb([128, 1], F32, "subln")
    sg_t, B_sg = sb([128, 1], F32, "sg")
    lam_t, B_lam = sb([1, 256], F32, "lam")
    lam_w, B_lamw = sb([1, 8], F32, "lamw")
    lam_bf, B_lambf = sb([1, 128], F32, "lambf")
    nlam_t, B_nlam = sb([128, 1], F32, "nlam")
    rotg, B_rotg = sb([128, 6, 128], BF16, "rotg")

    SB_TOP_CONST = nc.sbuf_base

    def phase_reset():
        S.barrier()
        nc.sbuf_base = SB_TOP_CONST

    for l in range(depth):
        lam_init = lam_inits[l]
        phase_reset()
        S.dma("sp", lambda e, l=l: e.dma_start(out=b_ada_t[:], in_=b_adaT[l]), "ld_bada", writes=[B_bada])
        S.dma("sp", lambda e, l=l: e.dma_start(out=ng_t[:], in_=norm_gT[l]), "ld_ng", writes=[B_ng])
        S.dma("sp", lambda e, l=l: e.dma_start(out=gcols[:], in_=gcols_in[l]), "ld_gcols", writes=[B_gcols])
        S.dma("sp", lambda e, l=l: e.dma_start(out=bbg_t[:], in_=b_bgT[l]), "ld_bbg", writes=[B_bbg])
        S.dma("sp", lambda e, l=l: e.dma_start(out=subln_t[:], in_=subln_in[l]), "ld_subln", writes=[B_subln])
        S.dma("sp", lambda e, l=l: e.dma_start(out=lam_t[:], in_=lam_in[l]), "ld_lam", writes=[B_lam])
        S.op("dve", lambda e: e.tensor_scalar(out=sg_t[:], in0=subln_t[:], scalar1=float(1.0 - lam_init), scalar2=None,
                                              op0=ALU.mult), reads=[B_subln], writes=[B_sg])
        S.op("dve", lambda e: e.tensor_tensor(out=lam_t[:, 0:64], in0=lam_t[:, 0:64], in1=lam_t[:, 64:128], op=ALU.mult),
             reads=[B_lam], writes=[B_lam])
        S.op("dve", lambda e: e.tensor_tensor(out=lam_t[:, 128:192], in0=lam_t[:, 128:192], in1=lam_t[:, 192:256], op=ALU.mult),
             reads=[B_lam], writes=[B_lam])
        S.op("dve", lambda e: e.reduce_sum(out=lam_w[:, 0:1], in_=lam_t[:, 0:64], axis=mybir.AxisListType.X),
             reads=[B_lam], writes=[B_lamw])
        S.op("dve", lambda e: e.reduce_sum(out=lam_w[:, 1:2], in_=lam_t[:, 128:192], axis=mybir.AxisListType.X),
             reads=[B_lam], writes=[B_lamw])
        S.op("act", lambda e: e.activation(out=lam_w[:, 2:4], in_=lam_w[:, 0:2], func=AF.Exp), reads=[B_lamw], writes=[B_lamw])
        S.op("dve", lambda e: e.tensor_tensor(out=lam_w[:, 4:5], in0=lam_w[:, 3:4], in1=lam_w[:, 2:3], op=ALU.subtract),
             reads=[B_lamw], writes=[B_lamw])
        S.op("dve", lambda e: e.tensor_scalar(out=lam_w[:, 5:6], in0=lam_w[:, 4:5], scalar1=float(-lam_init), scalar2=None,
                                              op0=ALU.add), reads=[B_lamw], writes=[B_lamw])
        S.op("pool", lambda e: e.memset(lam_bf[:], 1.0), writes=[B_lambf])
        S.op("pe", lambda e: e.matmul(psum[0][:, 0:1], lhsT=lam_bf[0:1, :], rhs=lam_w[0:1, 5:6], start=True, stop=True),
             reads=[B_lambf, B_lamw], writes=[PB[0]])
        S.op("dve", lambda e: e.tensor_copy(out=nlam_t[:], in_=psum[0][:, 0:1]), reads=[PB[0]], writes=[B_nlam])
        for j in range(6):
            S.op("dve", lambda e, j=j: e.tensor_scalar(out=rotg[:, j, :], in0=rot_f[:, j // 2, :], scalar1=gcols[:, j:j + 1],
                                                       scalar2=None, op0=ALU.mult),
                 reads=[B_rotf, B_gcols], writes=[B_rotg])
        wst, B_wst = sb([128, 8, 1536], F32)
        for half in range(2):
            for kc in range(8):
                S.dma("sp", lambda e, l=l, kc=kc, half=half: e.dma_start(
                    out=wst[:, kc, :], in_=w_ada[l, kc, :, half * 1536:(half + 1) * 1536]), "ld_wst", writes=[B_wst])
            for cc in range(12):
                ch = half * 12 + cc
                for kc in range(8):
                    S.op("pe", lambda e, kc=kc, cc=cc, ch=ch: e.matmul(
                        psum[1][:, ch * nseq:(ch + 1) * nseq], lhsT=wst[:, kc, cc * 128:(cc + 1) * 128], rhs=scT[:, kc, :],
                        start=(kc == 0), stop=(kc == 7)), reads=[B_wst, B_scT], writes=[PB[1]], sig=(kc == 7))
        for s in range(nseq):
            S.op("dve", lambda e, s=s: e.tensor_tensor(
                out=modT[:, :, s], in0=psum[1][:, 0:24 * nseq].rearrange("p (c s) -> p c s", s=nseq)[:, :, s], in1=b_ada_t[:],
                op=ALU.add), reads=[PB[1], B_bada], writes=[B_mod])
            S.op("dve", lambda e, s=s: e.scalar_tensor_tensor(
                out=gsT[:, :, s], in0=modT[:, 8:16, s], scalar=1.0, in1=ng_t[:], op0=ALU.add, op1=ALU.mult),
                reads=[B_mod, B_ng], writes=[B_gs])

        phase_reset()
        win_bf, B_win = sb([128, 8, DIN], BF16)
        wstg, B_wstg = sb([128, 1152], F32)
        for kc in range(8):
            for cg in range(6):
                S.dma("sp", lambda e, l=l, kc=kc, cg=cg: e.dma_start(
                    out=wstg[:], in_=w_in[l, kc, :, cg * 1152:(cg + 1) * 1152]), "ld_wstg", writes=[B_wstg])
                eng = ("act", "dve", "pool")[(kc * 6 + cg) % 3]
                if eng == "act":
                    S.op("act", lambda e, kc=kc, cg=cg: e.activation(out=win_bf[:, kc, cg * 1152:(cg + 1) * 1152], in_=wstg[:],
                                                                     func=AF.Copy), reads=[B_wstg], writes=[B_win])
                else:
                    S.op(eng, lambda e, kc=kc, cg=cg: e.tensor_copy(out=win_bf[:, kc, cg * 1152:(cg + 1) * 1152], in_=wstg[:]),
                         reads=[B_wstg], writes=[B_win])
        xT, B_xT = sb([128, 8, 512], F32)
        hT, B_hT = sb([128, 8, 512], BF16)
        if l == 0:
            xtok, B_xtok = sb([128, D], F32)
        tabs, B_tabs = sb([128, 3, 2, 512], F32)
        sq = [sb([128, 512], BF16) for _ in range(3)]
        ubf = [sb([128, 512], BF16) for _ in range(3)]
        rs = [sb([128, 512], F32) for _ in range(3)]
        t1 = [sb([128, 512], F32) for _ in range(3)]
        t2 = [sb([128, 512], F32) for _ in range(3)]
        qo = [sb([128, 512], BF16) for _ in range(3)]
        zo = [sb([128, 512], BF16) for _ in range(2)]
        vo = [sb([128, 1792], BF16) for _ in range(2)]
        tmpn, B_tmpn = sb([128, 512], F32)

        qk_chunks = []
        for h in range(12):
            qk_chunks.append((OFF["qa"] + 96 * h, 96, 0, 0, h, 96))
        for h in range(12):
            qk_chunks.append((OFF["ka"] + 96 * h, 96, 0, 1, 12 + h, 96))
        for h in range(0, 6, 2):
            qk_chunks.append((OFF["qb"] + 64 * h, 128, 1, 2, 24 + h, 64))
        for h in range(0, 2, 2):
            qk_chunks.append((OFF["kb"] + 64 * h, 128, 1, 3, 30 + h, 64))
        for h in range(0, 8, 2):
            qk_chunks.append((OFF["qc"] + 64 * h, 128, 2, 4, 32 + h, 64))
        for h in range(0, 8, 2):
            qk_chunks.append((OFF["kc"] + 64 * h, 128, 2, 5, 40 + h, 64))
        z_chunks = []
        for h in range(4):
            z_chunks.append((OFF["za"] + 96 * h, 96, h))
        for h in range(6):
            z_chunks.append((OFF["zb"] + 64 * h, 64, 4 + h))
        for h in range(4):
            z_chunks.append((OFF["zc"] + 128 * h, 128, 10 + h))
        v_groups = [(OFF["va"], 512, 0), (OFF["va"] + 512, 512, 512), (OFF["va"] + 1024, 128, 1024),
                    (OFF["vb"], 128, 1152), (OFF["vc"], 512, 1280)]

        it = 0
        if l > 0:
            xa, B_xa = sb([128, 4, 512], F32)
        for (sg0, sgn, si, full) in segs:
            for tt in range(sgn // 512):
                t0 = sg0 + tt * 512
                if l > 0 and not full:
                    o0 = t0 - (SP + H)
                    S.dma("sp", lambda e, o0=o0: e.dma_start(out=xT[:], in_=AGdst[:, 1, :, o0:o0 + 512].rearrange("k p t -> p k t")),
                          "ld_xT", reads=[B_AGdst], writes=[B_xT])
                    for hk in range(2):
                        S.dma("sp", lambda e, o0=o0, hk=hk: e.dma_start(
                            out=xa[:], in_=AGdst[hk * 4:(hk + 1) * 4, 0, :, o0:o0 + 512].rearrange("k p t -> p k t")),
                            "ld_xa", reads=[B_AGdst], writes=[B_xa])
                        S.op("pool", lambda e: e.tensor_scalar(out=xa[:], in0=xa[:], scalar1=flags[:, 0:1], scalar2=None, op0=ALU.mult),
                             reads=[B_xa, B_flags], writes=[B_xa])
                        S.op("dve", lambda e, hk=hk: e.scalar_tensor_tensor(
                            out=xT[:, hk * 4:(hk + 1) * 4, :], in0=xT[:, hk * 4:(hk + 1) * 4, :], scalar=flags[:, 1:2], in1=xa[:],
                            op0=ALU.mult, op1=ALU.add), reads=[B_xT, B_xa, B_flags], writes=[B_xT])
                elif l == 0:
                    for sub in range(4):
                        S.dma("sp", lambda e, t0=t0, sub=sub: e.dma_start(out=xtok[:], in_=x_in[t0 + sub * 128:t0 + (sub + 1) * 128, :]),
                              "ld_xtok", writes=[B_xtok])
                        for hf in range(2):
                            pb = 6 + hf
                            for k4 in range(4):
                                kc = hf * 4 + k4
                                S.op("pe", lambda e, pb=pb, k4=k4, kc=kc: e.transpose(
                                    psum[pb][:, k4 * 128:(k4 + 1) * 128], xtok[:, kc * 128:(kc + 1) * 128], ident[:]),
                                    reads=[B_xtok, B_ident], writes=[PB[pb]], sig=(k4 == 3))
                            S.op("dve" if hf == 0 else "act",
                                 (lambda e, pb=pb, hf=hf, sub=sub: e.tensor_copy(
                                     out=xT[:, hf * 4:(hf + 1) * 4, sub * 128:(sub + 1) * 128],
                                     in_=psum[pb][:].rearrange("p (k t) -> p k t", k=4))) if hf == 0 else
                                 (lambda e, pb=pb, hf=hf, sub=sub: e.activation(
                                     out=xT[:, hf * 4:(hf + 1) * 4, sub * 128:(sub + 1) * 128],
                                     in_=psum[pb][:].rearrange("p (k t) -> p k t", k=4), func=AF.Copy)),
                                 reads=[PB[pb]], writes=[B_xT])
                    if full:
                        S.dma("pool", lambda e, t0=t0, l=l: e.dma_start(
                            out=XT[l][:, :, t0:t0 + 512].rearrange("k p t -> p k t"), in_=xT[:]), "st_xT", reads=[B_xT])
                else:
                    S.dma("sp", lambda e, t0=t0, l=l: e.dma_start(
                        out=xT[:], in_=XT[l][:, :, t0:t0 + 512].rearrange("k p t -> p k t")), "ld_xT", writes=[B_xT])
                for kc in range(8):
                    sqt, B_sq = sq[kc % 2]
                    S.op("act", lambda e, kc=kc, sqt=sqt: e.activation(out=sqt[:], in_=xT[:, kc, :], func=AF.Square),
                         reads=[B_xT], writes=[B_sq])
                    S.op("pe", lambda e, kc=kc, sqt=sqt: e.matmul(psum[7][:], lhsT=ones_bf[:], rhs=sqt[:], start=(kc == 0),
                                                                  stop=(kc == 7)), reads=[B_ones, B_sq], writes=[PB[7]])
                rst, B_rs = rs[0]
                S.op("act", lambda e, rst=rst: e.activation(out=rst[:], in_=psum[7][:], func=AF.Sqrt, bias=eps_t[:], scale=1.0 / D),
                     reads=[PB[7], B_eps], writes=[B_rs])
                S.op("dve", lambda e, rst=rst: e.reciprocal(out=rst[:], in_=rst[:]), reads=[B_rs], writes=[B_rs])
                for kc in range(8):
                    S.op("dve", lambda e, kc=kc, rst=rst: e.tensor_tensor(out=tmpn[:], in0=xT[:, kc, :], in1=rst[:], op=ALU.mult),
                         reads=[B_xT, B_rs], writes=[B_tmpn])
                    S.op("act", lambda e, kc=kc, si=si: e.activation(out=hT[:, kc, :], in_=tmpn[:], func=AF.Identity,
                                                                     bias=modT[:, kc, si:si + 1], scale=gsT[:, kc, si:si + 1]),
                         reads=[B_tmpn, B_mod, B_gs], writes=[B_hT])
                if full:
                    S.dma("pool", lambda e, t0=t0: e.dma_start(out=HT[:, :, t0:t0 + 512].rearrange("k p t -> p k t"), in_=hT[:]),
                          "st_hT", reads=[B_hT])
                S.dma("sp", lambda e, t0=t0: e.dma_start(
                    out=tabs[:], in_=rope_in[:, :, :, t0:t0 + 512].rearrange("t c p s -> p t c s")), "ld_tabs", writes=[B_tabs])
                def stA(c0, dh, ty, gj, cid, nd, u):
                    pu = u % 3
                    sqt, B_sq = sq[u % len(sq)]
                    ubt, B_ub = ubf[u % len(ubf)]
                    for kc in range(8):
                        S.op("pe", lambda e, kc=kc, c0=c0, dh=dh, pu=pu: e.matmul(
                            psum[pu][0:dh, :], lhsT=win_bf[:, kc, c0:c0 + dh], rhs=hT[:, kc, :], start=(kc == 0), stop=(kc == 7)),
                            reads=[B_win, B_hT], writes=[PB[pu]], sig=(kc == 7))
                    S.op("act", lambda e, dh=dh, pu=pu, sqt=sqt: e.activation(out=sqt[0:dh, :], in_=psum[pu][0:dh, :], func=AF.Square),
                         reads=[PB[pu]], writes=[B_sq])
                    S.op("act", lambda e, dh=dh, pu=pu, ubt=ubt: e.activation(out=ubt[0:dh, :], in_=psum[pu][0:dh, :], func=AF.Copy),
                         reads=[PB[pu]], writes=[B_ub])

                def stB(c0, dh, ty, gj, cid, nd, u, t0=t0):
                    onesm = ones_bf if nd == dh else bd64
                    pu, pss, prt = u % 3, 3 + u % 2, 5 + u % 2
                    sqt, B_sq = sq[u % len(sq)]
                    ubt, B_ub = ubf[u % len(ubf)]
                    rst, B_rs = rs[u % len(rs)]
                    t1t, B_t1 = t1[u % len(t1)]
                    t2t, B_t2 = t2[u % len(t2)]
                    qot, B_qo = qo[u % len(qo)]
                    S.op("pe", lambda e, dh=dh, pss=pss, sqt=sqt, onesm=onesm: e.matmul(psum[pss][0:dh, :], lhsT=onesm[0:dh, 0:dh],
                                                                                       rhs=sqt[0:dh, :], start=True, stop=True),
                         reads=[B_ones, B_sq], writes=[PB[pss]])
                    S.op("pe", lambda e, dh=dh, prt=prt, ubt=ubt, gj=gj: e.matmul(psum[prt][0:dh, :], lhsT=rotg[0:dh, gj, 0:dh],
                                                                                 rhs=ubt[0:dh, :], start=True, stop=True),
                         reads=[B_rotg, B_ub], writes=[PB[prt]])
                    S.op("act", lambda e, dh=dh, pss=pss, rst=rst: e.activation(out=rst[0:dh, :], in_=psum[pss][0:dh, :], func=AF.Sqrt,
                                                                               bias=eps_t[0:dh, :], scale=1.0 / nd),
                         reads=[PB[pss], B_eps], writes=[B_rs])
                    S.op("dve", lambda e, dh=dh, rst=rst: e.reciprocal(out=rst[0:dh, :], in_=rst[0:dh, :]), reads=[B_rs], writes=[B_rs])
                    S.op("dve", lambda e, dh=dh, pu=pu, t1t=t1t, gj=gj, ty=ty: e.scalar_tensor_tensor(
                        out=t1t[0:dh, :], in0=psum[pu][0:dh, :], scalar=gcols[0:dh, gj:gj + 1], in1=tabs[0:dh, ty, 0, :],
                        op0=ALU.mult, op1=ALU.mult), reads=[PB[pu], B_gcols, B_tabs], writes=[B_t1])
                    S.op("dve", lambda e, dh=dh, prt=prt, t2t=t2t, ty=ty: e.tensor_tensor(
                        out=t2t[0:dh, :], in0=psum[prt][0:dh, :], in1=tabs[0:dh, ty, 1, :], op=ALU.mult),
                        reads=[PB[prt], B_tabs], writes=[B_t2])
                    S.op("pool", lambda e, dh=dh, t1t=t1t, t2t=t2t: e.tensor_tensor(out=t1t[0:dh, :], in0=t1t[0:dh, :], in1=t2t[0:dh, :],
                                                                                   op=ALU.add), reads=[B_t1, B_t2], writes=[B_t1])
                    S.op("pool", lambda e, dh=dh, t1t=t1t, rst=rst, qot=qot: e.tensor_tensor(out=qot[0:dh, :], in0=t1t[0:dh, :],
                                                                                            in1=rst[0:dh, :], op=ALU.mult),
                         reads=[B_t1, B_rs], writes=[B_qo])
                    if nd == dh:
                        S.dma("pool", lambda e, dh=dh, cid=cid, t0=t0, qot=qot: e.dma_start(out=QK[cid, 0:dh, t0:t0 + 512], in_=qot[0:dh, :]),
                              f"st_qo{u % len(qo)}", reads=[B_qo])
                    else:
                        for hb in range(2):
                            S.dma("pool", lambda e, cid=cid, t0=t0, qot=qot, hb=hb: e.dma_start(
                                out=QK[cid + hb, 0:64, t0:t0 + 512], in_=qot[hb * 64:(hb + 1) * 64, :]), f"st_qo{u % len(qo)}h{hb}",
                                reads=[B_qo])

                chs = qk_chunks if full else [c for c in qk_chunks if c[3] % 2 == 1]
                nch = len(chs)
                for i in range(nch + 1):
                    if i < nch:
                        stA(*chs[i], it + i)
                    if i >= 1:
                        stB(*chs[i - 1], it + i - 1)
                it += nch
                for (c0, dz, cid) in (z_chunks if full else []):
                    u = it % 2
                    pu = it % 3
                    it += 1
                    zot, B_zo = zo[u]
                    for kc in range(8):
                        S.op("pe", lambda e, kc=kc, c0=c0, dz=dz, pu=pu: e.matmul(
                            psum[pu][0:dz, :], lhsT=win_bf[:, kc, c0:c0 + dz], rhs=hT[:, kc, :], start=(kc == 0), stop=(kc == 7)),
                            reads=[B_win, B_hT], writes=[PB[pu]], sig=(kc == 7))
                    S.op("act", lambda e, dz=dz, pu=pu, zot=zot: e.activation(out=zot[0:dz, :], in_=psum[pu][0:dz, :], func=AF.Silu),
                         reads=[PB[pu]], writes=[B_zo])
                    S.dma("pool", lambda e, dz=dz, cid=cid, t0=t0, zot=zot: e.dma_start(out=ZT[cid, 0:dz, t0:t0 + 512], in_=zot[0:dz, :]),
                          f"st_zo{u}", reads=[B_zo])
                for sub in range(4):
                    vot, B_vo = vo[sub % 2]
                    for gi, (c0, n, o0) in enumerate(v_groups):
                        pu = it % 3
                        it += 1
                        for kc in range(8):
                            S.op("pe", lambda e, kc=kc, c0=c0, n=n, pu=pu, sub=sub: e.matmul(
                                psum[pu][:, 0:n], lhsT=hT[:, kc, sub * 128:(sub + 1) * 128], rhs=win_bf[:, kc, c0:c0 + n],
                                start=(kc == 0), stop=(kc == 7)), reads=[B_win, B_hT], writes=[PB[pu]], sig=(kc == 7))
                        if gi % 2 == 0:
                            S.op("dve", lambda e, n=n, pu=pu, o0=o0, vot=vot: e.tensor_copy(out=vot[:, o0:o0 + n], in_=psum[pu][:, 0:n]),
                                 reads=[PB[pu]], writes=[B_vo])
                        else:
                            S.op("act", lambda e, n=n, pu=pu, o0=o0, vot=vot: e.activation(out=vot[:, o0:o0 + n], in_=psum[pu][:, 0:n],
                                                                                          func=AF.Copy), reads=[PB[pu]], writes=[B_vo])
                    S.dma("pool", lambda e, t0=t0, sub=sub, vot=vot: e.dma_start(out=VS[t0 + sub * 128:t0 + (sub + 1) * 128, :], in_=vot[:]),
                          f"st_vo{sub % 2}", reads=[B_vo])

        phase_reset()
        KW = max(SMAX, H + 2048)
        QW = max(SP, H)
        KTs = [[sb([128, KW], BF16) for _ in range(2)] for _ in range(2)]
        QTs = [[sb([128, QW], BF16) for _ in range(2)] for _ in range(2)]
        VTs = [sb([128, SMAX // 128 + 1, 128], BF16) for _ in range(2)]
        VTA = [VTs[0], sb([128, H // 128 + 1, 96], BF16)]
        vta_i = [0]
        PT = [sb([128, 512], BF16) for _ in range(6)]
        zt, B_zt = sb([128, 512], BF16)
        e1, B_e1 = sb([128, 512], F32)
        e2, B_e2 = sb([128, 512], F32)
        e3, B_e3 = sb([128, 512], F32)
        e4, B_e4 = sb([128, 512], F32)
        esq, B_esq = sb([128, 512], BF16)
        yzo, B_yzo = sb([128, 512], BF16)
        AO, B_AO = sb([96, QW], F32)
        AL, B_AL = sb([96, QW], F32)
        sc_rot = [0]
        sc_nb = [4]
        sel64, B_sel = sb([128, 64], F32)
        S.op("pool", lambda e: e.memset(sel64[:], 0.0), writes=[B_sel])
        S.op("pool", lambda e: e.memset(sel64[64:65, :], 1.0), writes=[B_sel])

        def run_pipeline(blocks, G=2, LA=1, nb=4):
            sc_nb[0] = nb
            assert G * (LA + 1) <= nb
            groups = [blocks[i:i + G] for i in range(0, len(blocks), G)]
            n = len(groups)
            for i in range(n + LA):
                if i < n:
                    for b in groups[i]:
                        b[0]()
                if i >= LA:
                    for b in groups[i - LA]:
                        b[1]()

        def c_epilogue(h, t0):
            S.dma("sp", lambda e, h=h, t0=t0: e.dma_start(out=zt[:], in_=ZT[10 + h, :, t0:t0 + 512]), "ld_zt", writes=[B_zt])
            S.op("act", lambda e: e.activation(out=e1[:], in_=psum[5][:], func=AF.Ln), reads=[PB[5]], writes=[B_e1])
            S.op("act", lambda e: e.activation(out=e2[:], in_=psum[7][:], func=AF.Ln), reads=[PB[7]], writes=[B_e2])
            S.op("act", lambda e: e.activation(out=e1[:], in_=e1[:], func=AF.Exp, scale=-1.0), reads=[B_e1], writes=[B_e1])
            S.op("act", lambda e: e.activation(out=e2[:], in_=e2[:], func=AF.Exp, scale=-1.0), reads=[B_e2], writes=[B_e2])
            S.op("dve", lambda e: e.tensor_tensor(out=e1[:], in0=psum[4][:], in1=e1[:], op=ALU.mult), reads=[PB[4], B_e1],
                 writes=[B_e1])
            S.op("dve", lambda e: e.tensor_tensor(out=e2[:], in0=psum[6][:], in1=e2[:], op=ALU.mult), reads=[PB[6], B_e2],
                 writes=[B_e2])
            S.op("dve", lambda e: e.scalar_tensor_tensor(out=e3[:], in0=e2[:], scalar=nlam_t[:, 0:1], in1=e1[:], op0=ALU.mult,
                                                         op1=ALU.add), reads=[B_e1, B_e2, B_nlam], writes=[B_e3])
            S.op("act", lambda e: e.activation(out=esq[:], in_=e3[:], func=AF.Square), reads=[B_e3], writes=[B_esq])
            r = 5
            S.op("pe", lambda e, r=r: e.matmul(psum[r][:], lhsT=ones_bf[:], rhs=esq[:], start=True, stop=True), reads=[B_ones, B_esq],
                 writes=[PB[r]])
            S.op("act", lambda e, r=r: e.activation(out=e4[:], in_=psum[r][:], func=AF.Ln, bias=eps_t[:], scale=1.0 / 128),
                 reads=[PB[r], B_eps], writes=[B_e4])
            S.op("act", lambda e: e.activation(out=e4[:], in_=e4[:], func=AF.Exp, scale=-0.5), reads=[B_e4], writes=[B_e4])
            S.op("dve", lambda e: e.scalar_tensor_tensor(out=e3[:], in0=e3[:], scalar=sg_t[:, 0:1], in1=e4[:], op0=ALU.mult,
                                                         op1=ALU.mult), reads=[B_e3, B_sg, B_e4], writes=[B_e3])
            S.op("pool", lambda e: e.tensor_tensor(out=yzo[:], in0=e3[:], in1=zt[:], op=ALU.mult), reads=[B_e3, B_zt],
                 writes=[B_yzo])
            S.dma("pool", lambda e, h=h, t0=t0: e.dma_start(out=YZ[10 + h, :, t0:t0 + 512], in_=yzo[:]), "st_yzo", reads=[B_yzo])


        def score_exp(ktile, B_k, kcol, dh, qtile, B_q, q0, nq, scale, kstep=1, qstep=1):
            r = sc_rot[0] % sc_nb[0]
            sc_rot[0] += 1
            ptt, B_pt = PT[r]
            ksl = ktile[0:dh, kcol:kcol + 127 * kstep + 1:kstep] if kstep > 1 else ktile[0:dh, kcol:kcol + 128]
            qsl = qtile[0:dh, q0:q0 + (nq - 1) * qstep + 1:qstep] if qstep > 1 else qtile[0:dh, q0:q0 + nq]
            S.op("pe", lambda e, r=r, ksl=ksl, qsl=qsl, nq=nq: e.matmul(psum[r][:, 0:nq], lhsT=ksl, rhs=qsl, start=True, stop=True),
                 reads=[B_k, B_q], writes=[PB[r]])
            S.op("act", lambda e, r=r, nq=nq, ptt=ptt, scale=scale: e.activation(out=ptt[:, 0:nq], in_=psum[r][:, 0:nq], func=AF.Exp,
                                                                                 scale=scale), reads=[PB[r]], writes=[B_pt])
            return ptt, B_pt

        for (aq0, anq, ak0, ank, halo) in attn:
            Sq = ank
            s0 = ak0
            nkb = ank // 128
            nqt = anq // 512
            oth0 = aq0 + anq
            def b_load_kv(kvh):
                bs = kvh % 2
                kt, B_k = KTs[bs][0]
                vt_, B_vt = VTs[bs]
                S.dma("sp", lambda e, kvh=kvh, s0=s0, Sq=Sq, kt=kt: e.dma_start(out=kt[0:64, 0:Sq], in_=QK[30 + kvh, 0:64, s0:s0 + Sq]),
                      f"ld_K{bs}0", writes=[B_k])
                S.dma("sp", lambda e, kvh=kvh, s0=s0, Sq=Sq, nkb=nkb, vt_=vt_: e.dma_start(
                    out=vt_[:, 0:nkb, 0:64],
                    in_=VS[s0:s0 + Sq, 1152 + kvh * 64:1152 + (kvh + 1) * 64].rearrange("(b p) c -> p b c", p=128)),
                    f"ld_V{bs}", writes=[B_vt])
                S.op("pool", lambda e, nkb=nkb, vt_=vt_: e.memset(vt_[:, 0:nkb, 64:65], 1.0), writes=[B_vt])

            def b_load_q(kvh, g):
                bs = kvh % 2
                qh = kvh * 3 + g
                qt_, B_q = QTs[bs][g % 2]
                S.dma("sp", lambda e, qh=qh, aq0=aq0, anq=anq, qt_=qt_: e.dma_start(out=qt_[0:64, 0:anq],
                                                                                    in_=QK[24 + qh, 0:64, aq0:aq0 + anq]),
                      f"ld_Q{bs}{g % 2}", writes=[B_q])

            b_units = [(kvh, g) for kvh in range(2) for g in range(3)]
            b_load_kv(0)
            b_load_q(0, 0)
            for kvh in range(2):
                bs = kvh % 2
                kt, B_k = KTs[bs][0]
                vt_, B_vt = VTs[bs]
                for g in range(3):
                    qh = kvh * 3 + g
                    qt_, B_q = QTs[bs][g % 2]
                    ui = b_units.index((kvh, g))
                    if ui + 1 < len(b_units):
                        nk, ng = b_units[ui + 1]
                        if nk != kvh:
                            b_load_kv(nk)
                        b_load_q(nk, ng)
                    blocks = []
                    for qt in range(nqt):
                        for kb in range(nkb):
                            st = {}

                            def s1(st=st, kb=kb, qt=qt, kt=kt, B_k=B_k, qt_=qt_, B_q=B_q):
                                st["pt"] = score_exp(kt, B_k, kb * 128, 64, qt_, B_q, qt * 512, 512, 0.125)

                            def s2(st=st, kb=kb, qt=qt, qh=qh, nkb=nkb, s0=aq0, vt_=vt_, B_vt=B_vt):
                                ptt, B_pt = st["pt"]
                                S.op("pe", lambda e, kb=kb, ptt=ptt, nkb=nkb, vt_=vt_: e.matmul(psum[6][0:65, :], lhsT=vt_[:, kb, 0:65],
                                                                                               rhs=ptt[:], start=(kb == 0),
                                                                                               stop=(kb == nkb - 1)),
                                     reads=[B_vt, B_pt], writes=[PB[6]])
                                if kb == nkb - 1:
                                    t0 = s0 + qt * 512
                                    S.dma("sp", lambda e, qh=qh, t0=t0: e.dma_start(out=zt[0:64, :], in_=ZT[4 + qh, 0:64, t0:t0 + 512]),
                                          "ld_zt", writes=[B_zt])
                                    S.op("act", lambda e: e.activation(out=e3[0:65, :], in_=psum[6][0:65, :], func=AF.Copy),
                                         reads=[PB[6]], writes=[B_e3])
                                    S.op("pe", lambda e: e.matmul(psum[7][0:64, :], lhsT=sel64[0:65, :], rhs=e3[0:65, :], start=True,
                                                                  stop=True), reads=[B_sel, B_e3], writes=[PB[7]])
                                    S.op("act", lambda e: e.activation(out=e1[0:64, :], in_=psum[7][0:64, :], func=AF.Ln), reads=[PB[7]],
                                         writes=[B_e1])
                                    S.op("act", lambda e: e.activation(out=e1[0:64, :], in_=e1[0:64, :], func=AF.Exp, scale=-1.0),
                                         reads=[B_e1], writes=[B_e1])
                                    S.op("dve", lambda e: e.tensor_tensor(out=e2[0:64, :], in0=e3[0:64, :], in1=e1[0:64, :], op=ALU.mult),
                                         reads=[B_e3, B_e1], writes=[B_e2])
                                    S.op("pool", lambda e: e.tensor_tensor(out=yzo[0:64, :], in0=e2[0:64, :], in1=zt[0:64, :], op=ALU.mult),
                                         reads=[B_e2, B_zt], writes=[B_yzo])
                                    S.dma("pool", lambda e, qh=qh, t0=t0: e.dma_start(out=YZ[4 + qh, 0:64, t0:t0 + 512], in_=yzo[0:64, :]),
                                          "st_yzo", reads=[B_yzo])

                            blocks.append((s1, s2))
                    run_pipeline(blocks, G=3, LA=1, nb=6)
            def c_load(h):
                bs = h % 2
                vt_, B_vt = VTs[bs]
                for j in range(2):
                    kt, B_k = KTs[bs][j]
                    qt_, B_q = QTs[bs][j]
                    S.dma("sp", lambda e, h=h, j=j, s0=s0, Sq=Sq, kt=kt: e.dma_start(
                        out=kt[0:64, 0:Sq], in_=QK[40 + 2 * h + j, 0:64, s0:s0 + Sq]), f"ld_K{bs}{j}", writes=[B_k])
                    S.dma("sp", lambda e, h=h, j=j, aq0=aq0, anq=anq, qt_=qt_: e.dma_start(
                        out=qt_[0:64, 0:anq], in_=QK[32 + 2 * h + j, 0:64, aq0:aq0 + anq]), f"ld_Q{bs}{j}", writes=[B_q])
                S.dma("sp", lambda e, h=h, s0=s0, Sq=Sq, nkb=nkb, vt_=vt_: e.dma_start(
                    out=vt_[:, 0:nkb, :],
                    in_=VS[s0:s0 + Sq, 1280 + h * 128:1280 + (h + 1) * 128].rearrange("(b p) c -> p b c", p=128)),
                    f"ld_V{bs}", writes=[B_vt])

            c_load(0)
            for h in range(4):
                bs = h % 2
                vt_, B_vt = VTs[bs]
                if h + 1 < 4:
                    c_load(h + 1)
                blocks = []
                for qt in range(nqt):
                    for kb in range(nkb):
                        for j in range(2):
                            st = {}

                            def s1(st=st, kb=kb, qt=qt, j=j, bs=bs):
                                st["pt"] = score_exp(KTs[bs][j][0], KTs[bs][j][1], kb * 128, 64, QTs[bs][j][0], QTs[bs][j][1], qt * 512, 512,
                                                     0.125)

                            def s2(st=st, kb=kb, qt=qt, j=j, h=h, nkb=nkb, s0=aq0, vt_=vt_, B_vt=B_vt):
                                ptt, B_pt = st["pt"]
                                S.op("pe", lambda e, kb=kb, ptt=ptt, nkb=nkb, j=j, vt_=vt_: e.matmul(psum[4 + 2 * j][:, :], lhsT=vt_[:, kb, :],
                                                                                                    rhs=ptt[:], start=(kb == 0),
                                                                                                    stop=(kb == nkb - 1)),
                                     reads=[B_vt, B_pt], writes=[PB[4 + 2 * j]], sig=False)
                                S.op("pe", lambda e, kb=kb, ptt=ptt, nkb=nkb, j=j: e.matmul(psum[5 + 2 * j][:, :], lhsT=ones_bf[:, :],
                                                                                           rhs=ptt[:], start=(kb == 0), stop=(kb == nkb - 1)),
                                     reads=[B_ones, B_pt], writes=[PB[5 + 2 * j]])
                                if kb == nkb - 1 and j == 1:
                                    c_epilogue(h, s0 + qt * 512)

                            blocks.append((s1, s2))
                run_pipeline(blocks, G=2, LA=1, nb=4)
            def a_load(h, g, ui):
                win, dil = A_GROUPS[g]
                hh = g * 4 + h
                Sq = anq
                s0 = aq0
                pad = 64 * dil
                kt, B_k = KTs[ui % 2][0]
                qt_, B_q = QTs[ui % 2][0]
                kk = f"ld_K{ui % 2}0"
                if halo:
                    S.dma("sp", lambda e, hh=hh, kt=kt, pad=pad, oth0=oth0, anq=anq: e.dma_start(
                        out=kt[0:96, 0:pad], in_=QK[12 + hh, 0:96, oth0 + anq - pad:oth0 + anq]), kk, writes=[B_k])
                    S.dma("sp", lambda e, hh=hh, kt=kt, pad=pad, oth0=oth0, Sq=Sq: e.dma_start(
                        out=kt[0:96, pad + Sq:pad + Sq + pad], in_=QK[12 + hh, 0:96, oth0:oth0 + pad]), kk, writes=[B_k])
                else:
                    S.op("pool", lambda e, kt=kt, pad=pad: e.memset(kt[0:96, 0:pad], 0.0), writes=[B_k])
                    S.op("pool", lambda e, kt=kt, pad=pad, Sq=Sq: e.memset(kt[0:96, pad + Sq:pad + Sq + pad], 0.0), writes=[B_k])
                S.dma("sp", lambda e, hh=hh, s0=s0, Sq=Sq, kt=kt, pad=pad: e.dma_start(
                    out=kt[0:96, pad:pad + Sq], in_=QK[12 + hh, 0:96, s0:s0 + Sq]), kk, writes=[B_k])
                S.dma("sp", lambda e, hh=hh, s0=s0, Sq=Sq, qt_=qt_: e.dma_start(out=qt_[0:96, 0:Sq], in_=QK[hh, 0:96, s0:s0 + Sq]),
                      f"ld_Q{ui % 2}0", writes=[B_q])

            a_units = [(h, g) for h in range(4) for g in range(3)]
            a_load(0, 0, 0)
            scale_a = 96 ** -0.5
            for h in range(4):
                for g, (win, dil) in enumerate(A_GROUPS):
                    hh = g * 4 + h
                    Sq = anq
                    s0 = aq0
                    L = Sq // dil
                    T = min(512, L)
                    pad = 64 * dil
                    ui = a_units.index((h, g))
                    kt, B_k = KTs[ui % 2][0]
                    qt_, B_q = QTs[ui % 2][0]
                    if ui + 1 < len(a_units):
                        a_load(a_units[ui + 1][0], a_units[ui + 1][1], ui + 1)
                    nblk = L // 128 + 1
                    for r in range(dil):
                        bi = vta_i[0] % 2
                        VTa, B_VTa = VTA[bi]
                        vta_i[0] += 1
                        if halo:
                            lrow = oth0 + anq - 64 * dil + r
                            rrow = oth0 + r
                            vl = VS[lrow:lrow + 63 * dil + 1:dil, hh * 96:(hh + 1) * 96] if dil > 1 else VS[lrow:lrow + 64, hh * 96:(hh + 1) * 96]
                            vr = VS[rrow:rrow + 63 * dil + 1:dil, hh * 96:(hh + 1) * 96] if dil > 1 else VS[rrow:rrow + 64, hh * 96:(hh + 1) * 96]
                            S.dma("sp", lambda e, vl=vl: e.dma_start(out=VTa[0:64, 0, 0:96], in_=vl), f"ld_Va{bi}", writes=[B_VTa])
                            S.dma("sp", lambda e, vr=vr, nblk=nblk: e.dma_start(out=VTa[64:128, nblk - 1, 0:96], in_=vr), f"ld_Va{bi}", writes=[B_VTa])
                            S.op("dve", lambda e: e.tensor_scalar(out=VTa[0:64, 0, 0:96], in0=VTa[0:64, 0, 0:96], scalar1=flags[0:64, 0:1],
                                                                  scalar2=None, op0=ALU.mult), reads=[B_flags, B_VTa], writes=[B_VTa])
                            S.op("dve", lambda e, nblk=nblk: e.tensor_scalar(out=VTa[64:128, nblk - 1, 0:96], in0=VTa[64:128, nblk - 1, 0:96],
                                                                             scalar1=flags[64:128, 1:2], scalar2=None, op0=ALU.mult),
                                 reads=[B_flags, B_VTa], writes=[B_VTa])
                        else:
                            S.op("pool", lambda e: e.memset(VTa[0:64, 0:1, 0:96], 0.0), writes=[B_VTa])
                            S.op("pool", lambda e, nblk=nblk: e.memset(VTa[64:128, nblk - 1:nblk, 0:96], 0.0), writes=[B_VTa])
                        vsrc = VS[s0:s0 + Sq, hh * 96:(hh + 1) * 96].rearrange("(i d) c -> d i c", d=dil)[r]
                        vsrc = vsrc.rearrange("(b p) c -> p b c", p=128)
                        S.dma("sp", lambda e, vsrc=vsrc, nblk=nblk: e.dma_start(out=VTa[64:128, 0:nblk - 1, 0:96], in_=vsrc[0:64]),
                              f"ld_Va{bi}", writes=[B_VTa])
                        S.dma("sp", lambda e, vsrc=vsrc, nblk=nblk: e.dma_start(out=VTa[0:64, 1:nblk, 0:96], in_=vsrc[64:128]),
                              f"ld_Vb{bi}", writes=[B_VTa])
                        blocks = []
                        for qt in range(L // T):
                            i0 = qt * T
                            nb = T // 128 + 1
                            for b in range(nb):
                                st = {}
                                jb = i0 // 128 + b
                                w0 = max(i0, i0 - 128 + 128 * b)
                                w1 = min(i0 + T, i0 + 128 + 128 * b)
                                nq = w1 - w0
                                m0 = w0 - (i0 - 128 + 128 * b)
                                kcol = r + dil * 128 * jb
                                q0 = r + dil * w0

                                def s1(st=st, kcol=kcol, q0=q0, nq=nq, m0=m0, kt=kt, B_k=B_k, qt_=qt_, B_q=B_q, dil=dil):
                                    ptt, B_pt = score_exp(kt, B_k, kcol, 96, qt_, B_q, q0, nq, scale_a, kstep=dil, qstep=dil)
                                    S.op("dve", lambda e, ptt=ptt, nq=nq, m0=m0: e.tensor_tensor(
                                        out=ptt[:, 0:nq], in0=ptt[:, 0:nq], in1=mask_bf[:, m0:m0 + nq], op=ALU.mult),
                                        reads=[B_pt, B_mask], writes=[B_pt])
                                    st["pt"] = (ptt, B_pt)

                                def s2(st=st, b=b, nb=nb, jb=jb, nq=nq, w0=w0, i0=i0, T=T, nblk=nblk, g=g, r=r, dil=dil, halo=halo, VTa=VTa, B_VTa=B_VTa):
                                    ptt, B_pt = st["pt"]
                                    if b == 0:
                                        S.op("pe", lambda e, T=T: e.matmul(psum[4][0:96, 0:T], lhsT=zeros_bf[:, 0:96], rhs=zeros_w[:, 0:T],
                                                                           start=True, stop=False), reads=[B_zeros], writes=[PB[4]],
                                             sig=False)
                                        S.op("pe", lambda e, T=T: e.matmul(psum[5][0:96, 0:T], lhsT=zeros_bf[:, 0:96], rhs=zeros_w[:, 0:T],
                                                                           start=True, stop=False), reads=[B_zeros], writes=[PB[5]],
                                             sig=False)
                                    c0 = w0 - i0
                                    last = (b == nb - 1)
                                    S.op("pe", lambda e, jb=jb, ptt=ptt, nq=nq, c0=c0, last=last: e.matmul(
                                        psum[4][0:96, c0:c0 + nq], lhsT=VTa[:, jb, 0:96], rhs=ptt[:, 0:nq], start=False, stop=last),
                                        reads=[B_VTa, B_pt], writes=[PB[4]], sig=False)
                                    if halo:
                                        onesm = onesL if jb == 0 else (onesR if jb == nblk - 1 else ones_bf)
                                    else:
                                        onesm = ones_lo if jb == 0 else (ones_hi if jb == nblk - 1 else ones_bf)
                                    S.op("pe", lambda e, ptt=ptt, nq=nq, c0=c0, last=last, onesm=onesm: e.matmul(
                                        psum[5][0:96, c0:c0 + nq], lhsT=onesm[:, 0:96], rhs=ptt[:, 0:nq], start=False, stop=last),
                                        reads=[B_ones, B_onesL, B_pt], writes=[PB[5]])
                                    if not last:
                                        return
                                    a0 = r + dil * i0
                                    if dil > 1:
                                        ao_sl = AO[:, a0:a0 + dil * (T - 1) + 1:dil]
                                        al_sl = AL[:, a0:a0 + dil * (T - 1) + 1:dil]
                                    else:
                                        ao_sl = AO[:, a0:a0 + T]
                                        al_sl = AL[:, a0:a0 + T]
                                    if g == 0:
                                        S.op("dve", lambda e, ao_sl=ao_sl, T=T: e.tensor_copy(out=ao_sl, in_=psum[4][0:96, 0:T]),
                                             reads=[PB[4]], writes=[B_AO])
                                        S.op("act", lambda e, al_sl=al_sl, T=T: e.activation(out=al_sl, in_=psum[5][0:96, 0:T],
                                                                                            func=AF.Copy), reads=[PB[5]], writes=[B_AL])
                                    else:
                                        S.op("dve", lambda e, ao_sl=ao_sl, T=T: e.tensor_tensor(out=ao_sl, in0=psum[4][0:96, 0:T],
                                                                                               in1=ao_sl, op=ALU.add),
                                             reads=[PB[4], B_AO], writes=[B_AO])
                                        S.op("dve", lambda e, al_sl=al_sl, T=T: e.tensor_tensor(out=al_sl, in0=psum[5][0:96, 0:T],
                                                                                               in1=al_sl, op=ALU.add),
                                             reads=[PB[5], B_AL], writes=[B_AL])

                                blocks.append((s1, s2))
                        run_pipeline(blocks, G=2, LA=1, nb=4)
                for qt in range(nqt):
                    t0 = aq0 + qt * 512
                    c0 = qt * 512
                    S.dma("sp", lambda e, h=h, t0=t0: e.dma_start(out=zt[0:96, :], in_=ZT[h, 0:96, t0:t0 + 512]), "ld_zt", writes=[B_zt])
                    S.op("act", lambda e, c0=c0: e.activation(out=e1[0:96, :], in_=AL[:, c0:c0 + 512], func=AF.Ln), reads=[B_AL],
                         writes=[B_e1])
                    S.op("act", lambda e: e.activation(out=e1[0:96, :], in_=e1[0:96, :], func=AF.Exp, scale=-1.0), reads=[B_e1],
                         writes=[B_e1])
                    S.op("dve", lambda e, c0=c0: e.tensor_tensor(out=e2[0:96, :], in0=AO[:, c0:c0 + 512], in1=e1[0:96, :], op=ALU.mult),
                         reads=[B_AO, B_e1], writes=[B_e2])
                    S.op("pool", lambda e: e.tensor_tensor(out=yzo[0:96, :], in0=e2[0:96, :], in1=zt[0:96, :], op=ALU.mult),
                         reads=[B_e2, B_zt], writes=[B_yzo])
                    S.dma("pool", lambda e, h=h, t0=t0: e.dma_start(out=YZ[h, 0:96, t0:t0 + 512], in_=yzo[0:96, :]), "st_yzo",
                          reads=[B_yzo])

        phase_reset()
        wbg_bf, B_wbg = sb([128, 8, 3 * D], BF16)
        woa_bf, B_woa = sb([96, 4, D], BF16)
        wob_bf, B_wob = sb([64, 6, D], BF16)
        woc_bf, B_woc = sb([128, 4, D], BF16)
        wout_bf, B_wout = sb([128, 8, D], BF16)
        wstg, B_wstg = sb([128, 1024], F32)
        cvt = [0]

        def load_cast(dst_ap, src_ap, np_):
            S.dma("sp", lambda e: e.dma_start(out=wstg[0:np_, :], in_=src_ap), "ld_wstg", writes=[B_wstg])
            eng = ("act", "dve", "pool")[cvt[0] % 3]
            cvt[0] += 1
            if eng == "act":
                S.op("act", lambda e: e.activation(out=dst_ap, in_=wstg[0:np_, :], func=AF.Copy), reads=[B_wstg], writes=[B_wbg])
            else:
                S.op(eng, lambda e: e.tensor_copy(out=dst_ap, in_=wstg[0:np_, :]), reads=[B_wstg], writes=[B_wbg])

        for kc in range(8):
            for cg in range(3):
                load_cast(wbg_bf[:, kc, cg * 1024:(cg + 1) * 1024], w_bg[l, kc, :, cg * 1024:(cg + 1) * 1024], 128)
            load_cast(wout_bf[:, kc, :], w_out[l, kc], 128)
        for hh in range(4):
            load_cast(woa_bf[:, hh, :], w_oa[l, hh], 96)
            load_cast(woc_bf[:, hh, :], w_oc[l, hh], 128)
        for hh in range(6):
            load_cast(wob_bf[:, hh, :], w_ob[l, hh], 64)
        B_woa = B_wob = B_woc = B_wout = B_wbg

        hT, B_hT = sb([128, 8, 512], BF16)
        xT, B_xT = sb([128, 8, 512], F32)
        yz, B_yz = sb([128, 14, 512], BF16)
        gsb = [sb([128, 512], F32) for _ in range(3)]
        msb = [sb([128, 512], F32) for _ in range(3)]
        mg, B_mg = sb([128, 8, 512], BF16)
        xn, B_xn = sb([128, 8, 512], F32)
        ytok = [sb([128, D], F32) for _ in range(2)]
        last_layer = (l == depth - 1)
        for (sg0, sgn, si, full) in segs:
            if not full:
                continue
            for tt in range(sgn // 512):
                t0 = sg0 + tt * 512
                S.dma("sp", lambda e, t0=t0: e.dma_start(out=hT[:], in_=HT[:, :, t0:t0 + 512].rearrange("k p t -> p k t")), "ld_hT",
                      writes=[B_hT])
                S.dma("sp", lambda e, t0=t0, l=l: e.dma_start(out=xT[:], in_=XT[l][:, :, t0:t0 + 512].rearrange("k p t -> p k t")),
                      "ld_xT", writes=[B_xT])
                S.dma("sp", lambda e, t0=t0: e.dma_start(out=yz[0:96, 0:4, :], in_=YZ[0:4, 0:96, t0:t0 + 512].rearrange("c p t -> p c t")),
                      "ld_yz", writes=[B_yz])
                S.dma("sp", lambda e, t0=t0: e.dma_start(out=yz[0:64, 4:10, :], in_=YZ[4:10, 0:64, t0:t0 + 512].rearrange("c p t -> p c t")),
                      "ld_yz", writes=[B_yz])
                S.dma("sp", lambda e, t0=t0: e.dma_start(out=yz[:, 10:14, :], in_=YZ[10:14, :, t0:t0 + 512].rearrange("c p t -> p c t")),
                      "ld_yz", writes=[B_yz])
                for oc in range(8):
                    osl = slice(oc * 128, (oc + 1) * 128)
                    for hh in range(4):
                        S.op("pe", lambda e, hh=hh, osl=osl: e.matmul(psum[0][:], lhsT=woa_bf[0:96, hh, osl], rhs=yz[0:96, hh, :],
                                                                     start=(hh == 0), stop=(hh == 3)),
                             reads=[B_wbg, B_yz], writes=[PB[0]], sig=(hh == 3))
                    for hh in range(6):
                        S.op("pe", lambda e, hh=hh, osl=osl: e.matmul(psum[1][:], lhsT=wob_bf[0:64, hh, osl], rhs=yz[0:64, 4 + hh, :],
                                                                     start=(hh == 0), stop=(hh == 5)),
                             reads=[B_wbg, B_yz], writes=[PB[1]], sig=(hh == 5))
                    for hh in range(4):
                        S.op("pe", lambda e, hh=hh, osl=osl: e.matmul(psum[2][:], lhsT=woc_bf[:, hh, osl], rhs=yz[:, 10 + hh, :],
                                                                     start=(hh == 0), stop=(hh == 3)),
                             reads=[B_wbg, B_yz], writes=[PB[2]], sig=(hh == 3))
                    for br in range(3):
                        c0 = br * 1024 + oc * 128
                        for kc in range(8):
                            S.op("pe", lambda e, kc=kc, c0=c0, br=br: e.matmul(psum[3 + br][:], lhsT=wbg_bf[:, kc, c0:c0 + 128],
                                                                              rhs=hT[:, kc, :], start=(kc == 0), stop=(kc == 7)),
                                 reads=[B_wbg, B_hT], writes=[PB[3 + br]], sig=(kc == 7))
                        gt, B_g = gsb[br]
                        mt, B_m = msb[br]
                        ch = br * 8 + oc
                        S.op("act", lambda e, br=br, gt=gt, ch=ch: e.activation(out=gt[:], in_=psum[3 + br][:], func=AF.Sigmoid,
                                                                               bias=bbg_t[:, ch:ch + 1], scale=1.0),
                             reads=[PB[3 + br], B_bbg], writes=[B_g])
                        S.op("dve", lambda e, br=br, gt=gt, mt=mt: e.tensor_tensor(out=mt[:], in0=psum[br][:], in1=gt[:], op=ALU.mult),
                             reads=[PB[br], B_g], writes=[B_m])
                    S.op("pool", lambda e: e.tensor_tensor(out=msb[0][0][:], in0=msb[0][0][:], in1=msb[1][0][:], op=ALU.add),
                         reads=[msb[0][1], msb[1][1]], writes=[msb[0][1]])
                    S.op("pool", lambda e, oc=oc: e.tensor_tensor(out=mg[:, oc, :], in0=msb[0][0][:], in1=msb[2][0][:], op=ALU.add),
                         reads=[msb[0][1], msb[2][1]], writes=[B_mg])
                for oc in range(8):
                    osl = slice(oc * 128, (oc + 1) * 128)
                    pb = 6 + oc % 2
                    for kc in range(8):
                        S.op("pe", lambda e, kc=kc, osl=osl, pb=pb: e.matmul(psum[pb][:], lhsT=wout_bf[:, kc, osl], rhs=mg[:, kc, :],
                                                                            start=(kc == 0), stop=(kc == 7)),
                             reads=[B_wbg, B_mg], writes=[PB[pb]], sig=(kc == 7))
                    S.op("dve", lambda e, oc=oc, pb=pb, si=si: e.scalar_tensor_tensor(
                        out=xn[:, oc, :], in0=psum[pb][:], scalar=modT[:, 16 + oc, si:si + 1], in1=xT[:, oc, :], op0=ALU.mult,
                        op1=ALU.add), reads=[PB[pb], B_mod, B_xT], writes=[B_xn])
                if not last_layer:
                    S.dma("pool", lambda e, t0=t0, l=l: e.dma_start(out=XT[l + 1][:, :, t0:t0 + 512].rearrange("k p t -> p k t"),
                                                                   in_=xn[:]), "st_xn", reads=[B_xn])
                    if sg0 == SP:
                        o0 = t0 - SP
                        S.dma("pool", lambda e, o0=o0: e.dma_start(out=AGsrc[:, :, o0:o0 + 512].rearrange("k p t -> p k t"), in_=xn[:]),
                              "st_ag", reads=[B_xn], writes=[B_AGsrc])
                else:
                    for sub in range(4):
                        yt, B_y = ytok[sub % 2]
                        for hf in range(2):
                            pb = 0 + hf
                            for k4 in range(4):
                                kc = hf * 4 + k4
                                S.op("pe", lambda e, pb=pb, k4=k4, kc=kc, sub=sub: e.transpose(
                                    psum[pb][:, k4 * 128:(k4 + 1) * 128], xn[:, kc, sub * 128:(sub + 1) * 128], ident[:]),
                                    reads=[B_xn, B_ident], writes=[PB[pb]], sig=(k4 == 3))
                            if hf == 0:
                                S.op("dve", lambda e, pb=pb, yt=yt: e.tensor_copy(out=yt[:, 0:512], in_=psum[pb][:]), reads=[PB[pb]],
                                     writes=[B_y])
                            else:
                                S.op("act", lambda e, pb=pb, yt=yt: e.activation(out=yt[:, 512:1024], in_=psum[pb][:], func=AF.Copy),
                                     reads=[PB[pb]], writes=[B_y])
                        S.dma("pool", lambda e, t0=t0, sub=sub, yt=yt: e.dma_start(out=y_out[t0 + sub * 128:t0 + (sub + 1) * 128, :],
                                                                                  in_=yt[:]), f"st_y{sub % 2}", reads=[B_y])
        if not last_layer:
            for kc in range(8):
                S.cc(lambda e, kc=kc: e.collective_compute(
                    "AllGather", ALU.bypass, replica_groups=[[0, 1], [2, 3], [4, 5], [6, 7]],
                    ins=[AGsrc[kc]], outs=[AGdst[kc].rearrange("r p t -> (r p) t")]), f"cc_{l}_{kc}", reads=[B_AGsrc], writes=[B_AGdst])

    S.final_wait("sp")
    S.emit()
    return nc, S


def _prep_common(inp, depth):
    f = np.float32

    def A(x):
        return np.ascontiguousarray(np.asarray(x, dtype=f))

    gc = np.zeros((depth, 128, 6), f)
    for j, (k, n) in enumerate((("qn_a", 96), ("kn_a", 96), ("qn_b", 64), ("kn_b", 64), ("qn_c", 64), ("kn_c", 64))):
        gc[:, :n, j] = A(inp[k])
        if n == 64:
            gc[:, 64:128, j] = A(inp[k])
    lam = np.concatenate([A(inp["lam_q1"]), A(inp["lam_k1"]), A(inp["lam_q2"]), A(inp["lam_k2"])], axis=1)[:, None, :]
    return {
        "w_ada": A(inp["w_ada"]).reshape(depth, 8, 128, 3 * D),
        "b_adaT": A(A(inp["b_ada"]).reshape(depth, 24, 128).transpose(0, 2, 1)),
        "norm_gT": A(A(inp["norm_g"]).reshape(depth, 8, 128).transpose(0, 2, 1)),
        "w_in": A(inp["w_in"]).reshape(depth, 8, 128, DIN),
        "gcols": gc,
        "lam": A(lam),
        "subln": A(inp["subln_c"]).reshape(depth, 128, 1),
        "w_oa": A(inp["w_oa"]).reshape(depth, 4, 96, D),
        "w_ob": A(inp["w_ob"]).reshape(depth, 6, 64, D),
        "w_oc": A(inp["w_oc"]).reshape(depth, 4, 128, D),
        "w_bg": A(inp["w_bg"]).reshape(depth, 8, 128, 3 * D),
        "b_bgT": A(A(inp["b_bg"]).reshape(depth, 24, 128).transpose(0, 2, 1)),
        "w_out": A(inp["w_out"]).reshape(depth, 8, 128, D),
        "rotm": _rot_mats(),
        "bandmask": _band_mask(),
        "ident": np.eye(128, dtype=f),
    }


def _core_inputs(common, rope_g, xp_c, xs_j, cp_c, cs_j, hf, SP, H):
    m = dict(common)
    own = slice(hf * H, (hf + 1) * H)
    oth = slice((1 - hf) * H, (2 - hf) * H)
    m["x"] = np.ascontiguousarray(np.concatenate([xp_c, xs_j[own], xs_j[oth]], axis=0))
    m["rope"] = np.ascontiguousarray(np.concatenate([rope_g[..., 0:SP], rope_g[..., own], rope_g[..., oth]], axis=-1))
    cc = np.stack([cp_c, cs_j], axis=0)
    m["cT"] = np.ascontiguousarray(cc.reshape(2, 8, 128).transpose(2, 1, 0))
    fl = np.zeros((128, 2), np.float32)
    fl[:, 0] = float(hf)
    fl[:, 1] = float(1 - hf)
    m["flags"] = fl
    return m


def kernel(**inp):
    xp = np.asarray(inp["x_prompt"], np.float32)
    xs = np.asarray(inp["x_sample"], np.float32)
    cp = np.asarray(inp["c_prompt"], np.float32)
    cs = np.asarray(inp["c_sample"], np.float32)
    depth = int(np.asarray(inp["norm_g"]).shape[0])
    SP, SS = xp.shape[1], xs.shape[1]
    H = SS // 2
    lam_inits = [0.8 - 0.6 * math.exp(-0.3 * l) for l in range(depth)]
    nc, _ = build(SP, H, depth, lam_inits)
    common = _prep_common(inp, depth)
    rope_g = _rope_tables(max(SP, SS))
    in_maps = [_core_inputs(common, rope_g, xp[c], xs[c // 2], cp[c], cs[c // 2], c % 2, SP, H) for c in range(8)]
    res = run_bass_kernel_spmd(nc, in_maps, core_ids=list(range(8)))
    yp = np.stack([res.results[c]["y"][:SP] for c in range(8)], axis=0)
    ys = np.stack([np.concatenate([res.results[2 * j]["y"][SP:], res.results[2 * j + 1]["y"][SP:]], axis=0)
                   for j in range(xs.shape[0])], axis=0)
    return (yp.astype(np.float32), ys.astype(np.float32))
```

```python
import math
import types
import numpy as np
import concourse.bass as bass
import concourse.mybir as mybir
from concourse.bass_utils import run_bass_kernel_spmd

F32 = mybir.dt.float32
BF16 = mybir.dt.bfloat16
AF = mybir.ActivationFunctionType
ALU = mybir.AluOpType

D = 1024
DIN = 6912
EPS = 1e-6
A_GROUPS = ((128, 1), (512, 4), (2048, 16))
OFF = dict(qa=0, ka=1152, va=2304, za=3456, qb=3840, kb=4224, vb=4352, zb=4480, qc=4864, kc=5376, vc=5888, zc=6400)
ENGS = ("pe", "act", "dve", "pool", "sp")


def _freeze(fn):
    if fn.__closure__ is None:
        return fn
    cells = []
    for c in fn.__closure__:
        try:
            cells.append(types.CellType(c.cell_contents))
        except ValueError:
            cells.append(c)
    return types.FunctionType(fn.__code__, fn.__globals__, fn.__name__, fn.__defaults__, tuple(cells))


class Buf:
    __slots__ = ("name", "w", "r")

    def __init__(self, name):
        self.name = name
        self.w = None
        self.r = []


class Sched:
    def __init__(self, nc):
        self.nc = nc
        self.q = {e: [] for e in ENGS}
        self.sems = {}
        self.cnt = {}
        self.seen = {e: {} for e in ENGS}
        for e in ENGS:
            self._sem("E_" + e)
        self.n_ops = 0

    def _sem(self, key):
        if key not in self.sems:
            self.sems[key] = self.nc.alloc_semaphore(key)
            self.cnt[key] = 0
        return self.sems[key]

    def _need(self, eng, ev, force=False):
        if ev is None:
            return
        key, val = ev
        if val <= 0:
            return
        if eng == "pe" and key == "E_pe" and not force:
            return
        if self.seen[eng].get(key, 0) >= val:
            return
        self.seen[eng][key] = val
        self.q[eng].append(("wait", key, val))

    def _deps(self, eng, reads, writes):
        for b in reads:
            self._need(eng, b.w)
        for b in writes:
            self._need(eng, b.w)
            for ev in b.r:
                self._need(eng, ev)

    def _commit(self, ev, reads, writes):
        for b in reads:
            b.r.append(ev)
            if len(b.r) > 48:
                best = {}
                for k, v in b.r:
                    if best.get(k, 0) < v:
                        best[k] = v
                b.r = list(best.items())
        for b in writes:
            b.w = ev
            b.r = []

    def op(self, eng, fn, reads=(), writes=(), sig=True):
        self._deps(eng, reads, writes)
        key = "E_" + eng
        if sig:
            self.cnt[key] += 1
            ev = (key, self.cnt[key])
        else:
            ev = (key, self.cnt[key] + 1)
        self.q[eng].append(("op", _freeze(fn), key if sig else None))
        self._commit(ev, reads, writes)
        self.n_ops += 1

    def dma(self, eng, fn, sem_key, reads=(), writes=()):
        self._deps(eng, reads, writes)
        self._sem(sem_key)
        self.cnt[sem_key] += 16
        ev = (sem_key, self.cnt[sem_key])
        self.q[eng].append(("dma", _freeze(fn), sem_key))
        self._commit(ev, reads, writes)
        self.n_ops += 1

    def cc(self, fn, sem_key, reads=(), writes=()):
        eng = "pool"
        self._deps(eng, reads, writes)
        self._sem(sem_key)
        self.cnt[sem_key] += 1
        ev = (sem_key, self.cnt[sem_key])
        self.q[eng].append(("cc", _freeze(fn), sem_key))
        self._commit(ev, reads, writes)
        self.n_ops += 1

    def barrier(self):
        evs = [(k, v) for k, v in self.cnt.items() if v > 0 and not k.startswith("cc_")]
        for e in ENGS:
            for ev in evs:
                if ev[0] != "E_" + e:
                    self._need(e, ev, force=True)

    def final_wait(self, eng="sp"):
        for k, v in self.cnt.items():
            if v > 0 and k != "E_" + eng:
                self._need(eng, (k, v), force=True)

    def emit(self):
        nc = self.nc
        engmap = {"pe": "tensor", "act": "scalar", "dve": "vector", "pool": "gpsimd", "sp": "sync"}
        sems = self.sems
        with nc.Block() as block:
            for e in ENGS:
                items = self.q[e]
                if not items:
                    continue

                def body(eng, items=items):
                    for it in items:
                        if it[0] == "wait":
                            eng.wait_ge(sems[it[1]], it[2])
                        elif it[0] == "op":
                            ins = it[1](eng)
                            if it[2] is not None:
                                ins.then_inc(sems[it[2]], 1)
                        elif it[0] == "cc":
                            it[1](eng).then_inc(sems[it[2]])
                        else:
                            it[1](eng).then_inc(sems[it[2]], 16)

                getattr(block, engmap[e])(body)


def _rope_tables(smax):
    pos = np.arange(smax)
    f32 = np.float32

    def ang(p, dim, theta):
        inv = (f32(theta) ** (-np.arange(0, dim, 2, dtype=f32) / f32(dim))).astype(f32)
        return (p.astype(f32)[:, None] * inv[None, :]).astype(f32)

    tab = np.zeros((3, 2, 128, smax), f32)
    tab[:, 0] = 1.0
    aa = ang(pos, 24, 500000.0)
    tab[0, 0, 0:12] = np.cos(aa).T
    tab[0, 0, 12:24] = np.cos(aa).T
    tab[0, 1, 0:12] = np.sin(aa).T
    tab[0, 1, 12:24] = np.sin(aa).T
    tab[0, :, 96:] = 0.0
    ar = ang(pos // 64, 32, 10000.0)
    ac = ang(pos % 64, 32, 10000.0)
    tab[1, 0, 0:16] = np.cos(ar).T
    tab[1, 0, 16:32] = np.cos(ar).T
    tab[1, 1, 0:16] = np.sin(ar).T
    tab[1, 1, 16:32] = np.sin(ar).T
    tab[1, 0, 32:48] = np.cos(ac).T
    tab[1, 0, 48:64] = np.cos(ac).T
    tab[1, 1, 32:48] = np.sin(ac).T
    tab[1, 1, 48:64] = np.sin(ac).T
    a_c = ang(pos, 16, 500000.0)
    tab[2, 0, 0:8] = np.cos(a_c).T
    tab[2, 0, 8:16] = np.cos(a_c).T
    tab[2, 1, 0:8] = np.sin(a_c).T
    tab[2, 1, 8:16] = np.sin(a_c).T
    tab[1, :, 64:128] = tab[1, :, 0:64]
    tab[2, :, 64:128] = tab[2, :, 0:64]
    return tab


def _rot_mats():
    R = np.zeros((3, 128, 128), np.float32)

    def fill(t, base, n):
        h = n // 2
        for i in range(h):
            R[t, base + i + h, base + i] = -1.0
            R[t, base + i, base + i + h] = 1.0

    fill(0, 0, 24)
    for hb in (0, 64):
        fill(1, hb + 0, 32)
        fill(1, hb + 32, 32)
        fill(2, hb + 0, 16)
    return R


def _band_mask():
    p = np.arange(128)[:, None]
    j = np.arange(256)[None, :]
    return ((j >= p) & (j <= p + 128)).astype(np.float32)


def build(SP, H, depth, lam_inits):
    NT = SP + 2 * H
    NF = SP + H
    nseq = 2
    SMAX = max(SP, 2 * H)
    segs = [(0, SP, 0, True), (SP, H, 1, True), (SP + H, H, 1, False)]
    attn = [(0, SP, 0, SP, False), (SP, H, SP, 2 * H, True)]
    nc = bass.Bass("TRN2", target_bir_lowering=False)

    def din(name, shape, dt=F32):
        return nc.dram_tensor(name, list(shape), dt, kind="ExternalInput").ap()

    x_in = din("x", [NT, D])
    cT_in = din("cT", [128, 8, nseq])
    w_ada = din("w_ada", [depth, 8, 128, 3 * D])
    b_adaT = din("b_adaT", [depth, 128, 24])
    norm_gT = din("norm_gT", [depth, 128, 8])
    w_in = din("w_in", [depth, 8, 128, DIN])
    gcols_in = din("gcols", [depth, 128, 6])
    lam_in = din("lam", [depth, 1, 256])
    subln_in = din("subln", [depth, 128, 1])
    w_oa = din("w_oa", [depth, 4, 96, D])
    w_ob = din("w_ob", [depth, 6, 64, D])
    w_oc = din("w_oc", [depth, 4, 128, D])
    w_bg = din("w_bg", [depth, 8, 128, 3 * D])
    b_bgT = din("b_bgT", [depth, 128, 24])
    w_out = din("w_out", [depth, 8, 128, D])
    rope_in = din("rope", [3, 2, 128, NT])
    flags_in = din("flags", [128, 2])
    rot_in = din("rotm", [3, 128, 128])
    mask_in = din("bandmask", [128, 256])
    ident_in = din("ident", [128, 128])
    y_out = nc.dram_tensor("y", [NF, D], F32, kind="ExternalOutput").ap()

    def dscr(name, shape, dt):
        return nc.dram_tensor(name, list(shape), dt).ap()

    XT = [dscr(f"XT{l}", [8, 128, NT], F32) for l in range(depth)]
    HT = dscr("HT", [8, 128, NT], BF16)
    QK = dscr("QK", [48, 128, NT], BF16)
    ZT = dscr("ZT", [14, 128, NT], BF16)
    VS = dscr("VS", [NT, 1792], BF16)
    YZ = dscr("YZ", [14, 128, NT], BF16)
    AGsrc = dscr("AGsrc", [8, 128, H], F32)
    AGdst = dscr("AGdst", [8, 2, 128, H], F32)
    B_AGsrc, B_AGdst = Buf("AGsrc"), Buf("AGdst")

    S = Sched(nc)
    _bn = [0]

    def sb(shape, dt, name=None):
        _bn[0] += 1
        name = "s_" + (name or f"t{_bn[0]}")
        return nc.alloc_sbuf_tensor(name, list(shape), dt), Buf(name)

    psum = [nc.alloc_psum_tensor(f"ps{i}", [128, 512], F32) for i in range(8)]
    PB = [Buf(f"ps{i}") for i in range(8)]

    ident, B_ident = sb([128, 128], F32, "ident")
    ones_bf, B_ones = sb([128, 128], BF16, "ones")
    zeros_bf, B_zeros = sb([128, 128], BF16, "zeros")
    zeros_w, _ = sb([128, 512], BF16, "zerosw")
    bd64, _ = sb([128, 128], BF16, "bd64")
    ones_lo, _ = sb([128, 96], BF16, "oneslo")
    ones_hi, _ = sb([128, 96], BF16, "oneshi")
    eps_t, B_eps = sb([128, 1], F32, "eps")
    mask_f, B_maskf = sb([128, 256], F32, "maskf")
    mask_bf, B_mask = sb([128, 256], BF16, "maskbf")
    rot_f, B_rotf = sb([128, 3, 128], F32, "rotf")
    cT, B_cT = sb([128, 8, nseq], F32, "cT")
    scT, B_scT = sb([128, 8, nseq], F32, "scT")

    S.dma("sp", lambda e: e.dma_start(out=ident[:], in_=ident_in), "ld_ident", writes=[B_ident])
    S.dma("sp", lambda e: e.dma_start(out=mask_f[:], in_=mask_in), "ld_mask", writes=[B_maskf])
    S.dma("sp", lambda e: e.dma_start(out=rot_f[:], in_=rot_in.rearrange("t p m -> p t m")), "ld_rot", writes=[B_rotf])
    S.dma("sp", lambda e: e.dma_start(out=cT[:], in_=cT_in), "ld_cT", writes=[B_cT])
    S.op("pool", lambda e: e.memset(ones_bf[:], 1.0), writes=[B_ones])
    S.op("pool", lambda e: e.memset(zeros_bf[:], 0.0), writes=[B_zeros])
    S.op("pool", lambda e: e.memset(bd64[:], 0.0), writes=[B_ones])
    S.op("pool", lambda e: e.memset(bd64[0:64, 0:64], 1.0), writes=[B_ones])
    S.op("pool", lambda e: e.memset(bd64[64:128, 64:128], 1.0), writes=[B_ones])
    S.op("pool", lambda e: e.memset(zeros_w[:], 0.0), writes=[B_zeros])
    S.op("pool", lambda e: e.memset(ones_lo[:], 1.0), writes=[B_ones])
    S.op("pool", lambda e: e.memset(ones_lo[0:64, :], 0.0), writes=[B_ones])
    S.op("pool", lambda e: e.memset(ones_hi[:], 0.0), writes=[B_ones])
    S.op("pool", lambda e: e.memset(ones_hi[0:64, :], 1.0), writes=[B_ones])
    S.op("pool", lambda e: e.memset(eps_t[:], EPS), writes=[B_eps])
    S.op("dve", lambda e: e.tensor_copy(out=mask_bf[:], in_=mask_f[:]), reads=[B_maskf], writes=[B_mask])
    S.op("act", lambda e: e.activation(out=scT[:], in_=cT[:], func=AF.Silu), reads=[B_cT], writes=[B_scT])
    flags, B_flags = sb([128, 2], F32, "flags")
    onesL, B_onesL = sb([128, 96], BF16, "onesL")
    onesR, _ = sb([128, 96], BF16, "onesR")
    S.dma("sp", lambda e: e.dma_start(out=flags[:], in_=flags_in), "ld_flags", writes=[B_flags])
    S.op("pool", lambda e: e.memset(onesL[:], 1.0), writes=[B_onesL])
    S.op("pool", lambda e: e.memset(onesR[:], 1.0), writes=[B_onesL])
    S.op("dve", lambda e: e.tensor_scalar(out=onesL[0:64, :], in0=onesL[0:64, :], scalar1=flags[0:64, 0:1], scalar2=None, op0=ALU.mult),
         reads=[B_flags, B_onesL], writes=[B_onesL])
    S.op("dve", lambda e: e.tensor_scalar(out=onesR[64:128, :], in0=onesR[64:128, :], scalar1=flags[64:128, 1:2], scalar2=None,
                                          op0=ALU.mult), reads=[B_flags, B_onesL], writes=[B_onesL])

    modT, B_mod = sb([128, 24, nseq], F32, "modT")
    gsT, B_gs = sb([128, 8, nseq], F32, "gsT")
    b_ada_t, B_bada = sb([128, 24], F32, "bada")
    ng_t, B_ng = sb([128, 8], F32, "ng")
    gcols, B_gcols = sb([128, 6], F32, "gcols")
    bbg_t, B_bbg = sb([128, 24], F32, "bbg")
    subln_t, B_subln = sb([128, 1], F32, "subln")
    sg_t, B_sg = sb([128, 1], F32, "sg")
    lam_t, B_lam = sb([1, 256], F32, "lam")
    lam_w, B_lamw = sb([1, 8], F32, "lamw")
    lam_bf, B_lambf = sb([1, 128], F32, "lambf")
    nlam_t, B_nlam = sb([128, 1], F32, "nlam")
    rotg, B_rotg = sb([128, 6, 128], BF16, "rotg")

    SB_TOP_CONST = nc.sbuf_base

    def phase_reset():
        S.barrier()
        nc.sbuf_base = SB_TOP_CONST

    for l in range(depth):
        lam_init = lam_inits[l]
        phase_reset()
        S.dma("sp", lambda e, l=l: e.dma_start(out=b_ada_t[:], in_=b_adaT[l]), "ld_bada", writes=[B_bada])
        S.dma("sp", lambda e, l=l: e.dma_start(out=ng_t[:], in_=norm_gT[l]), "ld_ng", writes=[B_ng])
        S.dma("sp", lambda e, l=l: e.dma_start(out=gcols[:], in_=gcols_in[l]), "ld_gcols", writes=[B_gcols])
        S.dma("sp", lambda e, l=l: e.dma_start(out=bbg_t[:], in_=b_bgT[l]), "ld_bbg", writes=[B_bbg])
        S.dma("sp", lambda e, l=l: e.dma_start(out=subln_t[:], in_=subln_in[l]), "ld_subln", writes=[B_subln])
        S.dma("sp", lambda e, l=l: e.dma_start(out=lam_t[:], in_=lam_in[l]), "ld_lam", writes=[B_lam])
        S.op("dve", lambda e: e.tensor_scalar(out=sg_t[:], in0=subln_t[:], scalar1=float(1.0 - lam_init), scalar2=None,
                                              op0=ALU.mult), reads=[B_subln], writes=[B_sg])
        S.op("dve", lambda e: e.tensor_tensor(out=lam_t[:, 0:64], in0=lam_t[:, 0:64], in1=lam_t[:, 64:128], op=ALU.mult),
             reads=[B_lam], writes=[B_lam])
        S.op("dve", lambda e: e.tensor_tensor(out=lam_t[:, 128:192], in0=lam_t[:, 128:192], in1=lam_t[:, 192:256], op=ALU.mult),
             reads=[B_lam], writes=[B_lam])
        S.op("dve", lambda e: e.reduce_sum(out=lam_w[:, 0:1], in_=lam_t[:, 0:64], axis=mybir.AxisListType.X),
             reads=[B_lam], writes=[B_lamw])
        S.op("dve", lambda e: e.reduce_sum(out=lam_w[:, 1:2], in_=lam_t[:, 128:192], axis=mybir.AxisListType.X),
             reads=[B_lam], writes=[B_lamw])
        S.op("act", lambda e: e.activation(out=lam_w[:, 2:4], in_=lam_w[:, 0:2], func=AF.Exp), reads=[B_lamw], writes=[B_lamw])
        S.op("dve", lambda e: e.tensor_tensor(out=lam_w[:, 4:5], in0=lam_w[:, 3:4], in1=lam_w[:, 2:3], op=ALU.subtract),
             reads=[B_lamw], writes=[B_lamw])
        S.op("dve", lambda e: e.tensor_scalar(out=lam_w[:, 5:6], in0=lam_w[:, 4:5], scalar1=float(-lam_init), scalar2=None,
                                              op0=ALU.add), reads=[B_lamw], writes=[B_lamw])
        S.op("pool", lambda e: e.memset(lam_bf[:], 1.0), writes=[B_lambf])
        S.op("pe", lambda e: e.matmul(psum[0][:, 0:1], lhsT=lam_bf[0:1, :], rhs=lam_w[0:1, 5:6], start=True, stop=True),
             reads=[B_lambf, B_lamw], writes=[PB[0]])
        S.op("dve", lambda e: e.tensor_copy(out=nlam_t[:], in_=psum[0][:, 0:1]), reads=[PB[0]], writes=[B_nlam])
        for j in range(6):
            S.op("dve", lambda e, j=j: e.tensor_scalar(out=rotg[:, j, :], in0=rot_f[:, j // 2, :], scalar1=gcols[:, j:j + 1],
                                                       scalar2=None, op0=ALU.mult),
                 reads=[B_rotf, B_gcols], writes=[B_rotg])
        wst, B_wst = sb([128, 8, 1536], F32)
        for half in range(2):
            for kc in range(8):
                S.dma("sp", lambda e, l=l, kc=kc, half=half: e.dma_start(
                    out=wst[:, kc, :], in_=w_ada[l, kc, :, half * 1536:(half + 1) * 1536]), "ld_wst", writes=[B_wst])
            for cc in range(12):
                ch = half * 12 + cc
                for kc in range(8):
                    S.op("pe", lambda e, kc=kc, cc=cc, ch=ch: e.matmul(
                        psum[1][:, ch * nseq:(ch + 1) * nseq], lhsT=wst[:, kc, cc * 128:(cc + 1) * 128], rhs=scT[:, kc, :],
                        start=(kc == 0), stop=(kc == 7)), reads=[B_wst, B_scT], writes=[PB[1]], sig=(kc == 7))
        for s in range(nseq):
            S.op("dve", lambda e, s=s: e.tensor_tensor(
                out=modT[:, :, s], in0=psum[1][:, 0:24 * nseq].rearrange("p (c s) -> p c s", s=nseq)[:, :, s], in1=b_ada_t[:],
                op=ALU.add), reads=[PB[1], B_bada], writes=[B_mod])
            S.op("dve", lambda e, s=s: e.scalar_tensor_tensor(
                out=gsT[:, :, s], in0=modT[:, 8:16, s], scalar=1.0, in1=ng_t[:], op0=ALU.add, op1=ALU.mult),
                reads=[B_mod, B_ng], writes=[B_gs])

        phase_reset()
        win_bf, B_win = sb([128, 8, DIN], BF16)
        wstg, B_wstg = sb([128, 1152], F32)
        for kc in range(8):
            for cg in range(6):
                S.dma("sp", lambda e, l=l, kc=kc, cg=cg: e.dma_start(
                    out=wstg[:], in_=w_in[l, kc, :, cg * 1152:(cg + 1) * 1152]), "ld_wstg", writes=[B_wstg])
                eng = ("act", "dve", "pool")[(kc * 6 + cg) % 3]
                if eng == "act":
                    S.op("act", lambda e, kc=kc, cg=cg: e.activation(out=win_bf[:, kc, cg * 1152:(cg + 1) * 1152], in_=wstg[:],
                                                                     func=AF.Copy), reads=[B_wstg], writes=[B_win])
                else:
                    S.op(eng, lambda e, kc=kc, cg=cg: e.tensor_copy(out=win_bf[:, kc, cg * 1152:(cg + 1) * 1152], in_=wstg[:]),
                         reads=[B_wstg], writes=[B_win])
        xT, B_xT = sb([128, 8, 512], F32)
        hT, B_hT = sb([128, 8, 512], BF16)
        if l == 0:
            xtok, B_xtok = sb([128, D], F32)
        tabs, B_tabs = sb([128, 3, 2, 512], F32)
        sq = [sb([128, 512], BF16) for _ in range(3)]
        ubf = [sb([128, 512], BF16) for _ in range(3)]
        rs = [sb([128, 512], F32) for _ in range(3)]
        t1 = [sb([128, 512], F32) for _ in range(3)]
        t2 = [sb([128, 512], F32) for _ in range(3)]
        qo = [sb([128, 512], BF16) for _ in range(3)]
        zo = [sb([128, 512], BF16) for _ in range(2)]
        vo = [sb([128, 1792], BF16) for _ in range(2)]
        tmpn, B_tmpn = sb([128, 512], F32)

        qk_chunks = []
        for h in range(12):
            qk_chunks.append((OFF["qa"] + 96 * h, 96, 0, 0, h, 96))
        for h in range(12):
            qk_chunks.append((OFF["ka"] + 96 * h, 96, 0, 1, 12 + h, 96))
        for h in range(0, 6, 2):
            qk_chunks.append((OFF["qb"] + 64 * h, 128, 1, 2, 24 + h, 64))
        for h in range(0, 2, 2):
            qk_chunks.append((OFF["kb"] + 64 * h, 128, 1, 3, 30 + h, 64))
        for h in range(0, 8, 2):
            qk_chunks.append((OFF["qc"] + 64 * h, 128, 2, 4, 32 + h, 64))
        for h in range(0, 8, 2):
            qk_chunks.append((OFF["kc"] + 64 * h, 128, 2, 5, 40 + h, 64))
        z_chunks = []
        for h in range(4):
            z_chunks.append((OFF["za"] + 96 * h, 96, h))
        for h in range(6):
            z_chunks.append((OFF["zb"] + 64 * h, 64, 4 + h))
        for h in range(4):
            z_chunks.append((OFF["zc"] + 128 * h, 128, 10 + h))
        v_groups = [(OFF["va"], 512, 0), (OFF["va"] + 512, 512, 512), (OFF["va"] + 1024, 128, 1024),
                    (OFF["vb"], 128, 1152), (OFF["vc"], 512, 1280)]

        it = 0
        if l > 0:
            xa, B_xa = sb([128, 4, 512], F32)
        for (sg0, sgn, si, full) in segs:
            for tt in range(sgn // 512):
                t0 = sg0 + tt * 512
                if l > 0 and not full:
                    o0 = t0 - (SP + H)
                    S.dma("sp", lambda e, o0=o0: e.dma_start(out=xT[:], in_=AGdst[:, 1, :, o0:o0 + 512].rearrange("k p t -> p k t")),
                          "ld_xT", reads=[B_AGdst], writes=[B_xT])
                    for hk in range(2):
                        S.dma("sp", lambda e, o0=o0, hk=hk: e.dma_start(
                            out=xa[:], in_=AGdst[hk * 4:(hk + 1) * 4, 0, :, o0:o0 + 512].rearrange("k p t -> p k t")),
                            "ld_xa", reads=[B_AGdst], writes=[B_xa])
                        S.op("pool", lambda e: e.tensor_scalar(out=xa[:], in0=xa[:], scalar1=flags[:, 0:1], scalar2=None, op0=ALU.mult),
                             reads=[B_xa, B_flags], writes=[B_xa])
                        S.op("dve", lambda e, hk=hk: e.scalar_tensor_tensor(
                            out=xT[:, hk * 4:(hk + 1) * 4, :], in0=xT[:, hk * 4:(hk + 1) * 4, :], scalar=flags[:, 1:2], in1=xa[:],
                            op0=ALU.mult, op1=ALU.add), reads=[B_xT, B_xa, B_flags], writes=[B_xT])
                elif l == 0:
                    for sub in range(4):
                        S.dma("sp", lambda e, t0=t0, sub=sub: e.dma_start(out=xtok[:], in_=x_in[t0 + sub * 128:t0 + (sub + 1) * 128, :]),
                              "ld_xtok", writes=[B_xtok])
                        for hf in range(2):
                            pb = 6 + hf
                            for k4 in range(4):
                                kc = hf * 4 + k4
                                S.op("pe", lambda e, pb=pb, k4=k4, kc=kc: e.transpose(
                                    psum[pb][:, k4 * 128:(k4 + 1) * 128], xtok[:, kc * 128:(kc + 1) * 128], ident[:]),
                                    reads=[B_xtok, B_ident], writes=[PB[pb]], sig=(k4 == 3))
                            S.op("dve" if hf == 0 else "act",
                                 (lambda e, pb=pb, hf=hf, sub=sub: e.tensor_copy(
                                     out=xT[:, hf * 4:(hf + 1) * 4, sub * 128:(sub + 1) * 128],
                                     in_=psum[pb][:].rearrange("p (k t) -> p k t", k=4))) if hf == 0 else
                                 (lambda e, pb=pb, hf=hf, sub=sub: e.activation(
                                     out=xT[:, hf * 4:(hf + 1) * 4, sub * 128:(sub + 1) * 128],
                                     in_=psum[pb][:].rearrange("p (k t) -> p k t", k=4), func=AF.Copy)),
                                 reads=[PB[pb]], writes=[B_xT])
                    if full:
                        S.dma("pool", lambda e, t0=t0, l=l: e.dma_start(
                            out=XT[l][:, :, t0:t0 + 512].rearrange("k p t -> p k t"), in_=xT[:]), "st_xT", reads=[B_xT])
                else:
                    S.dma("sp", lambda e, t0=t0, l=l: e.dma_start(
                        out=xT[:], in_=XT[l][:, :, t0:t0 + 512].rearrange("k p t -> p k t")), "ld_xT", writes=[B_xT])
                for kc in range(8):
                    sqt, B_sq = sq[kc % 2]
                    S.op("act", lambda e, kc=kc, sqt=sqt: e.activation(out=sqt[:], in_=xT[:, kc, :], func=AF.Square),
                         reads=[B_xT], writes=[B_sq])
                    S.op("pe", lambda e, kc=kc, sqt=sqt: e.matmul(psum[7][:], lhsT=ones_bf[:], rhs=sqt[:], start=(kc == 0),
                                                                  stop=(kc == 7)), reads=[B_ones, B_sq], writes=[PB[7]])
                rst, B_rs = rs[0]
                S.op("act", lambda e, rst=rst: e.activation(out=rst[:], in_=psum[7][:], func=AF.Sqrt, bias=eps_t[:], scale=1.0 / D),
                     reads=[PB[7], B_eps], writes=[B_rs])
                S.op("dve", lambda e, rst=rst: e.reciprocal(out=rst[:], in_=rst[:]), reads=[B_rs], writes=[B_rs])
                for kc in range(8):
                    S.op("dve", lambda e, kc=kc, rst=rst: e.tensor_tensor(out=tmpn[:], in0=xT[:, kc, :], in1=rst[:], op=ALU.mult),
                         reads=[B_xT, B_rs], writes=[B_tmpn])
                    S.op("act", lambda e, kc=kc, si=si: e.activation(out=hT[:, kc, :], in_=tmpn[:], func=AF.Identity,
                                                                     bias=modT[:, kc, si:si + 1], scale=gsT[:, kc, si:si + 1]),
                         reads=[B_tmpn, B_mod, B_gs], writes=[B_hT])
                if full:
                    S.dma("pool", lambda e, t0=t0: e.dma_start(out=HT[:, :, t0:t0 + 512].rearrange("k p t -> p k t"), in_=hT[:]),
                          "st_hT", reads=[B_hT])
                S.dma("sp", lambda e, t0=t0: e.dma_start(
                    out=tabs[:], in_=rope_in[:, :, :, t0:t0 + 512].rearrange("t c p s -> p t c s")), "ld_tabs", writes=[B_tabs])
                def stA(c0, dh, ty, gj, cid, nd, u):
                    pu = u % 3
                    sqt, B_sq = sq[u % len(sq)]
                    ubt, B_ub = ubf[u % len(ubf)]
                    for kc in range(8):
                        S.op("pe", lambda e, kc=kc, c0=c0, dh=dh, pu=pu: e.matmul(
                            psum[pu][0:dh, :], lhsT=win_bf[:, kc, c0:c0 + dh], rhs=hT[:, kc, :], start=(kc == 0), stop=(kc == 7)),
                            reads=[B_win, B_hT], writes=[PB[pu]], sig=(kc == 7))
                    S.op("act", lambda e, dh=dh, pu=pu, sqt=sqt: e.activation(out=sqt[0:dh, :], in_=psum[pu][0:dh, :], func=AF.Square),
                         reads=[PB[pu]], writes=[B_sq])
                    S.op("act", lambda e, dh=dh, pu=pu, ubt=ubt: e.activation(out=ubt[0:dh, :], in_=psum[pu][0:dh, :], func=AF.Copy),
                         reads=[PB[pu]], writes=[B_ub])

                def stB(c0, dh, ty, gj, cid, nd, u, t0=t0):
                    onesm = ones_bf if nd == dh else bd64
                    pu, pss, prt = u % 3, 3 + u % 2, 5 + u % 2
                    sqt, B_sq = sq[u % len(sq)]
                    ubt, B_ub = ubf[u % len(ubf)]
                    rst, B_rs = rs[u % len(rs)]
                    t1t, B_t1 = t1[u % len(t1)]
                    t2t, B_t2 = t2[u % len(t2)]
                    qot, B_qo = qo[u % len(qo)]
                    S.op("pe", lambda e, dh=dh, pss=pss, sqt=sqt, onesm=onesm: e.matmul(psum[pss][0:dh, :], lhsT=onesm[0:dh, 0:dh],
                                                                                       rhs=sqt[0:dh, :], start=True, stop=True),
                         reads=[B_ones, B_sq], writes=[PB[pss]])
                    S.op("pe", lambda e, dh=dh, prt=prt, ubt=ubt, gj=gj: e.matmul(psum[prt][0:dh, :], lhsT=rotg[0:dh, gj, 0:dh],
                                                                                 rhs=ubt[0:dh, :], start=True, stop=True),
                         reads=[B_rotg, B_ub], writes=[PB[prt]])
                    S.op("act", lambda e, dh=dh, pss=pss, rst=rst: e.activation(out=rst[0:dh, :], in_=psum[pss][0:dh, :], func=AF.Sqrt,
                                                                               bias=eps_t[0:dh, :], scale=1.0 / nd),
                         reads=[PB[pss], B_eps], writes=[B_rs])
                    S.op("dve", lambda e, dh=dh, rst=rst: e.reciprocal(out=rst[0:dh, :], in_=rst[0:dh, :]), reads=[B_rs], writes=[B_rs])
                    S.op("dve", lambda e, dh=dh, pu=pu, t1t=t1t, gj=gj, ty=ty: e.scalar_tensor_tensor(
                        out=t1t[0:dh, :], in0=psum[pu][0:dh, :], scalar=gcols[0:dh, gj:gj + 1], in1=tabs[0:dh, ty, 0, :],
                        op0=ALU.mult, op1=ALU.mult), reads=[PB[pu], B_gcols, B_tabs], writes=[B_t1])
                    S.op("dve", lambda e, dh=dh, prt=prt, t2t=t2t, ty=ty: e.tensor_tensor(
                        out=t2t[0:dh, :], in0=psum[prt][0:dh, :], in1=tabs[0:dh, ty, 1, :], op=ALU.mult),
                        reads=[PB[prt], B_tabs], writes=[B_t2])
                    S.op("pool", lambda e, dh=dh, t1t=t1t, t2t=t2t: e.tensor_tensor(out=t1t[0:dh, :], in0=t1t[0:dh, :], in1=t2t[0:dh, :],
                                                                                   op=ALU.add), reads=[B_t1, B_t2], writes=[B_t1])
                    S.op("pool", lambda e, dh=dh, t1t=t1t, rst=rst, qot=qot: e.tensor_tensor(out=qot[0:dh, :], in0=t1t[0:dh, :],
                                                                                            in1=rst[0:dh, :], op=ALU.mult),
                         reads=[B_t1, B_rs], writes=[B_qo])
                    if nd == dh:
                        S.dma("pool", lambda e, dh=dh, cid=cid, t0=t0, qot=qot: e.dma_start(out=QK[cid, 0:dh, t0:t0 + 512], in_=qot[0:dh, :]),
                              f"st_qo{u % len(qo)}", reads=[B_qo])
                    else:
                        for hb in range(2):
                            S.dma("pool", lambda e, cid=cid, t0=t0, qot=qot, hb=hb: e.dma_start(
                                out=QK[cid + hb, 0:64, t0:t0 + 512], in_=qot[hb * 64:(hb + 1) * 64, :]), f"st_qo{u % len(qo)}h{hb}",
                                reads=[B_qo])

                chs = qk_chunks if full else [c for c in qk_chunks if c[3] % 2 == 1]
                nch = len(chs)
                for i in range(nch + 1):
                    if i < nch:
                        stA(*chs[i], it + i)
                    if i >= 1:
                        stB(*chs[i - 1], it + i - 1)
                it += nch
                for (c0, dz, cid) in (z_chunks if full else []):
                    u = it % 2
                    pu = it % 3
                    it += 1
                    zot, B_zo = zo[u]
                    for kc in range(8):
                        S.op("pe", lambda e, kc=kc, c0=c0, dz=dz, pu=pu: e.matmul(
                            psum[pu][0:dz, :], lhsT=win_bf[:, kc, c0:c0 + dz], rhs=hT[:, kc, :], start=(kc == 0), stop=(kc == 7)),
                            reads=[B_win, B_hT], writes=[PB[pu]], sig=(kc == 7))
                    S.op("act", lambda e, dz=dz, pu=pu, zot=zot: e.activation(out=zot[0:dz, :], in_=psum[pu][0:dz, :], func=AF.Silu),
                         reads=[PB[pu]], writes=[B_zo])
                    S.dma("pool", lambda e, dz=dz, cid=cid, t0=t0, zot=zot: e.dma_start(out=ZT[cid, 0:dz, t0:t0 + 512], in_=zot[0:dz, :]),
                          f"st_zo{u}", reads=[B_zo])
                for sub in range(4):
                    vot, B_vo = vo[sub % 2]
                    for gi, (c0, n, o0) in enumerate(v_groups):
                        pu = it % 3
                        it += 1
                        for kc in range(8):
                            S.op("pe", lambda e, kc=kc, c0=c0, n=n, pu=pu, sub=sub: e.matmul(
                                psum[pu][:, 0:n], lhsT=hT[:, kc, sub * 128:(sub + 1) * 128], rhs=win_bf[:, kc, c0:c0 + n],
                                start=(kc == 0), stop=(kc == 7)), reads=[B_win, B_hT], writes=[PB[pu]], sig=(kc == 7))
                        if gi % 2 == 0:
                            S.op("dve", lambda e, n=n, pu=pu, o0=o0, vot=vot: e.tensor_copy(out=vot[:, o0:o0 + n], in_=psum[pu][:, 0:n]),
                                 reads=[PB[pu]], writes=[B_vo])
                        else:
                            S.op("act", lambda e, n=n, pu=pu, o0=o0, vot=vot: e.activation(out=vot[:, o0:o0 + n], in_=psum[pu][:, 0:n],
                                                                                          func=AF.Copy), reads=[PB[pu]], writes=[B_vo])
                    S.dma("pool", lambda e, t0=t0, sub=sub, vot=vot: e.dma_start(out=VS[t0 + sub * 128:t0 + (sub + 1) * 128, :], in_=vot[:]),
                          f"st_vo{sub % 2}", reads=[B_vo])

        phase_reset()
        KW = max(SMAX, H + 2048)
        QW = max(SP, H)
        KTs = [[sb([128, KW], BF16) for _ in range(2)] for _ in range(2)]
        QTs = [[sb([128, QW], BF16) for _ in range(2)] for _ in range(2)]
        VTs = [sb([128, SMAX // 128 + 1, 128], BF16) for _ in range(2)]
        PT = [sb([128, 512], BF16) for _ in range(6)]
        zt, B_zt = sb([128, 512], BF16)
        e1, B_e1 = sb([128, 512], F32)
        e2, B_e2 = sb([128, 512], F32)
        e3, B_e3 = sb([128, 512], F32)
        e4, B_e4 = sb([128, 512], F32)
        esq, B_esq = sb([128, 512], BF16)
        yzo, B_yzo = sb([128, 512], BF16)
        AO, B_AO = sb([96, QW], F32)
        AL, B_AL = sb([96, QW], F32)
        sc_rot = [0]
        sc_nb = [4]
        sel64, B_sel = sb([128, 64], F32)
        S.op("pool", lambda e: e.memset(sel64[:], 0.0), writes=[B_sel])
        S.op("pool", lambda e: e.memset(sel64[64:65, :], 1.0), writes=[B_sel])

        def run_pipeline(blocks, G=2, LA=1, nb=4):
            sc_nb[0] = nb
            assert G * (LA + 1) <= nb
            groups = [blocks[i:i + G] for i in range(0, len(blocks), G)]
            n = len(groups)
            for i in range(n + LA):
                if i < n:
                    for b in groups[i]:
                        b[0]()
                if i >= LA:
                    for b in groups[i - LA]:
                        b[1]()

        def c_epilogue(h, t0):
            S.dma("sp", lambda e, h=h, t0=t0: e.dma_start(out=zt[:], in_=ZT[10 + h, :, t0:t0 + 512]), "ld_zt", writes=[B_zt])
            S.op("act", lambda e: e.activation(out=e1[:], in_=psum[5][:], func=AF.Ln), reads=[PB[5]], writes=[B_e1])
            S.op("act", lambda e: e.activation(out=e2[:], in_=psum[7][:], func=AF.Ln), reads=[PB[7]], writes=[B_e2])
            S.op("act", lambda e: e.activation(out=e1[:], in_=e1[:], func=AF.Exp, scale=-1.0), reads=[B_e1], writes=[B_e1])
            S.op("act", lambda e: e.activation(out=e2[:], in_=e2[:], func=AF.Exp, scale=-1.0), reads=[B_e2], writes=[B_e2])
            S.op("dve", lambda e: e.tensor_tensor(out=e1[:], in0=psum[4][:], in1=e1[:], op=ALU.mult), reads=[PB[4], B_e1],
                 writes=[B_e1])
            S.op("dve", lambda e: e.tensor_tensor(out=e2[:], in0=psum[6][:], in1=e2[:], op=ALU.mult), reads=[PB[6], B_e2],
                 writes=[B_e2])
            S.op("dve", lambda e: e.scalar_tensor_tensor(out=e3[:], in0=e2[:], scalar=nlam_t[:, 0:1], in1=e1[:], op0=ALU.mult,
                                                         op1=ALU.add), reads=[B_e1, B_e2, B_nlam], writes=[B_e3])
            S.op("act", lambda e: e.activation(out=esq[:], in_=e3[:], func=AF.Square), reads=[B_e3], writes=[B_esq])
            r = 5
            S.op("pe", lambda e, r=r: e.matmul(psum[r][:], lhsT=ones_bf[:], rhs=esq[:], start=True, stop=True), reads=[B_ones, B_esq],
                 writes=[PB[r]])
            S.op("act", lambda e, r=r: e.activation(out=e4[:], in_=psum[r][:], func=AF.Ln, bias=eps_t[:], scale=1.0 / 128),
                 reads=[PB[r], B_eps], writes=[B_e4])
            S.op("act", lambda e: e.activation(out=e4[:], in_=e4[:], func=AF.Exp, scale=-0.5), reads=[B_e4], writes=[B_e4])
            S.op("dve", lambda e: e.scalar_tensor_tensor(out=e3[:], in0=e3[:], scalar=sg_t[:, 0:1], in1=e4[:], op0=ALU.mult,
                                                         op1=ALU.mult), reads=[B_e3, B_sg, B_e4], writes=[B_e3])
            S.op("pool", lambda e: e.tensor_tensor(out=yzo[:], in0=e3[:], in1=zt[:], op=ALU.mult), reads=[B_e3, B_zt],
                 writes=[B_yzo])
            S.dma("pool", lambda e, h=h, t0=t0: e.dma_start(out=YZ[10 + h, :, t0:t0 + 512], in_=yzo[:]), "st_yzo", reads=[B_yzo])


        def score_exp(ktile, B_k, kcol, dh, qtile, B_q, q0, nq, scale, kstep=1, qstep=1):
            r = sc_rot[0] % sc_nb[0]
            sc_rot[0] += 1
            ptt, B_pt = PT[r]
            ksl = ktile[0:dh, kcol:kcol + 127 * kstep + 1:kstep] if kstep > 1 else ktile[0:dh, kcol:kcol + 128]
            qsl = qtile[0:dh, q0:q0 + (nq - 1) * qstep + 1:qstep] if qstep > 1 else qtile[0:dh, q0:q0 + nq]
            S.op("pe", lambda e, r=r, ksl=ksl, qsl=qsl, nq=nq: e.matmul(psum[r][:, 0:nq], lhsT=ksl, rhs=qsl, start=True, stop=True),
                 reads=[B_k, B_q], writes=[PB[r]])
            S.op("act", lambda e, r=r, nq=nq, ptt=ptt, scale=scale: e.activation(out=ptt[:, 0:nq], in_=psum[r][:, 0:nq], func=AF.Exp,
                                                                                 scale=scale), reads=[PB[r]], writes=[B_pt])
            return ptt, B_pt

        for (aq0, anq, ak0, ank, halo) in attn:
            Sq = ank
            s0 = ak0
            nkb = ank // 128
            nqt = anq // 512
            oth0 = aq0 + anq
            def b_load_kv(kvh):
                bs = kvh % 2
                kt, B_k = KTs[bs][0]
                vt_, B_vt = VTs[bs]
                S.dma("sp", lambda e, kvh=kvh, s0=s0, Sq=Sq, kt=kt: e.dma_start(out=kt[0:64, 0:Sq], in_=QK[30 + kvh, 0:64, s0:s0 + Sq]),
                      f"ld_K{bs}0", writes=[B_k])
                S.dma("sp", lambda e, kvh=kvh, s0=s0, Sq=Sq, nkb=nkb, vt_=vt_: e.dma_start(
                    out=vt_[:, 0:nkb, 0:64],
                    in_=VS[s0:s0 + Sq, 1152 + kvh * 64:1152 + (kvh + 1) * 64].rearrange("(b p) c -> p b c", p=128)),
                    f"ld_V{bs}", writes=[B_vt])
                S.op("pool", lambda e, nkb=nkb, vt_=vt_: e.memset(vt_[:, 0:nkb, 64:65], 1.0), writes=[B_vt])

            def b_load_q(kvh, g):
                bs = kvh % 2
                qh = kvh * 3 + g
                qt_, B_q = QTs[bs][g % 2]
                S.dma("sp", lambda e, qh=qh, aq0=aq0, anq=anq, qt_=qt_: e.dma_start(out=qt_[0:64, 0:anq],
                                                                                    in_=QK[24 + qh, 0:64, aq0:aq0 + anq]),
                      f"ld_Q{bs}{g % 2}", writes=[B_q])

            b_units = [(kvh, g) for kvh in range(2) for g in range(3)]
            b_load_kv(0)
            b_load_q(0, 0)
            for kvh in range(2):
                bs = kvh % 2
                kt, B_k = KTs[bs][0]
                vt_, B_vt = VTs[bs]
                for g in range(3):
                    qh = kvh * 3 + g
                    qt_, B_q = QTs[bs][g % 2]
                    ui = b_units.index((kvh, g))
                    if ui + 1 < len(b_units):
                        nk, ng = b_units[ui + 1]
                        if nk != kvh:
                            b_load_kv(nk)
                        b_load_q(nk, ng)
                    blocks = []
                    for qt in range(nqt):
                        for kb in range(nkb):
                            st = {}

                            def s1(st=st, kb=kb, qt=qt, kt=kt, B_k=B_k, qt_=qt_, B_q=B_q):
                                st["pt"] = score_exp(kt, B_k, kb * 128, 64, qt_, B_q, qt * 512, 512, 0.125)

                            def s2(st=st, kb=kb, qt=qt, qh=qh, nkb=nkb, s0=aq0, vt_=vt_, B_vt=B_vt):
                                ptt, B_pt = st["pt"]
                                S.op("pe", lambda e, kb=kb, ptt=ptt, nkb=nkb, vt_=vt_: e.matmul(psum[6][0:65, :], lhsT=vt_[:, kb, 0:65],
                                                                                               rhs=ptt[:], start=(kb == 0),
                                                                                               stop=(kb == nkb - 1)),
                                     reads=[B_vt, B_pt], writes=[PB[6]])
                                if kb == nkb - 1:
                                    t0 = s0 + qt * 512
                                    S.dma("sp", lambda e, qh=qh, t0=t0: e.dma_start(out=zt[0:64, :], in_=ZT[4 + qh, 0:64, t0:t0 + 512]),
                                          "ld_zt", writes=[B_zt])
                                    S.op("act", lambda e: e.activation(out=e3[0:65, :], in_=psum[6][0:65, :], func=AF.Copy),
                                         reads=[PB[6]], writes=[B_e3])
                                    S.op("pe", lambda e: e.matmul(psum[7][0:64, :], lhsT=sel64[0:65, :], rhs=e3[0:65, :], start=True,
                                                                  stop=True), reads=[B_sel, B_e3], writes=[PB[7]])
                                    S.op("act", lambda e: e.activation(out=e1[0:64, :], in_=psum[7][0:64, :], func=AF.Ln), reads=[PB[7]],
                                         writes=[B_e1])
                                    S.op("act", lambda e: e.activation(out=e1[0:64, :], in_=e1[0:64, :], func=AF.Exp, scale=-1.0),
                                         reads=[B_e1], writes=[B_e1])
                                    S.op("dve", lambda e: e.tensor_tensor(out=e2[0:64, :], in0=e3[0:64, :], in1=e1[0:64, :], op=ALU.mult),
                                         reads=[B_e3, B_e1], writes=[B_e2])
                                    S.op("pool", lambda e: e.tensor_tensor(out=yzo[0:64, :], in0=e2[0:64, :], in1=zt[0:64, :], op=ALU.mult),
                                         reads=[B_e2, B_zt], writes=[B_yzo])
                                    S.dma("pool", lambda e, qh=qh, t0=t0: e.dma_start(out=YZ[4 + qh, 0:64, t0:t0 + 512], in_=yzo[0:64, :]),
                                          "st_yzo", reads=[B_yzo])

                            blocks.append((s1, s2))
                    run_pipeline(blocks, G=3, LA=1, nb=6)
            def c_load(h):
                bs = h % 2
                vt_, B_vt = VTs[bs]
                for j in range(2):
                    kt, B_k = KTs[bs][j]
                    qt_, B_q = QTs[bs][j]
                    S.dma("sp", lambda e, h=h, j=j, s0=s0, Sq=Sq, kt=kt: e.dma_start(
                        out=kt[0:64, 0:Sq], in_=QK[40 + 2 * h + j, 0:64, s0:s0 + Sq]), f"ld_K{bs}{j}", writes=[B_k])
                    S.dma("sp", lambda e, h=h, j=j, aq0=aq0, anq=anq, qt_=qt_: e.dma_start(
                        out=qt_[0:64, 0:anq], in_=QK[32 + 2 * h + j, 0:64, aq0:aq0 + anq]), f"ld_Q{bs}{j}", writes=[B_q])
                S.dma("sp", lambda e, h=h, s0=s0, Sq=Sq, nkb=nkb, vt_=vt_: e.dma_start(
                    out=vt_[:, 0:nkb, :],
                    in_=VS[s0:s0 + Sq, 1280 + h * 128:1280 + (h + 1) * 128].rearrange("(b p) c -> p b c", p=128)),
                    f"ld_V{bs}", writes=[B_vt])

            c_load(0)
            for h in range(4):
                bs = h % 2
                vt_, B_vt = VTs[bs]
                if h + 1 < 4:
                    c_load(h + 1)
                blocks = []
                for qt in range(nqt):
                    for kb in range(nkb):
                        for j in range(2):
                            st = {}

                            def s1(st=st, kb=kb, qt=qt, j=j, bs=bs):
                                st["pt"] = score_exp(KTs[bs][j][0], KTs[bs][j][1], kb * 128, 64, QTs[bs][j][0], QTs[bs][j][1], qt * 512, 512,
                                                     0.125)

                            def s2(st=st, kb=kb, qt=qt, j=j, h=h, nkb=nkb, s0=aq0, vt_=vt_, B_vt=B_vt):
                                ptt, B_pt = st["pt"]
                                S.op("pe", lambda e, kb=kb, ptt=ptt, nkb=nkb, j=j, vt_=vt_: e.matmul(psum[4 + 2 * j][:, :], lhsT=vt_[:, kb, :],
                                                                                                    rhs=ptt[:], start=(kb == 0),
                                                                                                    stop=(kb == nkb - 1)),
                                     reads=[B_vt, B_pt], writes=[PB[4 + 2 * j]], sig=False)
                                S.op("pe", lambda e, kb=kb, ptt=ptt, nkb=nkb, j=j: e.matmul(psum[5 + 2 * j][:, :], lhsT=ones_bf[:, :],
                                                                                           rhs=ptt[:], start=(kb == 0), stop=(kb == nkb - 1)),
                                     reads=[B_ones, B_pt], writes=[PB[5 + 2 * j]])
                                if kb == nkb - 1 and j == 1:
                                    c_epilogue(h, s0 + qt * 512)

                            blocks.append((s1, s2))
                run_pipeline(blocks, G=2, LA=1, nb=4)
            def a_load(h, g, ui):
                win, dil = A_GROUPS[g]
                hh = g * 4 + h
                Sq = anq
                s0 = aq0
                pad = 64 * dil
                kt, B_k = KTs[ui % 2][0]
                qt_, B_q = QTs[ui % 2][0]
                kk = f"ld_K{ui % 2}0"
                if halo:
                    S.dma("sp", lambda e, hh=hh, kt=kt, pad=pad, oth0=oth0, anq=anq: e.dma_start(
                        out=kt[0:96, 0:pad], in_=QK[12 + hh, 0:96, oth0 + anq - pad:oth0 + anq]), kk, writes=[B_k])
                    S.dma("sp", lambda e, hh=hh, kt=kt, pad=pad, oth0=oth0, Sq=Sq: e.dma_start(
                        out=kt[0:96, pad + Sq:pad + Sq + pad], in_=QK[12 + hh, 0:96, oth0:oth0 + pad]), kk, writes=[B_k])
                else:
                    S.op("pool", lambda e, kt=kt, pad=pad: e.memset(kt[0:96, 0:pad], 0.0), writes=[B_k])
                    S.op("pool", lambda e, kt=kt, pad=pad, Sq=Sq: e.memset(kt[0:96, pad + Sq:pad + Sq + pad], 0.0), writes=[B_k])
                S.dma("sp", lambda e, hh=hh, s0=s0, Sq=Sq, kt=kt, pad=pad: e.dma_start(
                    out=kt[0:96, pad:pad + Sq], in_=QK[12 + hh, 0:96, s0:s0 + Sq]), kk, writes=[B_k])
                S.dma("sp", lambda e, hh=hh, s0=s0, Sq=Sq, qt_=qt_: e.dma_start(out=qt_[0:96, 0:Sq], in_=QK[hh, 0:96, s0:s0 + Sq]),
                      f"ld_Q{ui % 2}0", writes=[B_q])
                L = Sq // dil
                nblk = L // 128 + 1
                vt_, B_vt = VTs[ui % 2]
                vv = a_view(vt_, dil, nblk)
                vk = f"ld_V{ui % 2}"
                cs = slice(hh * 96, (hh + 1) * 96)
                if halo:
                    vl = VS[oth0 + anq - 64 * dil:oth0 + anq, cs].rearrange("(p r) c -> p r c", r=dil)
                    vr = VS[oth0:oth0 + 64 * dil, cs].rearrange("(p r) c -> p r c", r=dil)
                    S.dma("sp", lambda e, vv=vv, vl=vl: e.dma_start(out=vv[0:64, :, 0, :], in_=vl), vk, writes=[B_vt])
                    S.dma("sp", lambda e, vv=vv, vr=vr, nblk=nblk: e.dma_start(out=vv[64:128, :, nblk - 1, :], in_=vr), vk, writes=[B_vt])
                    S.op("dve", lambda e, vv=vv: e.tensor_scalar(out=vv[0:64, :, 0, :], in0=vv[0:64, :, 0, :], scalar1=flags[0:64, 0:1],
                                                                 scalar2=None, op0=ALU.mult), reads=[B_flags, B_vt], writes=[B_vt])
                    S.op("dve", lambda e, vv=vv, nblk=nblk: e.tensor_scalar(out=vv[64:128, :, nblk - 1, :], in0=vv[64:128, :, nblk - 1, :],
                                                                            scalar1=flags[64:128, 1:2], scalar2=None, op0=ALU.mult),
                         reads=[B_flags, B_vt], writes=[B_vt])
                else:
                    S.op("pool", lambda e, vv=vv: e.memset(vv[0:64, :, 0, :], 0.0), writes=[B_vt])
                    S.op("pool", lambda e, vv=vv, nblk=nblk: e.memset(vv[64:128, :, nblk - 1, :], 0.0), writes=[B_vt])
                vsrc = VS[s0:s0 + Sq, cs].rearrange("(b p r) c -> p r b c", p=128, r=dil)
                for r in range(dil):
                    S.dma("sp", lambda e, vv=vv, vsrc=vsrc, nblk=nblk, r=r: e.dma_start(out=vv[64:128, r, 0:nblk - 1, :], in_=vsrc[0:64, r]),
                          vk, writes=[B_vt])
                    S.dma("sp", lambda e, vv=vv, vsrc=vsrc, nblk=nblk, r=r: e.dma_start(out=vv[0:64, r, 1:nblk, :], in_=vsrc[64:128, r]),
                          vk, writes=[B_vt])

            def a_view(vt_, dil, nblk):
                return vt_[:].rearrange("p a b -> p (a b)")[:, 0:dil * nblk * 96].rearrange("p (r j c) -> p r j c", r=dil, j=nblk)

            a_units = [(h, g) for h in range(4) for g in range(3)]
            a_load(0, 0, 0)
            scale_a = 96 ** -0.5
            for h in range(4):
                for g, (win, dil) in enumerate(A_GROUPS):
                    hh = g * 4 + h
                    Sq = anq
                    s0 = aq0
                    L = Sq // dil
                    T = min(512, L)
                    pad = 64 * dil
                    ui = a_units.index((h, g))
                    kt, B_k = KTs[ui % 2][0]
                    qt_, B_q = QTs[ui % 2][0]
                    if ui + 1 < len(a_units):
                        a_load(a_units[ui + 1][0], a_units[ui + 1][1], ui + 1)
                    nblk = L // 128 + 1
                    vt_, B_VTa = VTs[ui % 2]
                    vv = a_view(vt_, dil, nblk)
                    for r in range(dil):
                        blocks = []
                        for qt in range(L // T):
                            i0 = qt * T
                            nb = T // 128 + 1
                            for b in range(nb):
                                st = {}
                                jb = i0 // 128 + b
                                w0 = max(i0, i0 - 128 + 128 * b)
                                w1 = min(i0 + T, i0 + 128 + 128 * b)
                                nq = w1 - w0
                                m0 = w0 - (i0 - 128 + 128 * b)
                                kcol = r + dil * 128 * jb
                                q0 = r + dil * w0

                                def s1(st=st, kcol=kcol, q0=q0, nq=nq, m0=m0, kt=kt, B_k=B_k, qt_=qt_, B_q=B_q, dil=dil):
                                    ptt, B_pt = score_exp(kt, B_k, kcol, 96, qt_, B_q, q0, nq, scale_a, kstep=dil, qstep=dil)
                                    S.op("dve", lambda e, ptt=ptt, nq=nq, m0=m0: e.tensor_tensor(
                                        out=ptt[:, 0:nq], in0=ptt[:, 0:nq], in1=mask_bf[:, m0:m0 + nq], op=ALU.mult),
                                        reads=[B_pt, B_mask], writes=[B_pt])
                                    st["pt"] = (ptt, B_pt)

                                def s2(st=st, b=b, nb=nb, jb=jb, nq=nq, w0=w0, i0=i0, T=T, nblk=nblk, g=g, r=r, dil=dil, halo=halo, vv=vv, B_VTa=B_VTa):
                                    ptt, B_pt = st["pt"]
                                    if b == 0:
                                        S.op("pe", lambda e, T=T: e.matmul(psum[4][0:96, 0:T], lhsT=zeros_bf[:, 0:96], rhs=zeros_w[:, 0:T],
                                                                           start=True, stop=False), reads=[B_zeros], writes=[PB[4]],
                                             sig=False)
                                        S.op("pe", lambda e, T=T: e.matmul(psum[5][0:96, 0:T], lhsT=zeros_bf[:, 0:96], rhs=zeros_w[:, 0:T],
                                                                           start=True, stop=False), reads=[B_zeros], writes=[PB[5]],
                                             sig=False)
                                    c0 = w0 - i0
                                    last = (b == nb - 1)
                                    S.op("pe", lambda e, jb=jb, ptt=ptt, nq=nq, c0=c0, last=last, vv=vv, r=r: e.matmul(
                                        psum[4][0:96, c0:c0 + nq], lhsT=vv[:, r, jb, :], rhs=ptt[:, 0:nq], start=False, stop=last),
                                        reads=[B_VTa, B_pt], writes=[PB[4]], sig=False)
                                    if halo:
                                        onesm = onesL if jb == 0 else (onesR if jb == nblk - 1 else ones_bf)
                                    else:
                                        onesm = ones_lo if jb == 0 else (ones_hi if jb == nblk - 1 else ones_bf)
                                    S.op("pe", lambda e, ptt=ptt, nq=nq, c0=c0, last=last, onesm=onesm: e.matmul(
                                        psum[5][0:96, c0:c0 + nq], lhsT=onesm[:, 0:96], rhs=ptt[:, 0:nq], start=False, stop=last),
                                        reads=[B_ones, B_onesL, B_pt], writes=[PB[5]])
                                    if not last:
                                        return
                                    a0 = r + dil * i0
                                    if dil > 1:
                                        ao_sl = AO[:, a0:a0 + dil * (T - 1) + 1:dil]
                                        al_sl = AL[:, a0:a0 + dil * (T - 1) + 1:dil]
                                    else:
                                        ao_sl = AO[:, a0:a0 + T]
                                        al_sl = AL[:, a0:a0 + T]
                                    if g == 0:
                                        S.op("dve", lambda e, ao_sl=ao_sl, T=T: e.tensor_copy(out=ao_sl, in_=psum[4][0:96, 0:T]),
                                             reads=[PB[4]], writes=[B_AO])
                                        S.op("act", lambda e, al_sl=al_sl, T=T: e.activation(out=al_sl, in_=psum[5][0:96, 0:T],
                                                                                            func=AF.Copy), reads=[PB[5]], writes=[B_AL])
                                    else:
                                        S.op("dve", lambda e, ao_sl=ao_sl, T=T: e.tensor_tensor(out=ao_sl, in0=psum[4][0:96, 0:T],
                                                                                               in1=ao_sl, op=ALU.add),
                                             reads=[PB[4], B_AO], writes=[B_AO])
                                        S.op("dve", lambda e, al_sl=al_sl, T=T: e.tensor_tensor(out=al_sl, in0=psum[5][0:96, 0:T],
                                                                                               in1=al_sl, op=ALU.add),
                                             reads=[PB[5], B_AL], writes=[B_AL])

                                blocks.append((s1, s2))
                        run_pipeline(blocks, G=2, LA=1, nb=4)
                for qt in range(nqt):
                    t0 = aq0 + qt * 512
                    c0 = qt * 512
                    S.dma("sp", lambda e, h=h, t0=t0: e.dma_start(out=zt[0:96, :], in_=ZT[h, 0:96, t0:t0 + 512]), "ld_zt", writes=[B_zt])
                    S.op("act", lambda e, c0=c0: e.activation(out=e1[0:96, :], in_=AL[:, c0:c0 + 512], func=AF.Ln), reads=[B_AL],
                         writes=[B_e1])
                    S.op("act", lambda e: e.activation(out=e1[0:96, :], in_=e1[0:96, :], func=AF.Exp, scale=-1.0), reads=[B_e1],
                         writes=[B_e1])
                    S.op("dve", lambda e, c0=c0: e.tensor_tensor(out=e2[0:96, :], in0=AO[:, c0:c0 + 512], in1=e1[0:96, :], op=ALU.mult),
                         reads=[B_AO, B_e1], writes=[B_e2])
                    S.op("pool", lambda e: e.tensor_tensor(out=yzo[0:96, :], in0=e2[0:96, :], in1=zt[0:96, :], op=ALU.mult),
                         reads=[B_e2, B_zt], writes=[B_yzo])
                    S.dma("pool", lambda e, h=h, t0=t0: e.dma_start(out=YZ[h, 0:96, t0:t0 + 512], in_=yzo[0:96, :]), "st_yzo",
                          reads=[B_yzo])

        phase_reset()
        wbg_bf, B_wbg = sb([128, 8, 3 * D], BF16)
        woa_bf, B_woa = sb([96, 4, D], BF16)
        wob_bf, B_wob = sb([64, 6, D], BF16)
        woc_bf, B_woc = sb([128, 4, D], BF16)
        wout_bf, B_wout = sb([128, 8, D], BF16)
        wstg, B_wstg = sb([128, 1024], F32)
        cvt = [0]

        def load_cast(dst_ap, src_ap, np_):
            S.dma("sp", lambda e: e.dma_start(out=wstg[0:np_, :], in_=src_ap), "ld_wstg", writes=[B_wstg])
            eng = ("act", "dve", "pool")[cvt[0] % 3]
            cvt[0] += 1
            if eng == "act":
                S.op("act", lambda e: e.activation(out=dst_ap, in_=wstg[0:np_, :], func=AF.Copy), reads=[B_wstg], writes=[B_wbg])
            else:
                S.op(eng, lambda e: e.tensor_copy(out=dst_ap, in_=wstg[0:np_, :]), reads=[B_wstg], writes=[B_wbg])

        for kc in range(8):
            for cg in range(3):
                load_cast(wbg_bf[:, kc, cg * 1024:(cg + 1) * 1024], w_bg[l, kc, :, cg * 1024:(cg + 1) * 1024], 128)
            load_cast(wout_bf[:, kc, :], w_out[l, kc], 128)
        for hh in range(4):
            load_cast(woa_bf[:, hh, :], w_oa[l, hh], 96)
            load_cast(woc_bf[:, hh, :], w_oc[l, hh], 128)
        for hh in range(6):
            load_cast(wob_bf[:, hh, :], w_ob[l, hh], 64)
        B_woa = B_wob = B_woc = B_wout = B_wbg

        hT, B_hT = sb([128, 8, 512], BF16)
        xT, B_xT = sb([128, 8, 512], F32)
        yz, B_yz = sb([128, 14, 512], BF16)
        gsb = [sb([128, 512], F32) for _ in range(3)]
        msb = [sb([128, 512], F32) for _ in range(3)]
        mg, B_mg = sb([128, 8, 512], BF16)
        xn, B_xn = sb([128, 8, 512], F32)
        ytok = [sb([128, D], F32) for _ in range(2)]
        last_layer = (l == depth - 1)
        for (sg0, sgn, si, full) in segs:
            if not full:
                continue
            for tt in range(sgn // 512):
                t0 = sg0 + tt * 512
                S.dma("sp", lambda e, t0=t0: e.dma_start(out=hT[:], in_=HT[:, :, t0:t0 + 512].rearrange("k p t -> p k t")), "ld_hT",
                      writes=[B_hT])
                S.dma("sp", lambda e, t0=t0, l=l: e.dma_start(out=xT[:], in_=XT[l][:, :, t0:t0 + 512].rearrange("k p t -> p k t")),
                      "ld_xT", writes=[B_xT])
                S.dma("sp", lambda e, t0=t0: e.dma_start(out=yz[0:96, 0:4, :], in_=YZ[0:4, 0:96, t0:t0 + 512].rearrange("c p t -> p c t")),
                      "ld_yz", writes=[B_yz])
                S.dma("sp", lambda e, t0=t0: e.dma_start(out=yz[0:64, 4:10, :], in_=YZ[4:10, 0:64, t0:t0 + 512].rearrange("c p t -> p c t")),
                      "ld_yz", writes=[B_yz])
                S.dma("sp", lambda e, t0=t0: e.dma_start(out=yz[:, 10:14, :], in_=YZ[10:14, :, t0:t0 + 512].rearrange("c p t -> p c t")),
                      "ld_yz", writes=[B_yz])
                for oc in range(8):
                    osl = slice(oc * 128, (oc + 1) * 128)
                    for hh in range(4):
                        S.op("pe", lambda e, hh=hh, osl=osl: e.matmul(psum[0][:], lhsT=woa_bf[0:96, hh, osl], rhs=yz[0:96, hh, :],
                                                                     start=(hh == 0), stop=(hh == 3)),
                             reads=[B_wbg, B_yz], writes=[PB[0]], sig=(hh == 3))
                    for hh in range(6):
                        S.op("pe", lambda e, hh=hh, osl=osl: e.matmul(psum[1][:], lhsT=wob_bf[0:64, hh, osl], rhs=yz[0:64, 4 + hh, :],
                                                                     start=(hh == 0), stop=(hh == 5)),
                             reads=[B_wbg, B_yz], writes=[PB[1]], sig=(hh == 5))
                    for hh in range(4):
                        S.op("pe", lambda e, hh=hh, osl=osl: e.matmul(psum[2][:], lhsT=woc_bf[:, hh, osl], rhs=yz[:, 10 + hh, :],
                                                                     start=(hh == 0), stop=(hh == 3)),
                             reads=[B_wbg, B_yz], writes=[PB[2]], sig=(hh == 3))
                    for br in range(3):
                        c0 = br * 1024 + oc * 128
                        for kc in range(8):
                            S.op("pe", lambda e, kc=kc, c0=c0, br=br: e.matmul(psum[3 + br][:], lhsT=wbg_bf[:, kc, c0:c0 + 128],
                                                                              rhs=hT[:, kc, :], start=(kc == 0), stop=(kc == 7)),
                                 reads=[B_wbg, B_hT], writes=[PB[3 + br]], sig=(kc == 7))
                        gt, B_g = gsb[br]
                        mt, B_m = msb[br]
                        ch = br * 8 + oc
                        S.op("act", lambda e, br=br, gt=gt, ch=ch: e.activation(out=gt[:], in_=psum[3 + br][:], func=AF.Sigmoid,
                                                                               bias=bbg_t[:, ch:ch + 1], scale=1.0),
                             reads=[PB[3 + br], B_bbg], writes=[B_g])
                        S.op("dve", lambda e, br=br, gt=gt, mt=mt: e.tensor_tensor(out=mt[:], in0=psum[br][:], in1=gt[:], op=ALU.mult),
                             reads=[PB[br], B_g], writes=[B_m])
                    S.op("pool", lambda e: e.tensor_tensor(out=msb[0][0][:], in0=msb[0][0][:], in1=msb[1][0][:], op=ALU.add),
                         reads=[msb[0][1], msb[1][1]], writes=[msb[0][1]])
                    S.op("pool", lambda e, oc=oc: e.tensor_tensor(out=mg[:, oc, :], in0=msb[0][0][:], in1=msb[2][0][:], op=ALU.add),
                         reads=[msb[0][1], msb[2][1]], writes=[B_mg])
                for oc in range(8):
                    osl = slice(oc * 128, (oc + 1) * 128)
                    pb = 6 + oc % 2
                    for kc in range(8):
                        S.op("pe", lambda e, kc=kc, osl=osl, pb=pb: e.matmul(psum[pb][:], lhsT=wout_bf[:, kc, osl], rhs=mg[:, kc, :],
                                                                            start=(kc == 0), stop=(kc == 7)),
                             reads=[B_wbg, B_mg], writes=[PB[pb]], sig=(kc == 7))
                    S.op("dve", lambda e, oc=oc, pb=pb, si=si: e.scalar_tensor_tensor(
                        out=xn[:, oc, :], in0=psum[pb][:], scalar=modT[:, 16 + oc, si:si + 1], in1=xT[:, oc, :], op0=ALU.mult,
                        op1=ALU.add), reads=[PB[pb], B_mod, B_xT], writes=[B_xn])
                if not last_layer:
                    S.dma("pool", lambda e, t0=t0, l=l: e.dma_start(out=XT[l + 1][:, :, t0:t0 + 512].rearrange("k p t -> p k t"),
                                                                   in_=xn[:]), "st_xn", reads=[B_xn])
                    if sg0 == SP:
                        o0 = t0 - SP
                        S.dma("pool", lambda e, o0=o0: e.dma_start(out=AGsrc[:, :, o0:o0 + 512].rearrange("k p t -> p k t"), in_=xn[:]),
                              "st_ag", reads=[B_xn], writes=[B_AGsrc])
                else:
                    for sub in range(4):
                        yt, B_y = ytok[sub % 2]
                        for hf in range(2):
                            pb = 0 + hf
                            for k4 in range(4):
                                kc = hf * 4 + k4
                                S.op("pe", lambda e, pb=pb, k4=k4, kc=kc, sub=sub: e.transpose(
                                    psum[pb][:, k4 * 128:(k4 + 1) * 128], xn[:, kc, sub * 128:(sub + 1) * 128], ident[:]),
                                    reads=[B_xn, B_ident], writes=[PB[pb]], sig=(k4 == 3))
                            if hf == 0:
                                S.op("dve", lambda e, pb=pb, yt=yt: e.tensor_copy(out=yt[:, 0:512], in_=psum[pb][:]), reads=[PB[pb]],
                                     writes=[B_y])
                            else:
                                S.op("act", lambda e, pb=pb, yt=yt: e.activation(out=yt[:, 512:1024], in_=psum[pb][:], func=AF.Copy),
                                     reads=[PB[pb]], writes=[B_y])
                        S.dma("pool", lambda e, t0=t0, sub=sub, yt=yt: e.dma_start(out=y_out[t0 + sub * 128:t0 + (sub + 1) * 128, :],
                                                                                  in_=yt[:]), f"st_y{sub % 2}", reads=[B_y])
        if not last_layer:
            for kc in range(8):
                S.cc(lambda e, kc=kc: e.collective_compute(
                    "AllGather", ALU.bypass, replica_groups=[[0, 1], [2, 3], [4, 5], [6, 7]],
                    ins=[AGsrc[kc]], outs=[AGdst[kc].rearrange("r p t -> (r p) t")]), f"cc_{l}_{kc}", reads=[B_AGsrc], writes=[B_AGdst])

    S.final_wait("sp")
    S.emit()
    return nc, S


def _prep_common(inp, depth):
    f = np.float32

    def A(x):
        return np.ascontiguousarray(np.asarray(x, dtype=f))

    gc = np.zeros((depth, 128, 6), f)
    for j, (k, n) in enumerate((("qn_a", 96), ("kn_a", 96), ("qn_b", 64), ("kn_b", 64), ("qn_c", 64), ("kn_c", 64))):
        gc[:, :n, j] = A(inp[k])
        if n == 64:
            gc[:, 64:128, j] = A(inp[k])
    lam = np.concatenate([A(inp["lam_q1"]), A(inp["lam_k1"]), A(inp["lam_q2"]), A(inp["lam_k2"])], axis=1)[:, None, :]
    return {
        "w_ada": A(inp["w_ada"]).reshape(depth, 8, 128, 3 * D),
        "b_adaT": A(A(inp["b_ada"]).reshape(depth, 24, 128).transpose(0, 2, 1)),
        "norm_gT": A(A(inp["norm_g"]).reshape(depth, 8, 128).transpose(0, 2, 1)),
        "w_in": A(inp["w_in"]).reshape(depth, 8, 128, DIN),
        "gcols": gc,
        "lam": A(lam),
        "subln": A(inp["subln_c"]).reshape(depth, 128, 1),
        "w_oa": A(inp["w_oa"]).reshape(depth, 4, 96, D),
        "w_ob": A(inp["w_ob"]).reshape(depth, 6, 64, D),
        "w_oc": A(inp["w_oc"]).reshape(depth, 4, 128, D),
        "w_bg": A(inp["w_bg"]).reshape(depth, 8, 128, 3 * D),
        "b_bgT": A(A(inp["b_bg"]).reshape(depth, 24, 128).transpose(0, 2, 1)),
        "w_out": A(inp["w_out"]).reshape(depth, 8, 128, D),
        "rotm": _rot_mats(),
        "bandmask": _band_mask(),
        "ident": np.eye(128, dtype=f),
    }


def _core_inputs(common, rope_g, xp_c, xs_j, cp_c, cs_j, hf, SP, H):
    m = dict(common)
    own = slice(hf * H, (hf + 1) * H)
    oth = slice((1 - hf) * H, (2 - hf) * H)
    m["x"] = np.ascontiguousarray(np.concatenate([xp_c, xs_j[own], xs_j[oth]], axis=0))
    m["rope"] = np.ascontiguousarray(np.concatenate([rope_g[..., 0:SP], rope_g[..., own], rope_g[..., oth]], axis=-1))
    cc = np.stack([cp_c, cs_j], axis=0)
    m["cT"] = np.ascontiguousarray(cc.reshape(2, 8, 128).transpose(2, 1, 0))
    fl = np.zeros((128, 2), np.float32)
    fl[:, 0] = float(hf)
    fl[:, 1] = float(1 - hf)
    m["flags"] = fl
    return m


def kernel(**inp):
    xp = np.asarray(inp["x_prompt"], np.float32)
    xs = np.asarray(inp["x_sample"], np.float32)
    cp = np.asarray(inp["c_prompt"], np.float32)
    cs = np.asarray(inp["c_sample"], np.float32)
    depth = int(np.asarray(inp["norm_g"]).shape[0])
    SP, SS = xp.shape[1], xs.shape[1]
    H = SS // 2
    lam_inits = [0.8 - 0.6 * math.exp(-0.3 * l) for l in range(depth)]
    nc, _ = build(SP, H, depth, lam_inits)
    common = _prep_common(inp, depth)
    rope_g = _rope_tables(max(SP, SS))
    in_maps = [_core_inputs(common, rope_g, xp[c], xs[c // 2], cp[c], cs[c // 2], c % 2, SP, H) for c in range(8)]
    res = run_bass_kernel_spmd(nc, in_maps, core_ids=list(range(8)))
    yp = np.stack([res.results[c]["y"][:SP] for c in range(8)], axis=0)
    ys = np.stack([np.concatenate([res.results[2 * j]["y"][SP:], res.results[2 * j + 1]["y"][SP:]], axis=0)
                   for j in range(xs.shape[0])], axis=0)
    return (yp.astype(np.float32), ys.astype(np.float32))
```
